# Optimizing a Trainium2 kernel written in Bass

```python
import jax, jax.numpy as jnp
from jax import lax
import numpy as np

D_MODEL = 4096
BATCH = 2
SEQ = 4096
DEPTH = 2

CHUNK = 64
N_META = 16
Q_BLOCK = 128
N_MIXERS = 2
N_DELTA = (DEPTH + N_MIXERS - 1) // N_MIXERS
N_SB = DEPTH // N_MIXERS
EPS = 1e-6

DN_HEAD_K = 128
DN_HEAD_V = 128
DN_HK = D_MODEL // DN_HEAD_K
DN_HV = 2 * DN_HK
DN_KEY = DN_HK * DN_HEAD_K
DN_VAL = DN_HV * DN_HEAD_V
DN_CONV_CH = 2 * DN_KEY + DN_VAL
DN_IN = DN_CONV_CH + DN_VAL + 2 * DN_HV
CONV_K = 4

SB_HEAD = 128
SB_HEADS = D_MODEL // SB_HEAD
SB_WIDTH = SB_HEADS * SB_HEAD

kernel_name = "hybrid_deltanet_stickbreaking_meta"


def rms_norm(x, w):
    xf = x.astype(jnp.float32)
    y = xf * lax.rsqrt(jnp.mean(xf * xf, axis=-1, keepdims=True) + EPS)
    return (y * w.astype(jnp.float32)).astype(x.dtype)


def l2_norm(x):
    xf = x.astype(jnp.float32)
    return xf * lax.rsqrt(jnp.sum(xf * xf, axis=-1, keepdims=True) + EPS)


def causal_depthwise_conv(x, w):
    c = x.shape[-1]
    return lax.conv_general_dilated(
        x, w[:, None, :].astype(x.dtype), window_strides=(1,),
        padding=((CONV_K - 1, 0),), dimension_numbers=("NWC", "WIO", "NWC"),
        feature_group_count=c)


def chunk_gated_delta_rule(q, k, v, g, beta):
    b, h, lc, dk = q.shape
    dv = v.shape[-1]
    n = lc // CHUNK
    f32 = jnp.float32
    q = q.astype(f32) * (dk ** -0.5)
    k = k.astype(f32)
    v = v.astype(f32)
    rs = lambda t: t.reshape(b, h, n, CHUNK, *t.shape[3:])
    q, k, v, g, beta = rs(q), rs(k), rs(v), rs(g.astype(f32)), rs(beta.astype(f32))
    g = jnp.cumsum(g, axis=-1)
    idx = jnp.arange(CHUNK)
    lower_incl = idx[:, None] >= idx[None, :]
    strict = idx[:, None] > idx[None, :]
    decay = jnp.exp(jnp.where(lower_incl, g[..., :, None] - g[..., None, :], -jnp.inf))
    k_beta = k * beta[..., None]
    v_beta = v * beta[..., None]
    l_mat = jnp.where(strict, jnp.einsum("bhnid,bhnjd->bhnij", k_beta, k) * decay, 0.0)
    eye = jnp.eye(CHUNK, dtype=f32)
    t_mat = lax.linalg.triangular_solve(eye + l_mat, jnp.broadcast_to(eye, l_mat.shape),
                                        left_side=True, lower=True)
    u = jnp.einsum("bhnij,bhnjd->bhnid", t_mat, v_beta)
    w = jnp.einsum("bhnij,bhnjd->bhnid", t_mat, k_beta * jnp.exp(g)[..., None])
    attn = jnp.where(lower_incl, jnp.einsum("bhnid,bhnjd->bhnij", q, k) * decay, 0.0)
    q_dec = q * jnp.exp(g)[..., None]
    g_last = g[..., -1]
    k_dec = k * jnp.exp(g_last[..., None] - g)[..., None]

    def step(state, xs):
        u_c, w_c, qd_c, kd_c, a_c, gl_c = xs
        v_new = u_c - jnp.einsum("bhck,bhkv->bhcv", w_c, state)
        out = jnp.einsum("bhck,bhkv->bhcv", qd_c, state) + jnp.einsum("bhij,bhjv->bhiv", a_c, v_new)
        state = state * jnp.exp(gl_c)[..., None, None] + jnp.einsum("bhck,bhcv->bhkv", kd_c, v_new)
        return state, out

    mv = lambda t: jnp.moveaxis(t, 2, 0)
    state0 = jnp.zeros((b, h, dk, dv), f32)
    _, out = lax.scan(step, state0, (mv(u), mv(w), mv(q_dec), mv(k_dec), mv(attn), mv(g_last)))
    return jnp.moveaxis(out, 0, 2).reshape(b, h, lc, dv)


def gated_deltanet_mixer(hn, w_in, conv_w, a_log, dt_bias, out_norm_w, w_out):
    bsz, seq_len, _ = hn.shape
    proj = hn @ w_in
    qkv, z, b_logit, a_logit = jnp.split(
        proj, [DN_CONV_CH, DN_CONV_CH + DN_VAL, DN_CONV_CH + DN_VAL + DN_HV], axis=-1)
    qkv = jax.nn.silu(causal_depthwise_conv(qkv, conv_w))
    q, k, v = jnp.split(qkv, [DN_KEY, 2 * DN_KEY], axis=-1)
    to_heads = lambda t, nh, hd: t.reshape(bsz, seq_len, nh, hd).transpose(0, 2, 1, 3)
    q = jnp.repeat(l2_norm(to_heads(q, DN_HK, DN_HEAD_K)), DN_HV // DN_HK, axis=1)
    k = jnp.repeat(l2_norm(to_heads(k, DN_HK, DN_HEAD_K)), DN_HV // DN_HK, axis=1)
    v = to_heads(v, DN_HV, DN_HEAD_V)
    beta = jax.nn.sigmoid(b_logit.astype(jnp.float32)).transpose(0, 2, 1)
    g = (-jnp.exp(a_log.astype(jnp.float32))
         * jax.nn.softplus(a_logit.astype(jnp.float32) + dt_bias.astype(jnp.float32))).transpose(0, 2, 1)
    pad = CHUNK - N_META
    p4 = lambda t: jnp.pad(t, ((0, 0), (0, 0), (pad, 0), (0, 0)))
    p3 = lambda t: jnp.pad(t, ((0, 0), (0, 0), (pad, 0)))
    o = chunk_gated_delta_rule(p4(q), p4(k), p4(v), p3(g), p3(beta))[:, :, pad:]
    o = rms_norm(o, out_norm_w) * jax.nn.silu(to_heads(z, DN_HV, DN_HEAD_V).astype(jnp.float32))
    o = o.transpose(0, 2, 1, 3).reshape(bsz, seq_len, DN_VAL).astype(hn.dtype)
    return o @ w_out


def stick_breaking_attention(q, k, v):
    lp, d = q.shape[2], q.shape[3]
    scale = d ** -0.5
    outs = []
    for i in range(lp // Q_BLOCK):
        t0, t1 = i * Q_BLOCK, (i + 1) * Q_BLOCK
        z = jnp.einsum("bhtd,bhsd->bhts", q[:, :, t0:t1], k[:, :, :t1]).astype(jnp.float32) * scale
        causal = jnp.arange(t1)[None, :] < jnp.arange(t0, t1)[:, None]
        log_keep = jnp.where(causal, jax.nn.log_sigmoid(-z), 0.0)
        tail = lax.cumsum(log_keep, axis=3, reverse=True) - log_keep
        a = jnp.where(causal, jnp.exp(jax.nn.log_sigmoid(z) + tail), 0.0)
        outs.append(jnp.einsum("bhts,bhsd->bhtd", a.astype(v.dtype), v[:, :, :t1]))
    return jnp.concatenate(outs, axis=2)


def stick_breaking_mixer(hn, w_in, q_norm_w, k_norm_w, w_out):
    bsz, seq_len, _ = hn.shape
    q, k, v, gate = jnp.split(hn @ w_in, 4, axis=-1)
    to_heads = lambda t: t.reshape(bsz, seq_len, SB_HEADS, SB_HEAD).transpose(0, 2, 1, 3)
    q = rms_norm(to_heads(q), q_norm_w)
    k = rms_norm(to_heads(k), k_norm_w)
    v = to_heads(v)
    lp = -(-seq_len // Q_BLOCK) * Q_BLOCK
    padr = lambda t: jnp.pad(t, ((0, 0), (0, 0), (0, lp - seq_len), (0, 0)))
    o = stick_breaking_attention(padr(q), padr(k), padr(v))[:, :, :seq_len]
    o = o.transpose(0, 2, 1, 3).reshape(bsz, seq_len, SB_WIDTH) * jax.nn.silu(gate)
    return o @ w_out


def setup_inputs(seed: int = 0) -> dict:
    key = jax.random.key(seed)
    ks = jax.random.split(key, 16)
    f32 = jnp.float32
    nrm = lambda k, shape, scale: jax.random.normal(k, shape, f32) * scale
    dt = jnp.exp(jax.random.uniform(ks[6], (N_DELTA, DN_HV), f32, np.log(1e-3), np.log(1e-1)))
    return {
        "x": nrm(ks[0], (BATCH, SEQ, D_MODEL), 1.0),
        "meta_tokens": nrm(ks[1], (N_META, D_MODEL), 1.0),
        "dn_norm_w": 1.0 + nrm(ks[2], (N_DELTA, D_MODEL), 0.02),
        "dn_w_in": nrm(ks[3], (N_DELTA, D_MODEL, DN_IN), D_MODEL ** -0.5),
        "dn_conv_w": nrm(ks[4], (N_DELTA, CONV_K, DN_CONV_CH), CONV_K ** -0.5),
        "dn_a_log": jnp.log(jax.random.uniform(ks[5], (N_DELTA, DN_HV), f32, 1.0, 16.0)),
        "dn_dt_bias": dt + jnp.log(-jnp.expm1(-dt)),
        "dn_out_norm_w": 1.0 + nrm(ks[7], (N_DELTA, DN_HEAD_V), 0.02),
        "dn_w_out": nrm(ks[8], (N_DELTA, DN_VAL, D_MODEL), DN_VAL ** -0.5),
        "sb_norm_w": 1.0 + nrm(ks[9], (N_SB, D_MODEL), 0.02),
        "sb_w_in": nrm(ks[10], (N_SB, D_MODEL, 4 * SB_WIDTH), D_MODEL ** -0.5),
        "sb_q_norm_w": 1.0 + nrm(ks[11], (N_SB, SB_HEAD), 0.02),
        "sb_k_norm_w": 1.0 + nrm(ks[12], (N_SB, SB_HEAD), 0.02),
        "sb_w_out": nrm(ks[13], (N_SB, SB_WIDTH, D_MODEL), SB_WIDTH ** -0.5),
    }


def reference(x, meta_tokens, dn_norm_w, dn_w_in, dn_conv_w, dn_a_log, dn_dt_bias,
              dn_out_norm_w, dn_w_out, sb_norm_w, sb_w_in, sb_q_norm_w, sb_k_norm_w, sb_w_out):
    bsz = x.shape[0]
    meta = jnp.broadcast_to(meta_tokens[None].astype(x.dtype), (bsz, N_META, D_MODEL))
    h = jnp.concatenate([meta, x], axis=1)
    for i in range(DEPTH):
        j = i // N_MIXERS
        if i % N_MIXERS == 0:
            h = h + gated_deltanet_mixer(rms_norm(h, dn_norm_w[j]), dn_w_in[j], dn_conv_w[j],
                                         dn_a_log[j], dn_dt_bias[j], dn_out_norm_w[j], dn_w_out[j])
        else:
            h = h + stick_breaking_mixer(rms_norm(h, sb_norm_w[j]), sb_w_in[j],
                                         sb_q_norm_w[j], sb_k_norm_w[j], sb_w_out[j])
    return h[:, N_META:]
```

```python
import contextlib
import numpy as np
import ml_dtypes
import concourse.bass as bass
import concourse.mybir as mybir
from concourse.bass_utils import run_bass_kernel_spmd

F32 = mybir.dt.float32
BF16 = mybir.dt.bfloat16
AF = mybir.ActivationFunctionType
ALU = mybir.AluOpType

D = 4096
NC_ = 32
SEQ = 4096
NMETA = 16
T = 4224
NT = 33
TB = 384
NB = 11
EPS = 1e-6
RGROUPS = [[0, 1, 2, 3], [4, 5, 6, 7]]
DNW = 772


class Tok:
    __slots__ = ("sem", "val")

    def __init__(self, sem, val):
        self.sem = sem
        self.val = val


class Buf:
    def __init__(self, k, t, name):
        self.k = k
        self.t = t
        self.name = name
        self.w = None
        self.r = {}
        self.dsem = None
        self.dcnt = 0
        self.root = self
        self.excl = False

    def sub(self, ap, name):
        b = Buf(self.k, ap, name)
        b.root = self.root
        return b

    def __getitem__(self, idx):
        return self.t[idx]


class K:
    def __init__(self, nc):
        self.nc = nc
        self.es = contextlib.ExitStack()
        self.eng = {"pe": nc.tensor, "act": nc.scalar, "dve": nc.vector,
                    "pool": nc.gpsimd, "sp": nc.sync}
        self.esem = {}
        self.ecnt = {}
        self.seen = {}
        for e in self.eng:
            self.esem[e] = self.es.enter_context(nc.semaphore("es_" + e))
            self.ecnt[e] = 0
            self.seen[e] = {}
        self.dbufs = []
        self.n = 0
        self.pst = None

    def phase_begin(self):
        self.pst = contextlib.ExitStack()

    def phase_end(self):
        self.barrier()
        self.pst.close()
        self.pst = None

    def sb(self, name, shape, dt):
        st = self.pst if self.pst is not None else self.es
        t = st.enter_context(self.nc.sbuf_tensor(name, list(shape), dt))
        return Buf(self, t, name)

    def ps(self, name, shape, dt):
        st = self.pst if self.pst is not None else self.es
        t = st.enter_context(self.nc.psum_tensor(name, list(shape), dt))
        b = Buf(self, t, name)
        b.excl = True
        return b

    def dram(self, name, shape, dt, kind="Internal"):
        t = self.nc.dram_tensor(name, list(shape), dt, kind=kind)
        return Buf(self, t.ap(), name)

    def _wait(self, e, tok):
        if tok is None:
            return
        if e == "pe" and tok.sem is self.esem["pe"]:
            return
        key = id(tok.sem)
        if self.seen[e].get(key, 0) >= tok.val:
            return
        self.seen[e][key] = tok.val
        self.eng[e].wait_ge(tok.sem, tok.val)

    def _deps(self, e, reads, writes):
        for b in reads:
            self._wait(e, b.w)
        for b in writes:
            self._wait(e, b.w)
            for t in b.r.values():
                self._wait(e, t)

    def _mark(self, tok, reads, writes):
        for b in reads:
            b.r[id(tok.sem)] = tok
        for b in writes:
            b.w = tok
            b.r = {}

    def _norm(self, reads, writes):
        rs = [b.root for b in reads]
        ws = [b.root for b in writes]
        ws = ws + [b for b in rs if b.excl]
        rs = [b for b in rs if not b.excl]
        return rs, ws

    def op(self, e, fn, reads, writes):
        reads, writes = self._norm(reads, writes)
        self._deps(e, reads, writes)
        ins = fn(self.eng[e])
        self.ecnt[e] += 1
        ins.then_inc(self.esem[e], 1)
        self._mark(Tok(self.esem[e], self.ecnt[e]), reads, writes)
        self.n += 1

    def dma(self, q, out_ap, in_ap, reads, writes, owner=None):
        ow = (owner if owner is not None else writes[0]).root
        reads, writes = self._norm(reads, writes)
        if ow.dsem is None:
            ow.dsem = self.es.enter_context(self.nc.semaphore("ds_" + ow.name))
            self.dbufs.append(ow)
        self._deps(q, reads, writes)
        ow.dcnt += 16
        self.eng[q].dma_start(out=out_ap, in_=in_ap).then_inc(ow.dsem, 16)
        self._mark(Tok(ow.dsem, ow.dcnt), reads, writes)
        self.n += 1

    def cc(self, in_ap, out_ap, reads, writes, owner):
        ow = owner.root
        reads, writes = self._norm(reads, writes)
        if ow.dsem is None:
            ow.dsem = self.es.enter_context(self.nc.semaphore("cs_" + ow.name))
            self.dbufs.append(ow)
        for b in reads:
            self._wait("pool", b.w)
        for b in writes:
            for t in b.r.values():
                self._wait("pool", t)
        ow.dcnt += 1
        self.eng["pool"].collective_compute("AllGather", ALU.bypass, replica_groups=RGROUPS,
                                            ins=[in_ap], outs=[out_ap]).then_inc(ow.dsem)
        self._mark(Tok(ow.dsem, ow.dcnt), reads, writes)
        self.n += 1

    def barrier(self):
        for e in self.eng:
            for e2 in self.eng:
                if e2 != e and self.ecnt[e2] > 0:
                    self._wait(e, Tok(self.esem[e2], self.ecnt[e2]))
            for b in self.dbufs:
                if b.dcnt:
                    self._wait(e, Tok(b.dsem, b.dcnt))

    def finish(self):
        self.barrier()
        self.es.close()


def act(k, out, in_, func, reads, writes, bias=None, scale=None, accum=None, e="act"):
    kw = {}
    if bias is not None:
        kw["bias"] = bias
    if scale is not None:
        kw["scale"] = scale
    if accum is not None:
        kw["accum_out"] = accum
    k.op(e, lambda en: en.activation(out=out, in_=in_, func=func, **kw), reads, writes)


def mm(k, out, lhsT, rhs, start, stop, reads, writes):
    k.op("pe", lambda en: en.matmul(out, lhsT, rhs, start=start, stop=stop), reads, writes)


def tr(k, out, in_, ident, reads, writes):
    k.op("pe", lambda en: en.transpose(out, in_, ident), reads, writes)


def tt(k, e, out, in0, in1, op, reads, writes):
    k.op(e, lambda en: en.tensor_tensor(out=out, in0=in0, in1=in1, op=op), reads, writes)


def ts(k, e, out, in0, s1, op0, reads, writes, s2=None, op1=None):
    if op1 is None:
        k.op(e, lambda en: en.tensor_scalar(out=out, in0=in0, scalar1=s1, scalar2=None, op0=op0),
             reads, writes)
    else:
        k.op(e, lambda en: en.tensor_scalar(out=out, in0=in0, scalar1=s1, scalar2=s2, op0=op0,
                                            op1=op1), reads, writes)


def stt(k, out, in0, scalar, in1, op0, op1, reads, writes):
    k.op("dve", lambda en: en.scalar_tensor_tensor(out=out, in0=in0, scalar=scalar, in1=in1,
                                                   op0=op0, op1=op1), reads, writes)


def cp(k, e, out, in_, reads, writes):
    if e == "act":
        k.op("act", lambda en: en.copy(out=out, in_=in_), reads, writes)
    else:
        k.op(e, lambda en: en.tensor_copy(out=out, in_=in_), reads, writes)


def load_consts(k, cdram, pfx=""):
    c = {}
    cf = k.sb(pfx + "c_f32", [128, 5 * 128], F32)
    k.dma("sp", cf[:, :], cdram[:, :], [], [cf])
    c["f"] = cf
    cb = k.sb(pfx + "c_bf", [128, 5 * 128], BF16)
    cp(k, "dve", cb[:, :], cf[:, :], [cf], [cb])
    c["b"] = cb
    c["I"] = lambda t=cf: t[:, 0:128]
    c["I2"] = lambda t=cf: t[:, 0:256]
    c["U"] = lambda t=cf: t[:, 256:384]
    c["Ls"] = lambda t=cf: t[:, 384:512]
    c["ones"] = lambda t=cf: t[:, 512:640]
    c["Ib"] = lambda t=cb: t[:, 0:128]
    c["onesb"] = lambda t=cb: t[:, 512:640]
    return c


def consts_np():
    I = np.eye(128, dtype=np.float32)
    r = np.arange(128)
    U = (r[:, None] <= r[None, :]).astype(np.float32)
    Ls = (r[:, None] > r[None, :]).astype(np.float32)
    ones = np.ones((128, 128), np.float32)
    return np.ascontiguousarray(np.concatenate([I, I, U, Ls, ones], axis=1))


def phase_norm_T(k, c, src, nw, dst, nb, pfx):
    wB = k.sb(pfx + "wB", [128, D], F32)
    k.dma("sp", wB[:, :], nw[:, :], [nw], [wB])
    xts = [k.sb(pfx + "xt%d" % i, [128, D], F32) for i in range(2)]
    hnb = [k.sb(pfx + "hnb%d" % i, [128, D], BF16) for i in range(2)]
    junk = k.sb(pfx + "junk", [128, D], BF16)
    st = [k.sb(pfx + "st%d" % i, [128, NC_, TB], BF16) for i in range(2)]
    ss = k.sb(pfx + "ss", [128, 4], F32)
    tp = [k.ps(pfx + "tp%d" % i, [128, 8, 128], BF16) for i in range(2)]
    ev = 0
    for blk in range(nb):
        stg = st[blk % 2]
        for tl in range(3):
            i = blk * 3 + tl
            xt = xts[i % 2]
            hb = hnb[i % 2]
            k.dma("sp", xt[:, :], src[i * 128:(i + 1) * 128, :], [src], [xt])
            act(k, junk[:, :], xt[:, :], AF.Square, [xt], [junk, ss], accum=ss[:, 0:1])
            act(k, ss[:, 1:2], ss[:, 0:1], AF.Sqrt, [ss], [ss], bias=EPS, scale=1.0 / D)
            k.op("dve", lambda en: en.reciprocal(out=ss[:, 2:3], in_=ss[:, 1:2]), [ss], [ss])
            stt(k, hb[:, :], xt[:, :], ss[:, 2:3], wB[:, :], ALU.mult, ALU.mult, [xt, ss, wB], [hb])
            for g in range(4):
                tpp = tp[g % 2]
                for cc in range(8):
                    ch = g * 8 + cc
                    tr(k, tpp[:, cc, :], hb[:, ch * 128:(ch + 1) * 128], c["Ib"](), [hb, c["b"]], [tpp])
                e = "act" if ev % 2 == 0 else "dve"
                ev += 1
                cp(k, e, stg[:, g * 8:(g + 1) * 8, tl * 128:(tl + 1) * 128], tpp[:, :, :], [tpp], [stg])
        k.dma("sp", dst[blk], stg[:, :, :], [stg], [dst])


def v3(ap, b=128):
    return ap.rearrange("p (a b) -> p a b", b=b)


class PsumSet:
    def __init__(self, k, pfx):
        self.full = [k.ps(pfx + "pf%d" % i, [128, 512], F32) for i in range(3)]
        self.half = []
        for i in range(5):
            b = k.ps(pfx + "pb%d" % i, [128, 512], F32)
            self.half.append(b.sub(b.t[:, 0:256], pfx + "ph%da" % i))
            self.half.append(b.sub(b.t[:, 256:512], pfx + "ph%db" % i))


DN_STAGE = 99
DBG_SKIP = False


def phase_dn(k, c, P, hnT, win, cw, hp, onw, o0T, nb, nheads, pfx, dbg=None, out_cb=None):
    nt = nb * 3
    TT = nt * 128
    mmA, mmB, DD = P.full
    GBg, KQ, KS, CHA, CHB, PP, VN, OO, SU, TR = P.half
    cf, cb = c["f"], c["b"]
    hs_ = [slice(0, 128), slice(128, 256)]

    w_sb = k.sb(pfx + "w", [128, NC_, DNW], BF16)
    hbk = [k.sb(pfx + "hb%d" % i, [128, NC_, TB], BF16) for i in range(2)]
    kq = k.sb(pfx + "kq", [128, nt, 256], BF16)
    vT = k.sb(pfx + "vT", [128, 2, TT], BF16)
    zsw = k.sb(pfx + "zsw", [128, nt, 256], BF16)
    oT = k.sb(pfx + "oT", [128, 2, TT], BF16)
    betas = k.sb(pfx + "beta", [128, nt, 2], F32)
    negb = k.sb(pfx + "negb", [128, nt, 2], F32)
    alog = k.sb(pfx + "alog", [128, nt, 2], F32)
    gstep = k.sb(pfx + "gstep", [128, nt, 2], F32)
    cwt = k.sb(pfx + "cw", [128, nheads, 4, 4], F32)
    hpt = k.sb(pfx + "hp", [128, nheads, 4], F32)
    negA = k.sb(pfx + "negA", [128, nheads, 2], F32)
    onwt = k.sb(pfx + "onw", [128, 256], F32)
    rawb = [k.sb(pfx + "raw%d" % i, [128, TB + 3], F32) for i in range(4)]
    acc = k.sb(pfx + "acc", [128, TB], F32)
    yb = k.sb(pfx + "y", [128, TB], F32)
    sqb = k.sb(pfx + "sq", [128, TB], BF16)
    sb_ = k.sb(pfx + "s", [128, TB], F32)
    rn = k.sb(pfx + "rn", [128, TB], F32)
    zt = k.sb(pfx + "zt", [128, 256], BF16)
    RG1 = k.sb(pfx + "RG1", [128, 256], F32)
    RG2 = k.sb(pfx + "RG2", [128, 256], F32)
    decs = k.sb(pfx + "decs", [128, 512], F32)
    EB = k.sb(pfx + "EB", [128, 256], F32)
    eg = k.sb(pfx + "eg", [128, 4], F32)
    t1 = k.sb(pfx + "t1", [128, 256], F32)
    t2 = k.sb(pfx + "t2", [128, 256], F32)
    Ab = [k.sb(pfx + "A%d" % i, [128, 256], F32) for i in range(2)]
    Bb = [k.sb(pfx + "B%d" % i, [128, 256], F32) for i in range(2)]
    Pb = [k.sb(pfx + "P%d" % i, [128, 256], F32) for i in range(2)]
    attnT = k.sb(pfx + "attnT", [128, 256], BF16)
    WT = k.sb(pfx + "WT", [128, 256], BF16)
    qdecT = k.sb(pfx + "qdecT", [128, 256], BF16)
    vtok = k.sb(pfx + "vtok", [128, 256], BF16)
    kdec = k.sb(pfx + "kdec", [128, 256], BF16)
    R = k.sb(pfx + "R", [128, 256], BF16)
    vn = k.sb(pfx + "vn", [128, 256], BF16)
    S = k.sb(pfx + "S", [128, 256], F32)
    Sb = k.sb(pfx + "Sb", [128, 256], BF16)
    ssq = k.sb(pfx + "ssq", [128, 8], F32)
    og = k.sb(pfx + "og", [128, 256], BF16)
    junk = k.sb(pfx + "junk", [128, 256], BF16)
    TRb = TR[:, :].bitcast(BF16)

    k.dma("sp", cwt[:, :, :, :], cw[:, :, :, :], [cw], [cwt])
    k.dma("sp", hpt[:, :, :], hp[:, :, :], [hp], [hpt])
    k.dma("sp", onwt[:, :], onw[:, :], [onw], [onwt])
    act(k, negA[:, :, :], hpt[:, :, 2:4], AF.Exp, [hpt], [negA])
    ts(k, "dve", negA[:, :, :], negA[:, :, :], -1.0, ALU.mult, [negA], [negA])

    bi = 0
    for j in range(nheads):
        for q4 in range(4):
            k.dma("pool", w_sb[:, q4 * 8:(q4 + 1) * 8, :],
                  win[j, q4 * 1024:(q4 + 1) * 1024, :].rearrange("(c p) n -> p c n", p=128),
                  [win], [w_sb])
        for fc in range(4):
            k.op("dve", lambda en: en.memset(rawb[fc][:, 0:3], 0.0), [], [rawb[fc]])
        for tb in range(nb):
            hb = hbk[bi % 2]
            bi += 1
            k.dma("sp", hb[:, :, :], hnT[tb], [hnT], [hb])
            for fc in range(4):
                ps = mmA if fc % 2 == 0 else mmB
                for cc in range(NC_):
                    mm(k, ps[:, 0:TB], w_sb[:, cc, fc * 128:(fc + 1) * 128], hb[:, cc, :],
                       cc == 0, cc == NC_ - 1, [w_sb, hb], [ps])
                rb = rawb[fc]
                cp(k, "act", rb[:, 3:TB + 3], ps[:, 0:TB], [ps], [rb])
                ts(k, "dve", acc[:, :], rb[:, 3:TB + 3], cwt[:, j, fc, 3:4], ALU.mult, [rb, cwt], [acc])
                for tap in (2, 1, 0):
                    stt(k, acc[:, :], rb[:, tap:tap + TB], cwt[:, j, fc, tap:tap + 1], acc[:, :],
                        ALU.mult, ALU.add, [rb, cwt, acc], [acc])
                cp(k, "dve", rb[:, 0:3], rb[:, TB:TB + 3], [rb], [rb])
                if fc >= 2:
                    act(k, vT[:, fc - 2, tb * TB:(tb + 1) * TB], acc[:, :], AF.Silu, [acc], [vT])
                else:
                    act(k, yb[:, :], acc[:, :], AF.Silu, [acc], [yb])
                    act(k, sqb[:, :], yb[:, :], AF.Square, [yb], [sqb])
                    mm(k, DD[:, 0:TB], c["onesb"](), sqb[:, :], True, True, [cb, sqb], [DD])
                    act(k, sb_[:, :], DD[:, 0:TB], AF.Sqrt, [DD], [sb_], bias=EPS)
                    k.op("dve", lambda en: en.reciprocal(out=rn[:, :], in_=sb_[:, :]), [sb_], [rn])
                    if fc == 0:
                        stt(k, kq[:, 3 * tb:3 * tb + 3, 128:256], v3(yb[:, :]), 128.0 ** -0.5, v3(rn[:, :]),
                            ALU.mult, ALU.mult, [yb, rn], [kq])
                    else:
                        tt(k, "dve", kq[:, 3 * tb:3 * tb + 3, 0:128], v3(yb[:, :]), v3(rn[:, :]), ALU.mult,
                           [yb, rn], [kq])
            for tl in range(3):
                i = 3 * tb + tl
                ps = mmA if tl % 2 == 0 else mmB
                for cc in range(NC_):
                    mm(k, ps[:, 0:260], hb[:, cc, tl * 128:(tl + 1) * 128], w_sb[:, cc, 512:772],
                       cc == 0, cc == NC_ - 1, [w_sb, hb], [ps])
                act(k, zt[:, :], ps[:, 0:256], AF.Silu, [ps], [zt])
                tt(k, "dve", zsw[:, i, :], zt[:, :], onwt[:, :], ALU.mult, [zt, onwt], [zsw])
                act(k, betas[:, i, :], ps[:, 256:258], AF.Sigmoid, [ps], [betas])
                tt(k, "dve", alog[:, i, :], ps[:, 258:260], hpt[:, j, 0:2], ALU.add, [ps, hpt], [alog])
        act(k, alog[:, :, :], alog[:, :, :], AF.Exp, [alog], [alog])
        act(k, alog[:, :, :], alog[:, :, :], AF.Ln, [alog], [alog], bias=1.0)
        for h in range(2):
            ts(k, "dve", gstep[:, :, h], alog[:, :, h], negA[:, j, h:h + 1], ALU.mult, [alog, negA], [gstep])
        ts(k, "dve", negb[:, :, :], betas[:, :, :], -1.0, ALU.mult, [betas], [negb])
        k.op("dve", lambda en: en.memset(S[:, :], 0.0), [], [S])
        k.op("dve", lambda en: en.memset(Sb[:, :], 0.0), [], [Sb])

        if dbg is not None:
            k.dma("sp", dbg["kq"][:, :, :], kq[:, :, :], [kq], [dbg["kq"]])
            k.dma("sp", dbg["vT"][:, :, :], vT[:, :, :], [vT], [dbg["vT"]])
            k.dma("sp", dbg["zsw"][:, :, :], zsw[:, :, :], [zsw], [dbg["zsw"]])
            k.dma("sp", dbg["beta"][:, :, :], betas[:, :, :], [betas], [dbg["beta"]])
            k.dma("sp", dbg["gstep"][:, :, :], gstep[:, :, :], [gstep], [dbg["gstep"]])
        for n in range(nt if DN_STAGE >= 2 else 0):
            ST = DN_STAGE
            kT = kq[:, n, 0:128]
            qT = kq[:, n, 128:256]
            for h in range(2):
                ts(k, "dve", RG1[:, hs_[h]], c["U"](), gstep[:, n, h:h + 1], ALU.mult, [cf, gstep], [RG1])
                ts(k, "pool", RG2[:, hs_[h]], c["Ls"](), gstep[:, n, h:h + 1], ALU.mult, [cf, gstep], [RG2])
            mm(k, DD[:, 0:256], c["U"](), RG2[:, :], True, True, [cf, RG2], [DD])
            mm(k, DD[:, 256:512], c["Ls"](), RG1[:, :], True, True, [cf, RG1], [DD])
            mm(k, GBg[:, :], c["ones"](), RG1[:, :], True, True, [cf, RG1], [GBg])
            mm(k, TR[:, 0:2], c["U"](), gstep[:, n, 0:2], True, True, [cf, gstep], [TR])
            act(k, decs[:, :], DD[:, :], AF.Exp, [DD], [decs])
            act(k, EB[:, :], GBg[:, :], AF.Exp, [GBg], [EB])
            act(k, eg[:, 0:2], TR[:, 0:2], AF.Exp, [TR], [eg])
            ts(k, "dve", eg[:, 2:4], eg[:, 0:2], -1.0, ALU.mult, [eg], [eg])
            if ST < 3:
                continue
            mm(k, KQ[:, :], kT, kq[:, n, :], True, True, [kq], [KQ])
            A0, B0 = Ab[0], Bb[0]
            for h in range(2):
                tt(k, "dve", t1[:, hs_[h]], KQ[:, 0:128], decs[:, h * 128:(h + 1) * 128], ALU.mult,
                   [KQ, decs], [t1])
                stt(k, A0[:, hs_[h]], t1[:, hs_[h]], negb[:, n, h:h + 1], c["Ls"](), ALU.mult, ALU.mult,
                    [t1, negb, cf], [A0])
                tt(k, "dve", t2[:, hs_[h]], KQ[:, 128:256], decs[:, 256 + h * 128:256 + (h + 1) * 128],
                   ALU.mult, [KQ, decs], [t2])
                tt(k, "pool", attnT[:, hs_[h]], t2[:, hs_[h]], c["U"](), ALU.mult, [t2, cf], [attnT])
            if ST < 4:
                continue
            for h in range(2):
                tr(k, TR[:, hs_[h]], A0[:, hs_[h]], c["I"](), [A0, cf], [TR])
            cp(k, "act", B0[:, :], TR[:, :], [TR], [B0])
            if ST < 5:
                continue
            ai, pi = 0, 0
            tt(k, "pool", Pb[0][:, :], c["I2"](), B0[:, :], ALU.add, [cf, B0], [Pb[0]])
            for m in range(1, 7):
                Ac, Bc = Ab[ai], Bb[ai]
                An, Bn = Ab[1 - ai], Bb[1 - ai]
                for h in range(2):
                    mm(k, CHA[:, hs_[h]], Bc[:, hs_[h]], Ac[:, hs_[h]], True, True, [Ac, Bc], [CHA])
                if m <= 5:
                    for h in range(2):
                        mm(k, CHB[:, hs_[h]], Ac[:, hs_[h]], Bc[:, hs_[h]], True, True, [Ac, Bc], [CHB])
                cp(k, "act", An[:, :], CHA[:, :], [CHA], [An])
                if m <= 5:
                    cp(k, "dve", Bn[:, :], CHB[:, :], [CHB], [Bn])
                Pc, Pn = Pb[pi], Pb[1 - pi]
                for h in range(2):
                    mm(k, PP[:, hs_[h]], An[:, hs_[h]], Pc[:, hs_[h]], True, True, [An, Pc], [PP])
                tt(k, "dve", Pn[:, :], Pc[:, :], PP[:, :], ALU.add, [Pc, PP], [Pn])
                ai, pi = 1 - ai, 1 - pi
            Pf = Pb[pi]
            for h in range(2):
                ts(k, "dve", WT[:, hs_[h]], Pf[:, hs_[h]], betas[:, n, h:h + 1], ALU.mult, [Pf, betas], [WT])
                tt(k, "pool", qdecT[:, hs_[h]], qT, EB[:, hs_[h]], ALU.mult, [kq, EB], [qdecT])
            if ST < 6:
                continue
            for h in range(2):
                tr(k, TRb[:, hs_[h]], vT[:, h, n * 128:(n + 1) * 128], c["Ib"](), [vT, cb], [TR])
            cp(k, "act", vtok[:, :], TRb[:, 0:256], [TR], [vtok])
            tr(k, TRb[:, 0:128], kT, c["Ib"](), [kq, cb], [TR])
            for h in range(2):
                col = 256 + h * 128 + 127
                ts(k, "dve", kdec[:, hs_[h]], TRb[:, 0:128], decs[:, col:col + 1], ALU.mult, [TR, decs], [kdec])
            if ST < 7:
                continue
            mm(k, KS[:, :], kT, Sb[:, :], True, True, [kq, Sb], [KS])
            for h in range(2):
                stt(k, R[:, hs_[h]], KS[:, hs_[h]], eg[:, 2 + h:3 + h], vtok[:, hs_[h]], ALU.mult, ALU.add,
                    [KS, eg, vtok], [R])
            for h in range(2):
                mm(k, VN[:, hs_[h]], WT[:, hs_[h]], R[:, hs_[h]], True, True, [WT, R], [VN])
            cp(k, "act", vn[:, :], VN[:, :], [VN], [vn])
            for h in range(2):
                mm(k, OO[:, hs_[h]], qdecT[:, hs_[h]], Sb[:, hs_[h]], True, False, [qdecT, Sb], [OO])
                mm(k, OO[:, hs_[h]], attnT[:, hs_[h]], vn[:, hs_[h]], False, True, [attnT, vn], [OO])
            for h in range(2):
                mm(k, SU[:, hs_[h]], kdec[:, hs_[h]], vn[:, hs_[h]], True, True, [kdec, vn], [SU])
            for h in range(2):
                col = h * 128 + 127
                stt(k, S[:, hs_[h]], S[:, hs_[h]], EB[:, col:col + 1], SU[:, hs_[h]], ALU.mult, ALU.add,
                    [S, EB, SU], [S])
            cp(k, "act", Sb[:, :], S[:, :], [S], [Sb])
            for h in range(2):
                act(k, junk[:, hs_[h]], OO[:, hs_[h]], AF.Square, [OO], [junk, ssq], accum=ssq[:, h:h + 1])
            act(k, ssq[:, 2:4], ssq[:, 0:2], AF.Sqrt, [ssq], [ssq], bias=EPS, scale=1.0 / 128)
            k.op("dve", lambda en: en.reciprocal(out=ssq[:, 4:6], in_=ssq[:, 2:4]), [ssq], [ssq])
            for h in range(2):
                stt(k, og[:, hs_[h]], OO[:, hs_[h]], ssq[:, 4 + h:5 + h], zsw[:, n, hs_[h]], ALU.mult, ALU.mult,
                    [OO, ssq, zsw], [og])
            for h in range(2):
                tr(k, TRb[:, hs_[h]], og[:, hs_[h]], c["Ib"](), [og, cb], [TR])
            cp(k, "act", oT[:, :, n * 128:(n + 1) * 128], v3(TRb[:, 0:256]), [TR], [oT])
        if out_cb is not None:
            out_cb(j, oT)
        else:
            for h in range(2):
                k.dma("sp", o0T[:, :, 2 * j + h, :].rearrange("t p x -> p t x"), v3(oT[:, h, :], TB), [oT], [o0T])


def run_chains(pre_gens, state_gen_fn, n, width):
    pres = {}
    nxt = 0
    done_pre = set()
    st = None
    st_i = 0
    while st_i < n:
        while nxt < n and len(pres) < width and nxt < st_i + width:
            pres[nxt] = pre_gens(nxt)
            nxt += 1
        for i in sorted(pres):
            try:
                next(pres[i])
            except StopIteration:
                done_pre.add(i)
                del pres[i]
        if st is None and st_i in done_pre:
            st = state_gen_fn(st_i)
        if st is not None:
            try:
                next(st)
            except StopIteration:
                st = None
                st_i += 1


def phase_dn2(k, c, hnT, win, cw, hp, onw, nb, nheads, pfx, out_cb, NS=3):
    nt = nb * 3
    TT = nt * 128
    banks = [k.ps(pfx + "bk%d" % i, [128, 512], F32) for i in range(8)]
    mmA, mmB, Z1, Z2 = banks[0:4]
    DD = Z2
    cf, cb = c["f"], c["b"]
    hs_ = [slice(0, 128), slice(128, 256)]

    w_sb = k.sb(pfx + "w", [128, NC_, DNW], BF16)
    hbk = [k.sb(pfx + "hb%d" % i, [128, NC_, TB], BF16) for i in range(2)]
    kq = k.sb(pfx + "kq", [128, nt, 256], BF16)
    vT = k.sb(pfx + "vT", [128, 2, TT], BF16)
    zsw = k.sb(pfx + "zsw", [128, nt, 256], BF16)
    ostg = [k.sb(pfx + "ostg%d" % i, [128, 2, TB], BF16) for i in range(2)]
    betas = k.sb(pfx + "beta", [128, nt, 2], F32)
    negb = k.sb(pfx + "negb", [128, nt, 2], F32)
    alog = k.sb(pfx + "alog", [128, nt, 2], F32)
    gstep = k.sb(pfx + "gstep", [128, nt, 2], F32)
    cwt = k.sb(pfx + "cw", [128, nheads, 4, 4], F32)
    hpt = k.sb(pfx + "hp", [128, nheads, 4], F32)
    negA = k.sb(pfx + "negA", [128, nheads, 2], F32)
    onwt = k.sb(pfx + "onw", [128, 256], F32)
    rawb = [k.sb(pfx + "raw%d" % i, [128, TB + 3], F32) for i in range(4)]
    sqb = k.sb(pfx + "sq", [128, TB], BF16)
    sb_ = k.sb(pfx + "s", [128, TB], F32)
    rn = k.sb(pfx + "rn", [128, TB], F32)
    zt = k.sb(pfx + "zt", [128, 256], BF16)
    junk = zt

    class Set:
        pass
    sets = []
    for i in range(NS):
        S_ = Set()
        S_.X, S_.Y = (banks[4 + 2 * i], banks[5 + 2 * i]) if i < 2 else (mmA, mmB)
        S_.X1 = k.sb(pfx + "X1_%d" % i, [128, 256], F32)
        S_.X2 = k.sb(pfx + "X2_%d" % i, [128, 256], F32)
        S_.decs = k.sb(pfx + "decs%d" % i, [128, 512], F32)
        S_.EB = k.sb(pfx + "EB%d" % i, [128, 256], F32)
        S_.eg = k.sb(pfx + "eg%d" % i, [128, 8], F32)
        S_.A = [k.sb(pfx + "A%d_%d" % (i, q), [128, 256], F32) for q in range(2)]
        S_.B = [k.sb(pfx + "B%d_%d" % (i, q), [128, 256], F32) for q in range(2)]
        S_.P = k.sb(pfx + "P%d" % i, [128, 256], F32)
        S_.attnT = k.sb(pfx + "attnT%d" % i, [128, 256], BF16)
        S_.WT = k.sb(pfx + "WT%d" % i, [128, 256], BF16)
        S_.qdecT = k.sb(pfx + "qdecT%d" % i, [128, 256], BF16)
        S_.vtok = k.sb(pfx + "vtok%d" % i, [128, 256], BF16)
        S_.kdec = k.sb(pfx + "kdec%d" % i, [128, 256], BF16)
        sets.append(S_)
    acc = sets[0].decs.sub(sets[0].decs.t[:, 0:TB], pfx + "acc")
    ybs = [sets[1 % NS].decs.sub(sets[1 % NS].decs.t[:, 0:TB], pfx + "yb0"),
           sets[2 % NS].decs.sub(sets[2 % NS].decs.t[:, 0:TB], pfx + "yb1")]
    deferred = []
    R = k.sb(pfx + "R", [128, 256], BF16)
    vn = k.sb(pfx + "vn", [128, 256], BF16)
    S = k.sb(pfx + "S", [128, 256], F32)
    Sb = k.sb(pfx + "Sb", [128, 256], BF16)
    ssq = k.sb(pfx + "ssq", [128, 8], F32)
    og = k.sb(pfx + "og", [128, 256], BF16)

    k.dma("sp", cwt[:, :, :, :], cw[:, :, :, :], [cw], [cwt])
    k.dma("sp", hpt[:, :, :], hp[:, :, :], [hp], [hpt])
    k.dma("sp", onwt[:, :], onw[:, :], [onw], [onwt])
    act(k, negA[:, :, :], hpt[:, :, 2:4], AF.Exp, [hpt], [negA])
    ts(k, "dve", negA[:, :, :], negA[:, :, :], -1.0, ALU.mult, [negA], [negA])

    def load_w(jj):
        for q4 in range(4):
            k.dma("pool", w_sb[:, q4 * 8:(q4 + 1) * 8, :],
                  win[jj, q4 * 1024:(q4 + 1) * 1024, :].rearrange("(c p) n -> p c n", p=128),
                  [win], [w_sb])

    bi = 0
    for j in range(nheads):
        if j == 0:
            load_w(0)
        for fc in range(4):
            k.op("dve", lambda en: en.memset(rawb[fc][:, 0:3], 0.0), [], [rawb[fc]])
        for tb in range(nb):
            hb = hbk[bi % 2]
            bi += 1
            k.dma("sp", hb[:, :, :], hnT[tb], [hnT], [hb])
            for fc in range(4):
                ps = mmA if fc % 2 == 0 else mmB
                for cc in range(NC_):
                    mm(k, ps[:, 0:TB], w_sb[:, cc, fc * 128:(fc + 1) * 128], hb[:, cc, :],
                       cc == 0, cc == NC_ - 1, [w_sb, hb], [ps])
                while deferred:
                    deferred.pop(0)()
                rb = rawb[fc]
                cp(k, "act", rb[:, 3:TB + 3], ps[:, 0:TB], [ps], [rb])
                ts(k, "dve", acc[:, :], rb[:, 3:TB + 3], cwt[:, j, fc, 3:4], ALU.mult, [rb, cwt], [acc])
                for tap in (2, 1, 0):
                    stt(k, acc[:, :], rb[:, tap:tap + TB], cwt[:, j, fc, tap:tap + 1], acc[:, :],
                        ALU.mult, ALU.add, [rb, cwt, acc], [acc])
                cp(k, "dve", rb[:, 0:3], rb[:, TB:TB + 3], [rb], [rb])
                if fc >= 2:
                    act(k, vT[:, fc - 2, tb * TB:(tb + 1) * TB], acc[:, :], AF.Silu, [acc], [vT])
                else:
                    ybf = ybs[fc]
                    act(k, ybf[:, :], acc[:, :], AF.Silu, [acc], [ybf])

                    def norm_part(fc=fc, tb=tb, ybf=ybf):
                        act(k, sqb[:, :], ybf[:, :], AF.Square, [ybf], [sqb])
                        mm(k, DD[:, 0:TB], c["onesb"](), sqb[:, :], True, True, [cb, sqb], [DD])
                        act(k, sb_[:, :], DD[:, 0:TB], AF.Sqrt, [DD], [sb_], bias=EPS)
                        k.op("dve", lambda en: en.reciprocal(out=rn[:, :], in_=sb_[:, :]), [sb_], [rn])
                        if fc == 0:
                            stt(k, kq[:, 3 * tb:3 * tb + 3, 128:256], v3(ybf[:, :]), 128.0 ** -0.5, v3(rn[:, :]),
                                ALU.mult, ALU.mult, [ybf, rn], [kq])
                        else:
                            tt(k, "dve", kq[:, 3 * tb:3 * tb + 3, 0:128], v3(ybf[:, :]), v3(rn[:, :]), ALU.mult,
                               [ybf, rn], [kq])
                    deferred.append(norm_part)
            for tl in range(3):
                i = 3 * tb + tl
                ps = mmA if tl % 2 == 0 else mmB
                for cc in range(NC_):
                    mm(k, ps[:, 0:260], hb[:, cc, tl * 128:(tl + 1) * 128], w_sb[:, cc, 512:772],
                       cc == 0, cc == NC_ - 1, [w_sb, hb], [ps])
                while deferred:
                    deferred.pop(0)()
                act(k, zt[:, :], ps[:, 0:256], AF.Silu, [ps], [zt])
                tt(k, "dve", zsw[:, i, :], zt[:, :], onwt[:, :], ALU.mult, [zt, onwt], [zsw])
                act(k, betas[:, i, :], ps[:, 256:258], AF.Sigmoid, [ps], [betas])
                tt(k, "dve", alog[:, i, :], ps[:, 258:260], hpt[:, j, 0:2], ALU.add, [ps, hpt], [alog])
        act(k, alog[:, :, :], alog[:, :, :], AF.Exp, [alog], [alog])
        act(k, alog[:, :, :], alog[:, :, :], AF.Ln, [alog], [alog], bias=1.0)
        for h in range(2):
            ts(k, "dve", gstep[:, :, h], alog[:, :, h], negA[:, j, h:h + 1], ALU.mult, [alog, negA], [gstep])
        ts(k, "dve", negb[:, :, :], betas[:, :, :], -1.0, ALU.mult, [betas], [negb])
        k.op("dve", lambda en: en.memset(S[:, :], 0.0), [], [S])
        k.op("dve", lambda en: en.memset(Sb[:, :], 0.0), [], [Sb])

        if j + 1 < nheads:
            load_w(j + 1)
        def pre(n, j=j):
            st_ = sets[n % NS]
            X, Y = st_.X, st_.Y
            Xb = X[:, :].bitcast(BF16)
            Yb = Y[:, :].bitcast(BF16)
            kT = kq[:, n, 0:128]
            qT = kq[:, n, 128:256]
            RG1, RG2, decs, EB, eg = st_.X1, st_.X2, st_.decs, st_.EB, st_.eg
            for h in range(2):
                ts(k, "dve", RG1[:, hs_[h]], c["U"](), gstep[:, n, h:h + 1], ALU.mult, [cf, gstep], [RG1])
                ts(k, "pool", RG2[:, hs_[h]], c["Ls"](), gstep[:, n, h:h + 1], ALU.mult, [cf, gstep], [RG2])
            yield
            mm(k, X[:, 0:256], c["U"](), RG2[:, :], True, True, [cf, RG2], [X])
            mm(k, X[:, 256:512], c["Ls"](), RG1[:, :], True, True, [cf, RG1], [X])
            mm(k, Y[:, 0:256], c["ones"](), RG1[:, :], True, True, [cf, RG1], [Y])
            mm(k, Y[:, 256:258], c["U"](), gstep[:, n, 0:2], True, True, [cf, gstep], [Y])
            yield
            act(k, decs[:, :], X[:, :], AF.Exp, [X], [decs])
            act(k, EB[:, :], Y[:, 0:256], AF.Exp, [Y], [EB])
            act(k, eg[:, 0:2], Y[:, 256:258], AF.Exp, [Y], [eg])
            yield
            ts(k, "dve", eg[:, 2:4], eg[:, 0:2], -1.0, ALU.mult, [eg], [eg])
            for h in range(2):
                col = h * 128 + 127
                cp(k, "dve", eg[:, 4 + h:5 + h], EB[:, col:col + 1], [EB], [eg])
            mm(k, X[:, 0:256], kT, kq[:, n, :], True, True, [kq], [X])
            yield
            A0, B0 = st_.A[0], st_.B[0]
            t1, t2 = RG1, RG2
            for h in range(2):
                tt(k, "dve", t1[:, hs_[h]], X[:, 0:128], decs[:, h * 128:(h + 1) * 128], ALU.mult, [X, decs], [t1])
                stt(k, A0[:, hs_[h]], t1[:, hs_[h]], negb[:, n, h:h + 1], c["Ls"](), ALU.mult, ALU.mult,
                    [t1, negb, cf], [A0])
            yield
            for h in range(2):
                tr(k, Y[:, hs_[h]], A0[:, hs_[h]], c["I"](), [A0, cf], [Y])
            for h in range(2):
                tt(k, "dve", t2[:, hs_[h]], X[:, 128:256], decs[:, 256 + h * 128:256 + (h + 1) * 128],
                   ALU.mult, [X, decs], [t2])
                tt(k, "pool", st_.attnT[:, hs_[h]], t2[:, hs_[h]], c["U"](), ALU.mult, [t2, cf], [st_.attnT])
            yield
            cp(k, "act", B0[:, :], Y[:, 0:256], [Y], [B0])
            yield
            Pm = st_.P
            tt(k, "pool", Pm[:, :], c["I2"](), B0[:, :], ALU.add, [cf, B0], [Pm])
            ai = 0
            for m in range(1, 7):
                Ac, Bc = st_.A[ai], st_.B[ai]
                An, Bn = st_.A[1 - ai], st_.B[1 - ai]
                for h in range(2):
                    mm(k, X[:, hs_[h]], Bc[:, hs_[h]], Ac[:, hs_[h]], True, True, [Ac, Bc], [X])
                if m <= 5:
                    for h in range(2):
                        mm(k, Y[:, hs_[h]], Ac[:, hs_[h]], Bc[:, hs_[h]], True, True, [Ac, Bc], [Y])
                yield
                cp(k, "act", An[:, :], X[:, 0:256], [X], [An])
                if m <= 5:
                    cp(k, "dve", Bn[:, :], Y[:, 0:256], [Y], [Bn])
                yield
                for h in range(2):
                    mm(k, X[:, 256 + h * 128:384 + h * 128], An[:, hs_[h]], Pm[:, hs_[h]], True, True, [An, Pm], [X])
                yield
                tt(k, "dve", Pm[:, :], Pm[:, :], X[:, 256:512], ALU.add, [Pm, X], [Pm])
                ai = 1 - ai
            yield
            for h in range(2):
                ts(k, "dve", st_.WT[:, hs_[h]], Pm[:, hs_[h]], betas[:, n, h:h + 1], ALU.mult, [Pm, betas], [st_.WT])
                tt(k, "pool", st_.qdecT[:, hs_[h]], qT, EB[:, hs_[h]], ALU.mult, [kq, EB], [st_.qdecT])
            for h in range(2):
                tr(k, Yb[:, hs_[h]], vT[:, h, n * 128:(n + 1) * 128], c["Ib"](), [vT, cb], [Y])
            tr(k, Xb[:, 0:128], kT, c["Ib"](), [kq, cb], [X])
            yield
            cp(k, "act", st_.vtok[:, :], Yb[:, 0:256], [Y], [st_.vtok])
            for h in range(2):
                col = 256 + h * 128 + 127
                ts(k, "dve", st_.kdec[:, hs_[h]], Xb[:, 0:128], decs[:, col:col + 1], ALU.mult, [X, decs], [st_.kdec])
            yield

        def state(n, j=j):
            st_ = sets[n % NS]
            eg = st_.eg
            kT = kq[:, n, 0:128]
            Z1b = Z1[:, :].bitcast(BF16)
            mm(k, Z1[:, 0:256], kT, Sb[:, :], True, True, [kq, Sb], [Z1])
            yield
            for h in range(2):
                stt(k, R[:, hs_[h]], Z1[:, hs_[h]], eg[:, 2 + h:3 + h], st_.vtok[:, hs_[h]], ALU.mult, ALU.add,
                    [Z1, eg, st_.vtok], [R])
            yield
            for h in range(2):
                mm(k, Z2[:, hs_[h]], st_.WT[:, hs_[h]], R[:, hs_[h]], True, True, [st_.WT, R], [Z2])
            yield
            cp(k, "act", vn[:, :], Z2[:, 0:256], [Z2], [vn])
            yield
            for h in range(2):
                mm(k, Z2[:, 256 + h * 128:384 + h * 128], st_.kdec[:, hs_[h]], vn[:, hs_[h]], True, True,
                   [st_.kdec, vn], [Z2])
            for h in range(2):
                mm(k, Z1[:, 256 + h * 128:384 + h * 128], st_.qdecT[:, hs_[h]], Sb[:, hs_[h]], True, False,
                   [st_.qdecT, Sb], [Z1])
                mm(k, Z1[:, 256 + h * 128:384 + h * 128], st_.attnT[:, hs_[h]], vn[:, hs_[h]], False, True,
                   [st_.attnT, vn], [Z1])
            yield
            for h in range(2):
                stt(k, S[:, hs_[h]], S[:, hs_[h]], eg[:, 4 + h:5 + h], Z2[:, 256 + h * 128:384 + h * 128],
                    ALU.mult, ALU.add, [S, eg, Z2], [S])
            yield
            cp(k, "act", Sb[:, :], S[:, :], [S], [Sb])
            for h in range(2):
                act(k, junk[:, hs_[h]], Z1[:, 256 + h * 128:384 + h * 128], AF.Square, [Z1], [junk, ssq],
                    accum=ssq[:, h:h + 1])
            yield
            act(k, ssq[:, 2:4], ssq[:, 0:2], AF.Sqrt, [ssq], [ssq], bias=EPS, scale=1.0 / 128)
            yield
            k.op("dve", lambda en: en.reciprocal(out=ssq[:, 4:6], in_=ssq[:, 2:4]), [ssq], [ssq])
            for h in range(2):
                stt(k, og[:, hs_[h]], Z1[:, 256 + h * 128:384 + h * 128], ssq[:, 4 + h:5 + h], zsw[:, n, hs_[h]],
                    ALU.mult, ALU.mult, [Z1, ssq, zsw], [og])
            yield
            for h in range(2):
                tr(k, Z1b[:, hs_[h]], og[:, hs_[h]], c["Ib"](), [og, cb], [Z1])
            yield
            stg = ostg[(n // 3) % 2]
            tl = n % 3
            cp(k, "act", stg[:, :, tl * 128:(tl + 1) * 128], v3(Z1b[:, 0:256]), [Z1], [stg])
            if tl == 2:
                out_cb(j, n // 3, stg)
            yield

        run_chains(pre, state, nt if not DBG_SKIP else 0, NS)


def phase_outproj(k, P, xTs, kcs, wout, hres, out, nb, pfx, ssq_out=None, out_rows=None, load_cb=None,
                  wq="pool"):
    KC = sum(kcs)
    nt = nb * 3
    mmA, mmB, DD = P.full
    w_sb = k.sb(pfx + "w", [128, KC, 512], BF16)
    xb = [k.sb(pfx + "xb%d" % i, [128, KC, TB], BF16) for i in range(2)]
    res = [k.sb(pfx + "res%d" % i, [128, 512], F32) for i in range(2)]
    ot = [k.sb(pfx + "ot%d" % i, [128, 512], F32) for i in range(2)]
    junk = k.sb(pfx + "junk", [128, 512], BF16)
    ssa = k.sb(pfx + "ssa", [128, 2, nt], F32)
    bi = 0
    ti = 0
    for cbk in range(2):
        cs = slice(cbk * 512, (cbk + 1) * 512)
        for q8 in range(KC // 8):
            k.dma(wq, w_sb[:, q8 * 8:(q8 + 1) * 8, :],
                  wout[q8 * 1024:(q8 + 1) * 1024, cs].rearrange("(c p) n -> p c n", p=128), [wout], [w_sb])
        for tb in range(nb):
            xbb = xb[bi % 2]
            bi += 1
            off = 0
            if load_cb is not None:
                load_cb(tb, xbb)
            else:
                for xT, kc in zip(xTs, kcs):
                    k.dma("sp", xbb[:, off:off + kc, :], xT[tb], [xT], [xbb])
                    off += kc
            for tl in range(3):
                i = 3 * tb + tl
                ps = mmA if ti % 2 == 0 else mmB
                rs, o_ = res[ti % 2], ot[ti % 2]
                ti += 1
                k.dma("sp", rs[:, :], hres[i * 128:(i + 1) * 128, cs], [hres], [rs])
                for cc in range(KC):
                    mm(k, ps[:, :], xbb[:, cc, tl * 128:(tl + 1) * 128], w_sb[:, cc, :], cc == 0, cc == KC - 1,
                       [xbb, w_sb], [ps])
                tt(k, "dve", o_[:, :], ps[:, :], rs[:, :], ALU.add, [ps, rs], [o_])
                if ssq_out is not None:
                    act(k, junk[:, :], o_[:, :], AF.Square, [o_], [junk, ssa], accum=ssa[:, cbk, i:i + 1])
                if out_rows is None:
                    k.dma("sp", out[i * 128:(i + 1) * 128, cs], o_[:, :], [o_], [out])
                else:
                    for dst_rows, src_rows in out_rows(i):
                        k.dma("sp", out[dst_rows, cs], o_[src_rows, :], [o_], [out])
    if ssq_out is not None:
        tt(k, "dve", ssa[:, 0, :], ssa[:, 0, :], ssa[:, 1, :], ALU.add, [ssa], [ssa])
        k.dma("sp", ssq_out[:, :], ssa[:, 0, :], [ssa], [ssq_out])


def phase_norm2(k, c, P, ssq_all, h1, nw, dst, nb, pfx, ssq_view=None, out_cb=None):
    nt = nb * 3
    TR = P.half[9]
    TRb = TR[:, :].bitcast(BF16)
    cb = c["b"]
    sq = k.sb(pfx + "sq", [128, 4, nt], F32)
    rstd = k.sb(pfx + "rstd", [128, nt], F32)
    wB = k.sb(pfx + "wB", [128, 1024], F32)
    xts = [k.sb(pfx + "xt%d" % i, [128, 1024], F32) for i in range(2)]
    hnb = [k.sb(pfx + "hn%d" % i, [128, 1024], BF16) for i in range(2)]
    st = [k.sb(pfx + "st%d" % i, [128, 8, TB], BF16) for i in range(2)]
    k.dma("sp", sq[:, :, :], ssq_view if ssq_view is not None else ssq_all[:, :, :].rearrange("g p n -> p g n"),
          [ssq_all], [sq])
    k.dma("sp", wB[:, :], nw[:, :], [nw], [wB])
    for g in range(1, 4):
        tt(k, "dve", sq[:, 0, :], sq[:, 0, :], sq[:, g, :], ALU.add, [sq], [sq])
    act(k, sq[:, 1, :], sq[:, 0, :], AF.Sqrt, [sq], [sq], bias=EPS, scale=1.0 / D)
    k.op("dve", lambda en: en.reciprocal(out=rstd[:, :], in_=sq[:, 1, :]), [sq], [rstd])
    for blk in range(nb):
        stg = st[blk % 2]
        for tl in range(3):
            i = blk * 3 + tl
            xt, hb = xts[i % 2], hnb[i % 2]
            k.dma("sp", xt[:, :], h1[i * 128:(i + 1) * 128, :], [h1], [xt])
            stt(k, hb[:, :], xt[:, :], rstd[:, i:i + 1], wB[:, :], ALU.mult, ALU.mult, [xt, rstd, wB], [hb])
            for cc in range(8):
                tr(k, TRb[:, (cc % 4) * 128:(cc % 4 + 1) * 128], hb[:, cc * 128:(cc + 1) * 128], c["Ib"](),
                   [hb, cb], [TR])
                if cc % 4 == 3:
                    g4 = cc // 4
                    cp(k, "act", stg[:, g4 * 4:(g4 + 1) * 4, tl * 128:(tl + 1) * 128], v3(TRb[:, 0:512]),
                       [TR], [stg])
        if out_cb is not None:
            out_cb(blk, stg)
        else:
            k.dma("sp", dst[blk], stg[:, :, :], [stg], [dst])


def phase_sb(k, c, P, hnTs, win, nwqk, o1T, nb, nheads, pfx, load_cb=None, out_cb=None):
    nt = nb * 3
    TT = nt * 128
    mmA, mmB, DD = P.full
    ZP = P.half[0].root
    TR = P.half[2]
    OO = P.half[4]
    TRb = TR[:, :].bitcast(BF16)
    cf, cb = c["f"], c["b"]
    w_sb = k.sb(pfx + "w", [128, NC_, 512], BF16)
    hbk = [k.sb(pfx + "hb%d" % i, [128, NC_, TB], BF16) for i in range(2)]
    qT = k.sb(pfx + "qT", [128, TT], BF16)
    kT = k.sb(pfx + "kT", [128, TT], BF16)
    vtok = k.sb(pfx + "vtok", [128, nt, 128], BF16)
    gs = k.sb(pfx + "gs", [128, nt, 128], BF16)
    oT = k.sb(pfx + "oT", [128, TT], BF16)
    nwt = k.sb(pfx + "nw", [128, 2], F32)
    sqb = k.sb(pfx + "sq", [128, TB], BF16)
    sb_ = k.sb(pfx + "s", [128, TB], F32)
    rn = k.sb(pfx + "rn", [128, TB], F32)
    ones512 = k.sb(pfx + "ones", [128, 512], F32)
    E1 = k.sb(pfx + "E1", [128, 512], F32)
    SP = k.sb(pfx + "SP", [128, 512], F32)
    PRE = k.sb(pfx + "PRE", [128, 512], F32)
    ARG = k.sb(pfx + "ARG", [128, 512], F32)
    Aw = k.sb(pfx + "Aw", [128, 512], BF16)
    AT = k.sb(pfx + "AT", [128, 512], BF16)
    car = k.sb(pfx + "car", [128, 4], F32)
    og = k.sb(pfx + "og", [128, 128], BF16)
    k.dma("sp", nwt[:, :], nwqk[:, :], [nwqk], [nwt])
    k.op("dve", lambda en: en.memset(ones512[:, :], 1.0), [], [ones512])
    bi = 0
    for j in range(nheads):
        for q4 in range(4):
            k.dma("pool", w_sb[:, q4 * 8:(q4 + 1) * 8, :],
                  win[j, q4 * 1024:(q4 + 1) * 1024, :].rearrange("(c p) n -> p c n", p=128), [win], [w_sb])
        for tb in range(nb):
            hb = hbk[bi % 2]
            bi += 1
            if load_cb is not None:
                load_cb(tb, hb)
            else:
                for g in range(4):
                    k.dma("sp", hb[:, g * 8:(g + 1) * 8, :], hnTs[g][tb], [hnTs[g]], [hb])
            for fc in range(2):
                ps = mmA if fc == 0 else mmB
                for cc in range(NC_):
                    mm(k, ps[:, 0:TB], w_sb[:, cc, fc * 128:(fc + 1) * 128], hb[:, cc, :], cc == 0, cc == NC_ - 1,
                       [w_sb, hb], [ps])
                act(k, sqb[:, :], ps[:, 0:TB], AF.Square, [ps], [sqb])
                mm(k, DD[:, 0:TB], c["onesb"](), sqb[:, :], True, True, [cb, sqb], [DD])
                act(k, sb_[:, :], DD[:, 0:TB], AF.Sqrt, [DD], [sb_], bias=EPS, scale=1.0 / 128)
                k.op("dve", lambda en: en.reciprocal(out=rn[:, :], in_=sb_[:, :]), [sb_], [rn])
                dstT = qT if fc == 0 else kT
                stt(k, rn[:, :], rn[:, :], nwt[:, fc:fc + 1], ps[:, 0:TB], ALU.mult, ALU.mult, [rn, nwt, ps], [rn])
                if fc == 0:
                    ts(k, "dve", dstT[:, tb * TB:(tb + 1) * TB], rn[:, :], 128.0 ** -0.5, ALU.mult, [rn], [dstT])
                else:
                    cp(k, "dve", dstT[:, tb * TB:(tb + 1) * TB], rn[:, :], [rn], [dstT])
            for tl in range(3):
                i = 3 * tb + tl
                ps = mmA if tl % 2 == 0 else mmB
                for cc in range(NC_):
                    mm(k, ps[:, 0:256], hb[:, cc, tl * 128:(tl + 1) * 128], w_sb[:, cc, 256:512], cc == 0,
                       cc == NC_ - 1, [w_sb, hb], [ps])
                cp(k, "dve", vtok[:, i, :], ps[:, 0:128], [ps], [vtok])
                act(k, gs[:, i, :], ps[:, 128:256], AF.Silu, [ps], [gs])
        for tq in range(nt):
            k.op("dve", lambda en: en.memset(car[:, 0:1], 0.0), [], [car])
            blocks = []
            t1 = tq + 1
            while t1 > 0:
                t0 = max(0, t1 - 4)
                blocks.append((t0, t1))
                t1 = t0
            nmm = tq + 1
            mi = 0
            for (t0, t1) in blocks:
                n = t1 - t0
                W = n * 128
                diag = (t1 - 1 == tq)
                mm(k, ZP[:, 0:W], qT[:, tq * 128:(tq + 1) * 128], kT[:, t0 * 128:t1 * 128], True, True,
                   [qT, kT], [ZP])
                act(k, E1[:, 0:W], ZP[:, 0:W], AF.Exp, [ZP], [E1])
                act(k, SP[:, 0:W], E1[:, 0:W], AF.Ln, [E1], [SP], bias=1.0)
                if diag:
                    tt(k, "pool", SP[:, W - 128:W], SP[:, W - 128:W], c["Ls"](), ALU.mult, [SP, cf], [SP])
                k.op("dve", lambda en: en.tensor_tensor_scan(out=PRE[:, 0:W], data0=ones512[:, 0:W],
                                                             data1=SP[:, 0:W], initial=0.0, op0=ALU.mult,
                                                             op1=ALU.add), [ones512, SP], [PRE])
                tt(k, "dve", ARG[:, 0:W], ZP[:, 0:W], SP[:, 0:W], ALU.subtract, [ZP, SP], [ARG])
                tt(k, "pool", ARG[:, 0:W], ARG[:, 0:W], PRE[:, 0:W], ALU.add, [ARG, PRE], [ARG])
                tt(k, "dve", car[:, 1:2], car[:, 0:1], PRE[:, W - 1:W], ALU.add, [car, PRE], [car])
                ts(k, "dve", car[:, 2:3], car[:, 1:2], -1.0, ALU.mult, [car], [car])
                act(k, Aw[:, 0:W], ARG[:, 0:W], AF.Exp, [ARG, car], [Aw], bias=car[:, 2:3])
                cp(k, "dve", car[:, 0:1], car[:, 1:2], [car], [car])
                if diag:
                    tt(k, "pool", Aw[:, W - 128:W], Aw[:, W - 128:W], c["b"][:, 384:512], ALU.mult, [Aw, cb], [Aw])
                for s_ in range(n):
                    tr(k, TRb[:, s_ * 128:(s_ + 1) * 128], Aw[:, s_ * 128:(s_ + 1) * 128], c["Ib"](), [Aw, cb], [TR])
                cp(k, "act", AT[:, 0:W], TRb[:, 0:W], [TR], [AT])
                for s_ in range(n):
                    mm(k, OO[:, 0:128], AT[:, s_ * 128:(s_ + 1) * 128], vtok[:, t0 + s_, :], mi == 0, mi == nmm - 1,
                       [AT, vtok], [OO])
                    mi += 1
            tt(k, "dve", og[:, :], OO[:, 0:128], gs[:, tq, :], ALU.mult, [OO, gs], [og])
            tr(k, TRb[:, 0:128], og[:, :], c["Ib"](), [og, cb], [TR])
            cp(k, "act", oT[:, tq * 128:(tq + 1) * 128], TRb[:, 0:128], [TR], [oT])
        if out_cb is not None:
            out_cb(j, oT)
        else:
            k.dma("sp", o1T[:, :, j, :].rearrange("t p x -> p t x"), v3(oT[:, :], TB), [oT], [o1T])


def _new(with_consts=True):
    nc = bass.Bass("TRN2", target_bir_lowering=False)
    k = K(nc)
    cd = k.dram("consts", [128, 640], F32, kind="ExternalInput") if with_consts else None
    return nc, k, cd


def build_l1(nb=NB, nh=8):
    nc, k, cd = _new()
    h0 = k.dram("h0", [nb * TB, D], F32, kind="ExternalInput")
    nw = k.dram("nw", [128, D], F32, kind="ExternalInput")
    win = k.dram("win", [nh, D, DNW], F32, kind="ExternalInput")
    cw = k.dram("cw", [128, nh, 4, 4], F32, kind="ExternalInput")
    hp = k.dram("hp", [128, nh, 4], F32, kind="ExternalInput")
    onw = k.dram("onw", [128, 256], F32, kind="ExternalInput")
    hnT = k.dram("hnT", [nb, 128, NC_, TB], BF16, kind="Internal")
    o0T = k.dram("o0T", [nb, 128, nh * 2, TB], BF16, kind="ExternalOutput")
    k.phase_begin()
    c = load_consts(k, cd, "a")
    phase_norm_T(k, c, h0, nw, hnT, nb, "n")
    k.phase_end()
    k.phase_begin()
    c = load_consts(k, cd, "b")
    def dn_out(j, tb, stg):
        k.dma("sp", o0T[tb, :, 2 * j:2 * j + 2, :], stg[:, :, :], [stg], [o0T])
    phase_dn2(k, c, hnT, win, cw, hp, onw, nb, nh, "d", dn_out)
    k.phase_end()
    k.finish()
    return nc


def build_outproj(kc_each, final, nb=NB):
    nc, k, cd = _new(False)
    xTs = [k.dram("xT%d" % g, [nb, 128, kc_each, TB], BF16, kind="ExternalInput") for g in range(4)]
    wout = k.dram("wout", [4 * kc_each * 128, 1024], F32, kind="ExternalInput")
    hres = k.dram("hres", [nb * TB, 1024], F32, kind="ExternalInput")
    P = PsumSet(k, "p")
    if not final:
        out = k.dram("h1", [nb * TB, 1024], F32, kind="ExternalOutput")
        ssq = k.dram("ssq", [128, nb * 3], F32, kind="ExternalOutput")
        phase_outproj(k, P, xTs, [kc_each] * 4, wout, hres, out, nb, "o", ssq_out=ssq)
    else:
        out = k.dram("out", [SEQ, 1024], F32, kind="ExternalOutput")

        def rows(i):
            lo = max(i * 128, NMETA)
            hi = min((i + 1) * 128, NMETA + SEQ)
            if hi <= lo:
                return []
            return [(slice(lo - NMETA, hi - NMETA), slice(lo - i * 128, hi - i * 128))]
        phase_outproj(k, P, xTs, [kc_each] * 4, wout, hres, out, nb, "o", out_rows=rows)
    k.finish()
    return nc


def build_l3(nb=NB):
    nc, k, cd = _new()
    ssq_all = k.dram("ssq_all", [4, 128, nb * 3], F32, kind="ExternalInput")
    h1 = k.dram("h1", [nb * TB, 1024], F32, kind="ExternalInput")
    nw = k.dram("nw", [128, 1024], F32, kind="ExternalInput")
    dst = k.dram("hn1T", [nb, 128, 8, TB], BF16, kind="ExternalOutput")
    c = load_consts(k, cd)
    P = PsumSet(k, "p")
    phase_norm2(k, c, P, ssq_all, h1, nw, dst, nb, "m")
    k.finish()
    return nc


def build_l4(nb=NB, nh=8):
    nc, k, cd = _new()
    hnTs = [k.dram("hnT%d" % g, [nb, 128, 8, TB], BF16, kind="ExternalInput") for g in range(4)]
    win = k.dram("win", [nh, D, 512], F32, kind="ExternalInput")
    nwqk = k.dram("nwqk", [128, 2], F32, kind="ExternalInput")
    o1T = k.dram("o1T", [nb, 128, nh, TB], BF16, kind="ExternalOutput")
    c = load_consts(k, cd)
    def load_h(tb, hb):
        for g in range(4):
            k.dma("sp", hb[:, g * 8:(g + 1) * 8, :], hnTs[g][tb], [hnTs[g]], [hb])

    def sb_out(j, oT):
        k.dma("sp", o1T[:, :, j, :].rearrange("t p x -> p t x"), v3(oT[:, :], TB), [oT], [o1T])
    phase_sb2(k, c, win, nwqk, nb, nh, "s", load_h, sb_out)
    k.finish()
    return nc


def run_rr(gens, width, bg=None):
    pending = list(gens)
    active = []
    while pending or active:
        while pending and len(active) < width:
            active.append(pending.pop(0)())
        for g in list(active):
            try:
                next(g)
            except StopIteration:
                active.remove(g)
        if bg is not None:
            try:
                next(bg)
            except StopIteration:
                bg = None
    if bg is not None:
        for _ in bg:
            pass


def phase_sb2(k, c, win, nwqk, nb, nheads, pfx, load_cb, out_cb, NS=3):
    nt = nb * 3
    TT = nt * 128
    banks = [k.ps(pfx + "bk%d" % i, [128, 512], F32) for i in range(8)]
    mmA, mmB = banks[0:2]
    cf, cb = c["f"], c["b"]
    hnTs = None
    w_sb = k.sb(pfx + "w", [128, NC_, 512], BF16)
    hbk = [k.sb(pfx + "hb%d" % i, [128, NC_, TB], BF16) for i in range(2)]
    arrs = [(k.sb(pfx + "qT%d" % i, [128, TT], BF16), k.sb(pfx + "kT%d" % i, [128, TT], BF16),
             k.sb(pfx + "vtok%d" % i, [128, nt, 128], BF16), k.sb(pfx + "gs%d" % i, [128, nt, 128], BF16))
            for i in range(2)]
    oT = k.sb(pfx + "oT", [128, TT], BF16)
    nwt = k.sb(pfx + "nw", [128, 2], F32)
    sqb = k.sb(pfx + "sq", [128, TB], BF16)
    sb_ = k.sb(pfx + "s", [128, TB], F32)
    rn = k.sb(pfx + "rn", [128, TB], F32)
    ones512 = k.sb(pfx + "ones", [128, 512], F32)

    class Set:
        pass
    sets = []
    for i in range(NS):
        S_ = Set()
        S_.ZP, S_.TA = banks[2 + 2 * i], banks[3 + 2 * i]
        S_.E1 = k.sb(pfx + "E1_%d" % i, [128, 512], F32)
        S_.SP = k.sb(pfx + "SP_%d" % i, [128, 512], F32)
        S_.PRE = k.sb(pfx + "PRE_%d" % i, [128, 512], F32)
        S_.Aw = k.sb(pfx + "Aw_%d" % i, [128, 512], BF16)
        S_.AT = k.sb(pfx + "AT_%d" % i, [128, 512], BF16)
        S_.car = k.sb(pfx + "car_%d" % i, [128, 4], F32)
        S_.Oa = k.sb(pfx + "Oa_%d" % i, [128, 128], F32)
        S_.og = k.sb(pfx + "og_%d" % i, [128, 128], BF16)
        sets.append(S_)
    def load_w(jj):
        for q4 in range(4):
            k.dma("pool", w_sb[:, q4 * 8:(q4 + 1) * 8, :],
                  win[jj, q4 * 1024:(q4 + 1) * 1024, :].rearrange("(c p) n -> p c n", p=128), [win], [w_sb])

    bi = [0]

    def inproj(j):
        qT, kT, vtok, gs = arrs[j % 2]
        for tb in range(nb):
            hb = hbk[bi[0] % 2]
            bi[0] += 1
            load_cb(tb, hb)
            for fc in range(2):
                ps, DD = (mmA, mmB) if fc == 0 else (mmB, mmA)
                for cc in range(NC_):
                    mm(k, ps[:, 0:TB], w_sb[:, cc, fc * 128:(fc + 1) * 128], hb[:, cc, :], cc == 0, cc == NC_ - 1,
                       [w_sb, hb], [ps])
                yield
                act(k, sqb[:, :], ps[:, 0:TB], AF.Square, [ps], [sqb])
                yield
                mm(k, DD[:, 0:TB], c["onesb"](), sqb[:, :], True, True, [cb, sqb], [DD])
                yield
                act(k, sb_[:, :], DD[:, 0:TB], AF.Sqrt, [DD], [sb_], bias=EPS, scale=1.0 / 128)
                yield
                k.op("dve", lambda en: en.reciprocal(out=rn[:, :], in_=sb_[:, :]), [sb_], [rn])
                dstT = qT if fc == 0 else kT
                stt(k, rn[:, :], rn[:, :], nwt[:, fc:fc + 1], ps[:, 0:TB], ALU.mult, ALU.mult, [rn, nwt, ps], [rn])
                if fc == 0:
                    ts(k, "dve", dstT[:, tb * TB:(tb + 1) * TB], rn[:, :], 128.0 ** -0.5, ALU.mult, [rn], [dstT])
                else:
                    cp(k, "dve", dstT[:, tb * TB:(tb + 1) * TB], rn[:, :], [rn], [dstT])
                yield
            for tl in range(3):
                i = 3 * tb + tl
                ps = mmA if tl % 2 == 0 else mmB
                for cc in range(NC_):
                    mm(k, ps[:, 0:256], hb[:, cc, tl * 128:(tl + 1) * 128], w_sb[:, cc, 256:512], cc == 0,
                       cc == NC_ - 1, [w_sb, hb], [ps])
                yield
                cp(k, "dve", vtok[:, i, :], ps[:, 0:128], [ps], [vtok])
                act(k, gs[:, i, :], ps[:, 128:256], AF.Silu, [ps], [gs])
                yield
        if j + 1 < nheads:
            load_w(j + 1)

    k.dma("sp", nwt[:, :], nwqk[:, :], [nwqk], [nwt])
    k.op("dve", lambda en: en.memset(ones512[:, :], 1.0), [], [ones512])
    load_w(0)
    for _ in inproj(0):
        pass
    for j in range(nheads):
        qT, kT, vtok, gs = arrs[j % 2]
        def chain(tq, j=j):
            st_ = sets[tq % NS]
            ZP, TA = st_.ZP, st_.TA
            TRb = TA[:, 0:256].bitcast(BF16)
            AV = TA[:, 256:384]
            E1, SP, PRE, ARG, Aw, AT, car, Oa = st_.E1, st_.SP, st_.PRE, st_.E1, st_.Aw, st_.AT, st_.car, st_.Oa
            k.op("dve", lambda en: en.memset(car[:, 0:1], 0.0), [], [car])
            t1 = tq + 1
            first = True
            while t1 > 0:
                t0 = max(0, t1 - 4)
                n = t1 - t0
                W = n * 128
                diag = (t1 - 1 == tq)
                mm(k, ZP[:, 0:W], qT[:, tq * 128:(tq + 1) * 128], kT[:, t0 * 128:t1 * 128], True, True,
                   [qT, kT], [ZP])
                yield
                act(k, E1[:, 0:W], ZP[:, 0:W], AF.Exp, [ZP], [E1])
                yield
                act(k, SP[:, 0:W], E1[:, 0:W], AF.Ln, [E1], [SP], bias=1.0)
                yield
                if diag:
                    tt(k, "pool", SP[:, W - 128:W], SP[:, W - 128:W], c["Ls"](), ALU.mult, [SP, cf], [SP])
                    yield
                k.op("dve", lambda en: en.tensor_tensor_scan(out=PRE[:, 0:W], data0=ones512[:, 0:W],
                                                             data1=SP[:, 0:W], initial=0.0, op0=ALU.mult,
                                                             op1=ALU.add), [ones512, SP], [PRE])
                tt(k, "dve", ARG[:, 0:W], ZP[:, 0:W], SP[:, 0:W], ALU.subtract, [ZP, SP], [ARG])
                yield
                tt(k, "pool", ARG[:, 0:W], ARG[:, 0:W], PRE[:, 0:W], ALU.add, [ARG, PRE], [ARG])
                tt(k, "dve", car[:, 1:2], car[:, 0:1], PRE[:, W - 1:W], ALU.add, [car, PRE], [car])
                ts(k, "dve", car[:, 2:3], car[:, 1:2], -1.0, ALU.mult, [car], [car])
                yield
                act(k, Aw[:, 0:W], ARG[:, 0:W], AF.Exp, [ARG, car], [Aw], bias=car[:, 2:3])
                yield
                cp(k, "dve", car[:, 0:1], car[:, 1:2], [car], [car])
                if diag:
                    tt(k, "pool", Aw[:, W - 128:W], Aw[:, W - 128:W], c["b"][:, 384:512], ALU.mult, [Aw, cb], [Aw])
                    yield
                for s_ in range(n):
                    tr(k, TRb[:, s_ * 128:(s_ + 1) * 128], Aw[:, s_ * 128:(s_ + 1) * 128], c["Ib"](), [Aw, cb], [TA])
                yield
                cp(k, "dve", AT[:, 0:W], TRb[:, 0:W], [TA], [AT])
                yield
                for s_ in range(n):
                    mm(k, AV, AT[:, s_ * 128:(s_ + 1) * 128], vtok[:, t0 + s_, :], s_ == 0, s_ == n - 1,
                       [AT, vtok], [TA])
                yield
                if first:
                    cp(k, "dve", Oa[:, :], AV, [TA], [Oa])
                else:
                    tt(k, "dve", Oa[:, :], Oa[:, :], AV, ALU.add, [Oa, TA], [Oa])
                first = False
                t1 = t0
                yield
            tt(k, "pool", st_.og[:, :], Oa[:, :], gs[:, tq, :], ALU.mult, [Oa, gs], [st_.og])
            yield
            tr(k, TRb[:, 0:128], st_.og[:, :], c["Ib"](), [st_.og, cb], [TA])
            yield
            cp(k, "act", oT[:, tq * 128:(tq + 1) * 128], TRb[:, 0:128], [TA], [oT])
            yield

        bg = inproj(j + 1) if j + 1 < nheads else None
        run_rr([(lambda tq=tq: chain(tq)) for tq in range(nt if not DBG_SKIP else 0)], NS, bg)
        out_cb(j, oT)


PCS1 = [(0, 4), (4, 8), (8, 11)]
PCS3 = [(0, 6), (6, 11)]


def _piece_of(pcs, tb):
    for i, (a, b) in enumerate(pcs):
        if a <= tb < b:
            return i, tb - a
    raise ValueError(tb)


def build_fused(nb=NB, nh=8):
    nc, k, cd = _new()
    nt = nb * 3
    h0 = k.dram("h0", [nb * TB, D], F32, kind="ExternalInput")
    h0c = k.dram("h0c", [nb * TB, 1024], F32, kind="ExternalInput")
    nw0 = k.dram("nw0", [128, D], F32, kind="ExternalInput")
    win0 = k.dram("win0", [nh, D, DNW], F32, kind="ExternalInput")
    cw = k.dram("cw", [128, nh, 4, 4], F32, kind="ExternalInput")
    hp = k.dram("hp", [128, nh, 4], F32, kind="ExternalInput")
    onw = k.dram("onw", [128, 256], F32, kind="ExternalInput")
    wout0 = k.dram("wout0", [nh * 4 * 256, 1024], F32, kind="ExternalInput")
    nw1 = k.dram("nw1", [128, 1024], F32, kind="ExternalInput")
    win1 = k.dram("win1", [nh, D, 512], F32, kind="ExternalInput")
    nwqk = k.dram("nwqk", [128, 2], F32, kind="ExternalInput")
    wout1 = k.dram("wout1", [nh * 4 * 128, 1024], F32, kind="ExternalInput")
    out = k.dram("out", [SEQ, 1024], F32, kind="ExternalOutput")
    hnT = k.dram("hnT", [nb, 128, NC_, TB], BF16)
    h1 = k.dram("h1", [nb * TB, 1024], F32)
    pcs1 = [(a, min(b, nb)) for (a, b) in PCS1 if a < nb]
    pcs3 = [(a, min(b, nb)) for (a, b) in PCS3 if a < nb]
    A1, G1, A2, G2, A3, G3 = [Buf(k, None, n) for n in ("A1", "G1", "A2", "G2", "A3", "G3")]
    a1 = [[nc.dram_tensor("a1_%d_%d" % (j, i), [(b - a) * 128, 768], BF16, kind="Internal").ap()
           for i, (a, b) in enumerate(pcs1)] for j in range(nh)]
    g1 = [[nc.dram_tensor("g1_%d_%d" % (j, i), [4 * (b - a) * 128, 768], BF16, kind="Internal").ap()
           for i, (a, b) in enumerate(pcs1)] for j in range(nh)]
    a2 = [nc.dram_tensor("a2_%d" % t, [128, 8 * TB], BF16, kind="Internal").ap() for t in range(nb)]
    g2 = [nc.dram_tensor("g2_%d" % t, [4 * 128, 8 * TB], BF16, kind="Internal").ap() for t in range(nb)]
    a3 = [[nc.dram_tensor("a3_%d_%d" % (j, i), [(b - a) * 128, TB], BF16, kind="Internal").ap()
           for i, (a, b) in enumerate(pcs3)] for j in range(nh)]
    g3 = [[nc.dram_tensor("g3_%d_%d" % (j, i), [4 * (b - a) * 128, TB], BF16, kind="Internal").ap()
           for i, (a, b) in enumerate(pcs3)] for j in range(nh)]
    ssq_in = k.dram("ssq_in", [128, nt], F32)
    ssq_all = k.dram("ssq_all", [4 * 128, nt], F32)

    wb0 = k.dram("wb0", [nh * 4 * 256, 1024], BF16)
    wb1 = k.dram("wb1", [nh * 4 * 128, 1024], BF16)
    for (src, dst) in ((wout0, wb0), (wout1, wb1)):
        nrow = src.t.shape[0]
        for r0 in range(0, nrow, 1024):
            r1 = min(nrow, r0 + 1024)
            k.dma("pool", dst[r0:r1, :], src[r0:r1, :], [src], [dst])

    k.phase_begin()
    c = load_consts(k, cd, "a")
    phase_norm_T(k, c, h0, nw0, hnT, nb, "n")
    k.phase_end()

    def dn_out(j, tb, stg):
        i, tl = _piece_of(pcs1, tb)
        dst = a1[j][i].rearrange("(t p) (h x) -> p t h x", p=128, h=2)
        k.dma("sp", dst[:, tl, :, :], stg[:, :, :], [stg], [A1])
        if tb == pcs1[i][1] - 1:
            k.cc(a1[j][i][:, :], g1[j][i][:, :], [A1], [G1], G1)

    k.phase_begin()
    c = load_consts(k, cd, "b")
    phase_dn2(k, c, hnT, win0, cw, hp, onw, nb, nh, "d", dn_out)
    k.phase_end()

    def load_x2(tb, xbb):
        i, tl = _piece_of(pcs1, tb)
        dstv = xbb[:, :, :].rearrange("p (r j h) x -> p r j h x", r=4, h=2)
        for j in range(nh):
            src = g1[j][i].rearrange("(r t p) (h x) -> p r t h x", r=4, p=128, h=2)
            k.dma("sp", dstv[:, :, j, :, :], src[:, :, tl, :, :], [G1], [xbb])

    k.phase_begin()
    P = PsumSet(k, "p2")
    phase_outproj(k, P, None, [nh * 2] * 4, wb0, h0c, h1, nb, "o", ssq_out=ssq_in, load_cb=load_x2, wq="sp")
    k.cc(ssq_in[:, :], ssq_all[:, :], [ssq_in], [ssq_all], ssq_all)
    k.phase_end()

    G2p = [Buf(k, None, "G2p%d" % t) for t in range(nb)]

    def n2_out(blk, stg):
        k.dma("sp", a2[blk].rearrange("p (c x) -> p c x", c=8), stg[:, :, :], [stg], [A2])
        k.cc(a2[blk][:, :], g2[blk][:, :], [A2], [G2p[blk]], G2)

    k.phase_begin()
    c = load_consts(k, cd, "c")
    P = PsumSet(k, "p3")
    phase_norm2(k, c, P, ssq_all, h1, nw1, None, nb, "m",
                ssq_view=ssq_all[:, :].rearrange("(g p) n -> p g n", p=128), out_cb=n2_out)
    k.phase_end()

    def load_h3(tb, hb):
        k.dma("sp", hb[:, :, :].rearrange("p (r c) x -> p r c x", r=4),
              g2[tb].rearrange("(r p) (c x) -> p r c x", p=128, c=8), [G2p[tb]], [hb])

    def sb_out(j, oT):
        for i, (a, b) in enumerate(pcs3):
            k.dma("sp", a3[j][i].rearrange("(t p) x -> p t x", p=128), v3(oT[:, a * TB:b * TB], TB), [oT], [A3])
        for i in range(len(pcs3)):
            k.cc(a3[j][i][:, :], g3[j][i][:, :], [A3], [G3], G3)

    k.phase_begin()
    c = load_consts(k, cd, "d")
    phase_sb2(k, c, win1, nwqk, nb, nh, "s", load_h3, sb_out)
    k.phase_end()

    def load_x4(tb, xbb):
        i, tl = _piece_of(pcs3, tb)
        dstv = xbb[:, :, :].rearrange("p (r j) x -> p r j x", r=4)
        for j in range(nh):
            src = g3[j][i].rearrange("(r t p) x -> p r t x", r=4, p=128)
            k.dma("sp", dstv[:, :, j, :], src[:, :, tl, :], [G3], [xbb])

    def rows(i):
        lo = max(i * 128, NMETA)
        hi = min((i + 1) * 128, NMETA + SEQ)
        if hi <= lo:
            return []
        return [(slice(lo - NMETA, hi - NMETA), slice(lo - i * 128, hi - i * 128))]

    k.phase_begin()
    P = PsumSet(k, "p5")
    phase_outproj(k, P, None, [nh] * 4, wb1, h1, out, nb, "q", out_rows=rows if nb == NB else None,
                  load_cb=load_x4, wq="sp")
    k.phase_end()
    k.finish()
    return nc


_DBG = None
FUSED = True


def _run(nc, in_maps):
    res = run_bass_kernel_spmd(nc, in_maps, core_ids=list(range(8)))
    return res.results


def _rep(v, n=128):
    return np.ascontiguousarray(np.tile(np.asarray(v, np.float32)[None], (n, 1)))


def kernel(x, meta_tokens, dn_norm_w, dn_w_in, dn_conv_w, dn_a_log, dn_dt_bias, dn_out_norm_w, dn_w_out,
           sb_norm_w, sb_w_in, sb_q_norm_w, sb_k_norm_w, sb_w_out):
    f32 = np.float32
    x = np.asarray(x, f32)
    cst = consts_np()
    h0p = []
    for b in range(2):
        h = np.zeros((T, D), f32)
        h[:NMETA] = np.asarray(meta_tokens, f32)
        h[NMETA:NMETA + SEQ] = x[b]
        h0p.append(h)
    w_in = np.asarray(dn_w_in, f32)[0]
    convw = np.asarray(dn_conv_w, f32)[0]
    a_log = np.asarray(dn_a_log, f32)[0]
    dtb = np.asarray(dn_dt_bias, f32)[0]
    onw = np.asarray(dn_out_norm_w, f32)[0]
    KQ, KV = 4096, 8192
    dn_win, dn_cw, dn_hp = [], [], []
    for g in range(4):
        ws, cws, hps = [], [], []
        for j in range(8):
            J = 8 * g + j
            cols = np.concatenate([np.arange(J * 128, (J + 1) * 128), KQ + np.arange(J * 128, (J + 1) * 128),
                                   2 * KQ + np.arange(2 * J * 128, (2 * J + 2) * 128),
                                   2 * KQ + KV + np.arange(2 * J * 128, (2 * J + 2) * 128),
                                   2 * KQ + 2 * KV + np.arange(2 * J, 2 * J + 2),
                                   2 * KQ + 2 * KV + 64 + np.arange(2 * J, 2 * J + 2)])
            ws.append(w_in[:, cols])
            cws.append(convw[:, cols[:512]].reshape(4, 4, 128).transpose(2, 1, 0))
            hps.append(np.concatenate([dtb[2 * J:2 * J + 2], a_log[2 * J:2 * J + 2]]))
        dn_win.append(np.ascontiguousarray(np.stack(ws)))
        dn_cw.append(np.ascontiguousarray(np.stack(cws, axis=1)))
        dn_hp.append(np.ascontiguousarray(np.tile(np.stack(hps)[None], (128, 1, 1))))
    dn_nw_r = _rep(np.asarray(dn_norm_w, f32)[0])
    onw_r = _rep(np.concatenate([onw, onw]))
    w_out0 = np.asarray(dn_w_out, f32)[0]
    sbw = np.asarray(sb_w_in, f32)[0]
    sb_win = []
    for g in range(4):
        ws = []
        for j in range(8):
            H = 8 * g + j
            cols = np.concatenate([q * 4096 + np.arange(H * 128, (H + 1) * 128) for q in range(4)])
            ws.append(sbw[:, cols])
        sb_win.append(np.ascontiguousarray(np.stack(ws)))
    sb_nw = np.asarray(sb_norm_w, f32)[0]
    nwqk = np.ascontiguousarray(np.stack([np.asarray(sb_q_norm_w, f32)[0], np.asarray(sb_k_norm_w, f32)[0]], 1))
    w_out1 = np.asarray(sb_w_out, f32)[0]

    cores = [(c // 4, c % 4) for c in range(8)]
    if FUSED:
        rf = _run(build_fused(), [{
            "consts": cst, "h0": h0p[b], "h0c": np.ascontiguousarray(h0p[b][:, g * 1024:(g + 1) * 1024]),
            "nw0": dn_nw_r, "win0": dn_win[g], "cw": dn_cw[g], "hp": dn_hp[g], "onw": onw_r,
            "wout0": np.ascontiguousarray(w_out0[:, g * 1024:(g + 1) * 1024]),
            "nw1": _rep(sb_nw[g * 1024:(g + 1) * 1024]), "win1": sb_win[g], "nwqk": nwqk,
            "wout1": np.ascontiguousarray(w_out1[:, g * 1024:(g + 1) * 1024])} for (b, g) in cores])
        out = np.empty((2, SEQ, D), f32)
        for ci, (b, g) in enumerate(cores):
            out[b][:, g * 1024:(g + 1) * 1024] = np.asarray(rf[ci]["out"])
        return out
    r1 = _run(build_l1(), [{"consts": cst, "h0": h0p[b], "nw": dn_nw_r, "win": dn_win[g], "cw": dn_cw[g],
                            "hp": dn_hp[g], "onw": onw_r} for (b, g) in cores])
    o0T = [np.asarray(r["o0T"]) for r in r1]
    r2 = _run(build_outproj(16, False),
              [dict({"wout": np.ascontiguousarray(w_out0[:, g * 1024:(g + 1) * 1024]),
                     "hres": np.ascontiguousarray(h0p[b][:, g * 1024:(g + 1) * 1024])},
                    **{"xT%d" % gg: o0T[4 * b + gg] for gg in range(4)}) for (b, g) in cores])
    h1 = [np.asarray(r["h1"]) for r in r2]
    ssq = [np.asarray(r["ssq"]) for r in r2]
    r3 = _run(build_l3(), [{"consts": cst, "ssq_all": np.ascontiguousarray(np.stack(ssq[4 * b:4 * b + 4])),
                            "h1": h1[4 * b + g], "nw": _rep(sb_nw[g * 1024:(g + 1) * 1024])} for (b, g) in cores])
    hn1T = [np.asarray(r["hn1T"]) for r in r3]
    r4 = _run(build_l4(), [dict({"consts": cst, "win": sb_win[g], "nwqk": nwqk},
                                **{"hnT%d" % gg: hn1T[4 * b + gg] for gg in range(4)}) for (b, g) in cores])
    o1T = [np.asarray(r["o1T"]) for r in r4]
    r5 = _run(build_outproj(8, True),
              [dict({"wout": np.ascontiguousarray(w_out1[:, g * 1024:(g + 1) * 1024]),
                     "hres": h1[4 * b + g]},
                    **{"xT%d" % gg: o1T[4 * b + gg] for gg in range(4)}) for (b, g) in cores])
    if _DBG is not None:
        _DBG.update(o0T=o0T, h1=h1, ssq=ssq, hn1T=hn1T, o1T=o1T)
    out = np.empty((2, SEQ, D), f32)
    for ci, (b, g) in enumerate(cores):
        out[b][:, g * 1024:(g + 1) * 1024] = np.asarray(r5[ci]["out"])
    return out
```

```python
import contextlib
import numpy as np
import ml_dtypes
import concourse.bass as bass
import concourse.mybir as mybir
from concourse.bass_utils import run_bass_kernel_spmd

F32 = mybir.dt.float32
BF16 = mybir.dt.bfloat16
AF = mybir.ActivationFunctionType
ALU = mybir.AluOpType

D = 4096
NC_ = 32
SEQ = 4096
NMETA = 16
T = 4224
NT = 33
TB = 384
NB = 11
EPS = 1e-6
RGROUPS = [[0, 1, 2, 3], [4, 5, 6, 7]]
DNW = 772


class Tok:
    __slots__ = ("sem", "val")

    def __init__(self, sem, val):
        self.sem = sem
        self.val = val


class Buf:
    def __init__(self, k, t, name):
        self.k = k
        self.t = t
        self.name = name
        self.w = None
        self.r = {}
        self.dsem = None
        self.dcnt = 0
        self.root = self
        self.excl = False

    def sub(self, ap, name):
        b = Buf(self.k, ap, name)
        b.root = self.root
        return b

    def __getitem__(self, idx):
        return self.t[idx]


class K:
    def __init__(self, nc):
        self.nc = nc
        self.es = contextlib.ExitStack()
        self.eng = {"pe": nc.tensor, "act": nc.scalar, "dve": nc.vector,
                    "pool": nc.gpsimd, "sp": nc.sync}
        self.esem = {}
        self.ecnt = {}
        self.seen = {}
        for e in self.eng:
            self.esem[e] = self.es.enter_context(nc.semaphore("es_" + e))
            self.ecnt[e] = 0
            self.seen[e] = {}
        self.dbufs = []
        self.n = 0
        self.pst = None

    def phase_begin(self):
        self.pst = contextlib.ExitStack()

    def phase_end(self):
        self.barrier()
        self.pst.close()
        self.pst = None

    def sb(self, name, shape, dt):
        st = self.pst if self.pst is not None else self.es
        t = st.enter_context(self.nc.sbuf_tensor(name, list(shape), dt))
        return Buf(self, t, name)

    def ps(self, name, shape, dt):
        st = self.pst if self.pst is not None else self.es
        t = st.enter_context(self.nc.psum_tensor(name, list(shape), dt))
        b = Buf(self, t, name)
        b.excl = True
        return b

    def dram(self, name, shape, dt, kind="Internal"):
        t = self.nc.dram_tensor(name, list(shape), dt, kind=kind)
        return Buf(self, t.ap(), name)

    def _wait(self, e, tok):
        if tok is None:
            return
        if e == "pe" and tok.sem is self.esem["pe"]:
            return
        key = id(tok.sem)
        if self.seen[e].get(key, 0) >= tok.val:
            return
        self.seen[e][key] = tok.val
        self.eng[e].wait_ge(tok.sem, tok.val)

    def _deps(self, e, reads, writes):
        for b in reads:
            self._wait(e, b.w)
        for b in writes:
            self._wait(e, b.w)
            for t in b.r.values():
                self._wait(e, t)

    def _mark(self, tok, reads, writes):
        for b in reads:
            b.r[id(tok.sem)] = tok
        for b in writes:
            b.w = tok
            b.r = {}

    def _norm(self, reads, writes):
        rs = [b.root for b in reads]
        ws = [b.root for b in writes]
        ws = ws + [b for b in rs if b.excl]
        rs = [b for b in rs if not b.excl]
        return rs, ws

    def op(self, e, fn, reads, writes):
        reads, writes = self._norm(reads, writes)
        self._deps(e, reads, writes)
        ins = fn(self.eng[e])
        self.ecnt[e] += 1
        ins.then_inc(self.esem[e], 1)
        self._mark(Tok(self.esem[e], self.ecnt[e]), reads, writes)
        self.n += 1

    def dma(self, q, out_ap, in_ap, reads, writes, owner=None):
        ow = (owner if owner is not None else writes[0]).root
        reads, writes = self._norm(reads, writes)
        if ow.dsem is None:
            ow.dsem = self.es.enter_context(self.nc.semaphore("ds_" + ow.name))
            self.dbufs.append(ow)
        self._deps(q, reads, writes)
        ow.dcnt += 16
        self.eng[q].dma_start(out=out_ap, in_=in_ap).then_inc(ow.dsem, 16)
        self._mark(Tok(ow.dsem, ow.dcnt), reads, writes)
        self.n += 1

    def cc(self, in_ap, out_ap, reads, writes, owner):
        ow = owner.root
        reads, writes = self._norm(reads, writes)
        if ow.dsem is None:
            ow.dsem = self.es.enter_context(self.nc.semaphore("cs_" + ow.name))
            self.dbufs.append(ow)
        for b in reads:
            self._wait("pool", b.w)
        for b in writes:
            for t in b.r.values():
                self._wait("pool", t)
        ow.dcnt += 1
        self.eng["pool"].collective_compute("AllGather", ALU.bypass, replica_groups=RGROUPS,
                                            ins=[in_ap], outs=[out_ap]).then_inc(ow.dsem)
        self._mark(Tok(ow.dsem, ow.dcnt), reads, writes)
        self.n += 1

    def barrier(self):
        for e in self.eng:
            for e2 in self.eng:
                if e2 != e and self.ecnt[e2] > 0:
                    self._wait(e, Tok(self.esem[e2], self.ecnt[e2]))
            for b in self.dbufs:
                if b.dcnt:
                    self._wait(e, Tok(b.dsem, b.dcnt))

    def finish(self):
        self.barrier()
        self.es.close()


def act(k, out, in_, func, reads, writes, bias=None, scale=None, accum=None, e="act"):
    kw = {}
    if bias is not None:
        kw["bias"] = bias
    if scale is not None:
        kw["scale"] = scale
    if accum is not None:
        kw["accum_out"] = accum
    k.op(e, lambda en: en.activation(out=out, in_=in_, func=func, **kw), reads, writes)


def mm(k, out, lhsT, rhs, start, stop, reads, writes):
    k.op("pe", lambda en: en.matmul(out, lhsT, rhs, start=start, stop=stop), reads, writes)


def tr(k, out, in_, ident, reads, writes):
    k.op("pe", lambda en: en.transpose(out, in_, ident), reads, writes)


def tt(k, e, out, in0, in1, op, reads, writes):
    k.op(e, lambda en: en.tensor_tensor(out=out, in0=in0, in1=in1, op=op), reads, writes)


def ts(k, e, out, in0, s1, op0, reads, writes, s2=None, op1=None):
    if op1 is None:
        k.op(e, lambda en: en.tensor_scalar(out=out, in0=in0, scalar1=s1, scalar2=None, op0=op0),
             reads, writes)
    else:
        k.op(e, lambda en: en.tensor_scalar(out=out, in0=in0, scalar1=s1, scalar2=s2, op0=op0,
                                            op1=op1), reads, writes)


def stt(k, out, in0, scalar, in1, op0, op1, reads, writes):
    k.op("dve", lambda en: en.scalar_tensor_tensor(out=out, in0=in0, scalar=scalar, in1=in1,
                                                   op0=op0, op1=op1), reads, writes)


def cp(k, e, out, in_, reads, writes):
    if e == "act":
        k.op("act", lambda en: en.copy(out=out, in_=in_), reads, writes)
    else:
        k.op(e, lambda en: en.tensor_copy(out=out, in_=in_), reads, writes)


def load_consts(k, cdram, pfx=""):
    c = {}
    cf = k.sb(pfx + "c_f32", [128, 5 * 128], F32)
    k.dma("sp", cf[:, :], cdram[:, :], [], [cf])
    c["f"] = cf
    cb = k.sb(pfx + "c_bf", [128, 5 * 128], BF16)
    cp(k, "dve", cb[:, :], cf[:, :], [cf], [cb])
    c["b"] = cb
    c["I"] = lambda t=cf: t[:, 0:128]
    c["I2"] = lambda t=cf: t[:, 0:256]
    c["U"] = lambda t=cf: t[:, 256:384]
    c["Ls"] = lambda t=cf: t[:, 384:512]
    c["ones"] = lambda t=cf: t[:, 512:640]
    c["Ib"] = lambda t=cb: t[:, 0:128]
    c["onesb"] = lambda t=cb: t[:, 512:640]
    return c


def consts_np():
    I = np.eye(128, dtype=np.float32)
    r = np.arange(128)
    U = (r[:, None] <= r[None, :]).astype(np.float32)
    Ls = (r[:, None] > r[None, :]).astype(np.float32)
    ones = np.ones((128, 128), np.float32)
    return np.ascontiguousarray(np.concatenate([I, I, U, Ls, ones], axis=1))


def phase_norm_T(k, c, src, nw, dst, nb, pfx):
    wB = k.sb(pfx + "wB", [128, D], F32)
    k.dma("sp", wB[:, :], nw[:, :], [nw], [wB])
    xts = [k.sb(pfx + "xt%d" % i, [128, D], F32) for i in range(2)]
    hnb = [k.sb(pfx + "hnb%d" % i, [128, D], BF16) for i in range(2)]
    junk = k.sb(pfx + "junk", [128, D], BF16)
    st = [k.sb(pfx + "st%d" % i, [128, NC_, TB], BF16) for i in range(2)]
    ss = k.sb(pfx + "ss", [128, 4], F32)
    tp = [k.ps(pfx + "tp%d" % i, [128, 8, 128], BF16) for i in range(2)]
    ev = 0
    for blk in range(nb):
        stg = st[blk % 2]
        for tl in range(3):
            i = blk * 3 + tl
            xt = xts[i % 2]
            hb = hnb[i % 2]
            k.dma("sp", xt[:, :], src[i * 128:(i + 1) * 128, :], [src], [xt])
            act(k, junk[:, :], xt[:, :], AF.Square, [xt], [junk, ss], accum=ss[:, 0:1])
            act(k, ss[:, 1:2], ss[:, 0:1], AF.Sqrt, [ss], [ss], bias=EPS, scale=1.0 / D)
            k.op("dve", lambda en: en.reciprocal(out=ss[:, 2:3], in_=ss[:, 1:2]), [ss], [ss])
            stt(k, hb[:, :], xt[:, :], ss[:, 2:3], wB[:, :], ALU.mult, ALU.mult, [xt, ss, wB], [hb])
            for g in range(4):
                tpp = tp[g % 2]
                for cc in range(8):
                    ch = g * 8 + cc
                    tr(k, tpp[:, cc, :], hb[:, ch * 128:(ch + 1) * 128], c["Ib"](), [hb, c["b"]], [tpp])
                e = "act" if ev % 2 == 0 else "dve"
                ev += 1
                cp(k, e, stg[:, g * 8:(g + 1) * 8, tl * 128:(tl + 1) * 128], tpp[:, :, :], [tpp], [stg])
        k.dma("sp", dst[blk], stg[:, :, :], [stg], [dst])


def v3(ap, b=128):
    return ap.rearrange("p (a b) -> p a b", b=b)


class PsumSet:
    def __init__(self, k, pfx):
        self.full = [k.ps(pfx + "pf%d" % i, [128, 512], F32) for i in range(3)]
        self.half = []
        for i in range(5):
            b = k.ps(pfx + "pb%d" % i, [128, 512], F32)
            self.half.append(b.sub(b.t[:, 0:256], pfx + "ph%da" % i))
            self.half.append(b.sub(b.t[:, 256:512], pfx + "ph%db" % i))


DN_STAGE = 99
DBG_SKIP = False


def phase_dn(k, c, P, hnT, win, cw, hp, onw, o0T, nb, nheads, pfx, dbg=None, out_cb=None):
    nt = nb * 3
    TT = nt * 128
    mmA, mmB, DD = P.full
    GBg, KQ, KS, CHA, CHB, PP, VN, OO, SU, TR = P.half
    cf, cb = c["f"], c["b"]
    hs_ = [slice(0, 128), slice(128, 256)]

    w_sb = k.sb(pfx + "w", [128, NC_, DNW], BF16)
    hbk = [k.sb(pfx + "hb%d" % i, [128, NC_, TB], BF16) for i in range(2)]
    kq = k.sb(pfx + "kq", [128, nt, 256], BF16)
    vT = k.sb(pfx + "vT", [128, 2, TT], BF16)
    zsw = k.sb(pfx + "zsw", [128, nt, 256], BF16)
    oT = k.sb(pfx + "oT", [128, 2, TT], BF16)
    betas = k.sb(pfx + "beta", [128, nt, 2], F32)
    negb = k.sb(pfx + "negb", [128, nt, 2], F32)
    alog = k.sb(pfx + "alog", [128, nt, 2], F32)
    gstep = k.sb(pfx + "gstep", [128, nt, 2], F32)
    cwt = k.sb(pfx + "cw", [128, nheads, 4, 4], F32)
    hpt = k.sb(pfx + "hp", [128, nheads, 4], F32)
    negA = k.sb(pfx + "negA", [128, nheads, 2], F32)
    onwt = k.sb(pfx + "onw", [128, 256], F32)
    rawb = [k.sb(pfx + "raw%d" % i, [128, TB + 3], F32) for i in range(4)]
    acc = k.sb(pfx + "acc", [128, TB], F32)
    yb = k.sb(pfx + "y", [128, TB], F32)
    sqb = k.sb(pfx + "sq", [128, TB], BF16)
    sb_ = k.sb(pfx + "s", [128, TB], F32)
    rn = k.sb(pfx + "rn", [128, TB], F32)
    zt = k.sb(pfx + "zt", [128, 256], BF16)
    RG1 = k.sb(pfx + "RG1", [128, 256], F32)
    RG2 = k.sb(pfx + "RG2", [128, 256], F32)
    decs = k.sb(pfx + "decs", [128, 512], F32)
    EB = k.sb(pfx + "EB", [128, 256], F32)
    eg = k.sb(pfx + "eg", [128, 4], F32)
    t1 = k.sb(pfx + "t1", [128, 256], F32)
    t2 = k.sb(pfx + "t2", [128, 256], F32)
    Ab = [k.sb(pfx + "A%d" % i, [128, 256], F32) for i in range(2)]
    Bb = [k.sb(pfx + "B%d" % i, [128, 256], F32) for i in range(2)]
    Pb = [k.sb(pfx + "P%d" % i, [128, 256], F32) for i in range(2)]
    attnT = k.sb(pfx + "attnT", [128, 256], BF16)
    WT = k.sb(pfx + "WT", [128, 256], BF16)
    qdecT = k.sb(pfx + "qdecT", [128, 256], BF16)
    vtok = k.sb(pfx + "vtok", [128, 256], BF16)
    kdec = k.sb(pfx + "kdec", [128, 256], BF16)
    R = k.sb(pfx + "R", [128, 256], BF16)
    vn = k.sb(pfx + "vn", [128, 256], BF16)
    S = k.sb(pfx + "S", [128, 256], F32)
    Sb = k.sb(pfx + "Sb", [128, 256], BF16)
    ssq = k.sb(pfx + "ssq", [128, 8], F32)
    og = k.sb(pfx + "og", [128, 256], BF16)
    junk = k.sb(pfx + "junk", [128, 256], BF16)
    TRb = TR[:, :].bitcast(BF16)

    k.dma("sp", cwt[:, :, :, :], cw[:, :, :, :], [cw], [cwt])
    k.dma("sp", hpt[:, :, :], hp[:, :, :], [hp], [hpt])
    k.dma("sp", onwt[:, :], onw[:, :], [onw], [onwt])
    act(k, negA[:, :, :], hpt[:, :, 2:4], AF.Exp, [hpt], [negA])
    ts(k, "dve", negA[:, :, :], negA[:, :, :], -1.0, ALU.mult, [negA], [negA])

    bi = 0
    for j in range(nheads):
        for q4 in range(4):
            k.dma("pool", w_sb[:, q4 * 8:(q4 + 1) * 8, :],
                  win[j, q4 * 1024:(q4 + 1) * 1024, :].rearrange("(c p) n -> p c n", p=128),
                  [win], [w_sb])
        for fc in range(4):
            k.op("dve", lambda en: en.memset(rawb[fc][:, 0:3], 0.0), [], [rawb[fc]])
        for tb in range(nb):
            hb = hbk[bi % 2]
            bi += 1
            k.dma("sp", hb[:, :, :], hnT[tb], [hnT], [hb])
            for fc in range(4):
                ps = mmA if fc % 2 == 0 else mmB
                for cc in range(NC_):
                    mm(k, ps[:, 0:TB], w_sb[:, cc, fc * 128:(fc + 1) * 128], hb[:, cc, :],
                       cc == 0, cc == NC_ - 1, [w_sb, hb], [ps])
                rb = rawb[fc]
                cp(k, "act", rb[:, 3:TB + 3], ps[:, 0:TB], [ps], [rb])
                ts(k, "dve", acc[:, :], rb[:, 3:TB + 3], cwt[:, j, fc, 3:4], ALU.mult, [rb, cwt], [acc])
                for tap in (2, 1, 0):
                    stt(k, acc[:, :], rb[:, tap:tap + TB], cwt[:, j, fc, tap:tap + 1], acc[:, :],
                        ALU.mult, ALU.add, [rb, cwt, acc], [acc])
                cp(k, "dve", rb[:, 0:3], rb[:, TB:TB + 3], [rb], [rb])
                if fc >= 2:
                    act(k, vT[:, fc - 2, tb * TB:(tb + 1) * TB], acc[:, :], AF.Silu, [acc], [vT])
                else:
                    act(k, yb[:, :], acc[:, :], AF.Silu, [acc], [yb])
                    act(k, sqb[:, :], yb[:, :], AF.Square, [yb], [sqb])
                    mm(k, DD[:, 0:TB], c["onesb"](), sqb[:, :], True, True, [cb, sqb], [DD])
                    act(k, sb_[:, :], DD[:, 0:TB], AF.Sqrt, [DD], [sb_], bias=EPS)
                    k.op("dve", lambda en: en.reciprocal(out=rn[:, :], in_=sb_[:, :]), [sb_], [rn])
                    if fc == 0:
                        stt(k, kq[:, 3 * tb:3 * tb + 3, 128:256], v3(yb[:, :]), 128.0 ** -0.5, v3(rn[:, :]),
                            ALU.mult, ALU.mult, [yb, rn], [kq])
                    else:
                        tt(k, "dve", kq[:, 3 * tb:3 * tb + 3, 0:128], v3(yb[:, :]), v3(rn[:, :]), ALU.mult,
                           [yb, rn], [kq])
            for tl in range(3):
                i = 3 * tb + tl
                ps = mmA if tl % 2 == 0 else mmB
                for cc in range(NC_):
                    mm(k, ps[:, 0:260], hb[:, cc, tl * 128:(tl + 1) * 128], w_sb[:, cc, 512:772],
                       cc == 0, cc == NC_ - 1, [w_sb, hb], [ps])
                act(k, zt[:, :], ps[:, 0:256], AF.Silu, [ps], [zt])
                tt(k, "dve", zsw[:, i, :], zt[:, :], onwt[:, :], ALU.mult, [zt, onwt], [zsw])
                act(k, betas[:, i, :], ps[:, 256:258], AF.Sigmoid, [ps], [betas])
                tt(k, "dve", alog[:, i, :], ps[:, 258:260], hpt[:, j, 0:2], ALU.add, [ps, hpt], [alog])
        act(k, alog[:, :, :], alog[:, :, :], AF.Exp, [alog], [alog])
        act(k, alog[:, :, :], alog[:, :, :], AF.Ln, [alog], [alog], bias=1.0)
        for h in range(2):
            ts(k, "dve", gstep[:, :, h], alog[:, :, h], negA[:, j, h:h + 1], ALU.mult, [alog, negA], [gstep])
        ts(k, "dve", negb[:, :, :], betas[:, :, :], -1.0, ALU.mult, [betas], [negb])
        k.op("dve", lambda en: en.memset(S[:, :], 0.0), [], [S])
        k.op("dve", lambda en: en.memset(Sb[:, :], 0.0), [], [Sb])

        if dbg is not None:
            k.dma("sp", dbg["kq"][:, :, :], kq[:, :, :], [kq], [dbg["kq"]])
            k.dma("sp", dbg["vT"][:, :, :], vT[:, :, :], [vT], [dbg["vT"]])
            k.dma("sp", dbg["zsw"][:, :, :], zsw[:, :, :], [zsw], [dbg["zsw"]])
            k.dma("sp", dbg["beta"][:, :, :], betas[:, :, :], [betas], [dbg["beta"]])
            k.dma("sp", dbg["gstep"][:, :, :], gstep[:, :, :], [gstep], [dbg["gstep"]])
        for n in range(nt if DN_STAGE >= 2 else 0):
            ST = DN_STAGE
            kT = kq[:, n, 0:128]
            qT = kq[:, n, 128:256]
            for h in range(2):
                ts(k, "dve", RG1[:, hs_[h]], c["U"](), gstep[:, n, h:h + 1], ALU.mult, [cf, gstep], [RG1])
                ts(k, "pool", RG2[:, hs_[h]], c["Ls"](), gstep[:, n, h:h + 1], ALU.mult, [cf, gstep], [RG2])
            mm(k, DD[:, 0:256], c["U"](), RG2[:, :], True, True, [cf, RG2], [DD])
            mm(k, DD[:, 256:512], c["Ls"](), RG1[:, :], True, True, [cf, RG1], [DD])
            mm(k, GBg[:, :], c["ones"](), RG1[:, :], True, True, [cf, RG1], [GBg])
            mm(k, TR[:, 0:2], c["U"](), gstep[:, n, 0:2], True, True, [cf, gstep], [TR])
            act(k, decs[:, :], DD[:, :], AF.Exp, [DD], [decs])
            act(k, EB[:, :], GBg[:, :], AF.Exp, [GBg], [EB])
            act(k, eg[:, 0:2], TR[:, 0:2], AF.Exp, [TR], [eg])
            ts(k, "dve", eg[:, 2:4], eg[:, 0:2], -1.0, ALU.mult, [eg], [eg])
            if ST < 3:
                continue
            mm(k, KQ[:, :], kT, kq[:, n, :], True, True, [kq], [KQ])
            A0, B0 = Ab[0], Bb[0]
            for h in range(2):
                tt(k, "dve", t1[:, hs_[h]], KQ[:, 0:128], decs[:, h * 128:(h + 1) * 128], ALU.mult,
                   [KQ, decs], [t1])
                stt(k, A0[:, hs_[h]], t1[:, hs_[h]], negb[:, n, h:h + 1], c["Ls"](), ALU.mult, ALU.mult,
                    [t1, negb, cf], [A0])
                tt(k, "dve", t2[:, hs_[h]], KQ[:, 128:256], decs[:, 256 + h * 128:256 + (h + 1) * 128],
                   ALU.mult, [KQ, decs], [t2])
                tt(k, "pool", attnT[:, hs_[h]], t2[:, hs_[h]], c["U"](), ALU.mult, [t2, cf], [attnT])
            if ST < 4:
                continue
            for h in range(2):
                tr(k, TR[:, hs_[h]], A0[:, hs_[h]], c["I"](), [A0, cf], [TR])
            cp(k, "act", B0[:, :], TR[:, :], [TR], [B0])
            if ST < 5:
                continue
            ai, pi = 0, 0
            tt(k, "pool", Pb[0][:, :], c["I2"](), B0[:, :], ALU.add, [cf, B0], [Pb[0]])
            for m in range(1, 7):
                Ac, Bc = Ab[ai], Bb[ai]
                An, Bn = Ab[1 - ai], Bb[1 - ai]
                for h in range(2):
                    mm(k, CHA[:, hs_[h]], Bc[:, hs_[h]], Ac[:, hs_[h]], True, True, [Ac, Bc], [CHA])
                if m <= 5:
                    for h in range(2):
                        mm(k, CHB[:, hs_[h]], Ac[:, hs_[h]], Bc[:, hs_[h]], True, True, [Ac, Bc], [CHB])
                cp(k, "act", An[:, :], CHA[:, :], [CHA], [An])
                if m <= 5:
                    cp(k, "dve", Bn[:, :], CHB[:, :], [CHB], [Bn])
                Pc, Pn = Pb[pi], Pb[1 - pi]
                for h in range(2):
                    mm(k, PP[:, hs_[h]], An[:, hs_[h]], Pc[:, hs_[h]], True, True, [An, Pc], [PP])
                tt(k, "dve", Pn[:, :], Pc[:, :], PP[:, :], ALU.add, [Pc, PP], [Pn])
                ai, pi = 1 - ai, 1 - pi
            Pf = Pb[pi]
            for h in range(2):
                ts(k, "dve", WT[:, hs_[h]], Pf[:, hs_[h]], betas[:, n, h:h + 1], ALU.mult, [Pf, betas], [WT])
                tt(k, "pool", qdecT[:, hs_[h]], qT, EB[:, hs_[h]], ALU.mult, [kq, EB], [qdecT])
            if ST < 6:
                continue
            for h in range(2):
                tr(k, TRb[:, hs_[h]], vT[:, h, n * 128:(n + 1) * 128], c["Ib"](), [vT, cb], [TR])
            cp(k, "act", vtok[:, :], TRb[:, 0:256], [TR], [vtok])
            tr(k, TRb[:, 0:128], kT, c["Ib"](), [kq, cb], [TR])
            for h in range(2):
                col = 256 + h * 128 + 127
                ts(k, "dve", kdec[:, hs_[h]], TRb[:, 0:128], decs[:, col:col + 1], ALU.mult, [TR, decs], [kdec])
            if ST < 7:
                continue
            mm(k, KS[:, :], kT, Sb[:, :], True, True, [kq, Sb], [KS])
            for h in range(2):
                stt(k, R[:, hs_[h]], KS[:, hs_[h]], eg[:, 2 + h:3 + h], vtok[:, hs_[h]], ALU.mult, ALU.add,
                    [KS, eg, vtok], [R])
            for h in range(2):
                mm(k, VN[:, hs_[h]], WT[:, hs_[h]], R[:, hs_[h]], True, True, [WT, R], [VN])
            cp(k, "act", vn[:, :], VN[:, :], [VN], [vn])
            for h in range(2):
                mm(k, OO[:, hs_[h]], qdecT[:, hs_[h]], Sb[:, hs_[h]], True, False, [qdecT, Sb], [OO])
                mm(k, OO[:, hs_[h]], attnT[:, hs_[h]], vn[:, hs_[h]], False, True, [attnT, vn], [OO])
            for h in range(2):
                mm(k, SU[:, hs_[h]], kdec[:, hs_[h]], vn[:, hs_[h]], True, True, [kdec, vn], [SU])
            for h in range(2):
                col = h * 128 + 127
                stt(k, S[:, hs_[h]], S[:, hs_[h]], EB[:, col:col + 1], SU[:, hs_[h]], ALU.mult, ALU.add,
                    [S, EB, SU], [S])
            cp(k, "act", Sb[:, :], S[:, :], [S], [Sb])
            for h in range(2):
                act(k, junk[:, hs_[h]], OO[:, hs_[h]], AF.Square, [OO], [junk, ssq], accum=ssq[:, h:h + 1])
            act(k, ssq[:, 2:4], ssq[:, 0:2], AF.Sqrt, [ssq], [ssq], bias=EPS, scale=1.0 / 128)
            k.op("dve", lambda en: en.reciprocal(out=ssq[:, 4:6], in_=ssq[:, 2:4]), [ssq], [ssq])
            for h in range(2):
                stt(k, og[:, hs_[h]], OO[:, hs_[h]], ssq[:, 4 + h:5 + h], zsw[:, n, hs_[h]], ALU.mult, ALU.mult,
                    [OO, ssq, zsw], [og])
            for h in range(2):
                tr(k, TRb[:, hs_[h]], og[:, hs_[h]], c["Ib"](), [og, cb], [TR])
            cp(k, "act", oT[:, :, n * 128:(n + 1) * 128], v3(TRb[:, 0:256]), [TR], [oT])
        if out_cb is not None:
            out_cb(j, oT)
        else:
            for h in range(2):
                k.dma("sp", o0T[:, :, 2 * j + h, :].rearrange("t p x -> p t x"), v3(oT[:, h, :], TB), [oT], [o0T])


def run_chains(pre_gens, state_gen_fn, n, width):
    pres = {}
    nxt = 0
    done_pre = set()
    st = None
    st_i = 0
    while st_i < n:
        while nxt < n and len(pres) < width and nxt < st_i + width:
            pres[nxt] = pre_gens(nxt)
            nxt += 1
        for i in sorted(pres):
            try:
                next(pres[i])
            except StopIteration:
                done_pre.add(i)
                del pres[i]
        if st is None and st_i in done_pre:
            st = state_gen_fn(st_i)
        if st is not None:
            try:
                next(st)
            except StopIteration:
                st = None
                st_i += 1


def phase_dn2(k, c, hnT, win, cw, hp, onw, nb, nheads, pfx, out_cb, NS=3, bg_cb=None):
    nt = nb * 3
    TT = nt * 128
    banks = [k.ps(pfx + "bk%d" % i, [128, 512], F32) for i in range(8)]
    mmA, mmB, Z1, Z2 = banks[0:4]
    DD = Z2
    cf, cb = c["f"], c["b"]
    hs_ = [slice(0, 128), slice(128, 256)]

    w_sb = k.sb(pfx + "w", [128, NC_, DNW], BF16)
    hbk = [k.sb(pfx + "hb%d" % i, [128, NC_, TB], BF16) for i in range(2)]
    kq = k.sb(pfx + "kq", [128, nt, 256], BF16)
    vT = k.sb(pfx + "vT", [128, 2, TT], BF16)
    zsw = k.sb(pfx + "zsw", [128, nt, 256], BF16)
    ostg = [k.sb(pfx + "ostg%d" % i, [128, 2, TB], BF16) for i in range(2)]
    betas = k.sb(pfx + "beta", [128, nt, 2], F32)
    negb = k.sb(pfx + "negb", [128, nt, 2], F32)
    alog = k.sb(pfx + "alog", [128, nt, 2], F32)
    gstep = k.sb(pfx + "gstep", [128, nt, 2], F32)
    cwt = k.sb(pfx + "cw", [128, nheads, 4, 4], F32)
    hpt = k.sb(pfx + "hp", [128, nheads, 4], F32)
    negA = k.sb(pfx + "negA", [128, nheads, 2], F32)
    onwt = k.sb(pfx + "onw", [128, 256], F32)
    rawb = [k.sb(pfx + "raw%d" % i, [128, TB + 3], F32) for i in range(4)]
    sqb = k.sb(pfx + "sq", [128, TB], BF16)
    sb_ = k.sb(pfx + "s", [128, TB], F32)
    rn = k.sb(pfx + "rn", [128, TB], F32)
    zt = k.sb(pfx + "zt", [128, 256], BF16)
    junk = zt

    class Set:
        pass
    sets = []
    for i in range(NS):
        S_ = Set()
        S_.X, S_.Y = (banks[4 + 2 * i], banks[5 + 2 * i]) if i < 2 else (mmA, mmB)
        S_.X1 = k.sb(pfx + "X1_%d" % i, [128, 256], F32)
        S_.X2 = k.sb(pfx + "X2_%d" % i, [128, 256], F32)
        S_.decs = k.sb(pfx + "decs%d" % i, [128, 512], F32)
        S_.EB = k.sb(pfx + "EB%d" % i, [128, 256], F32)
        S_.eg = k.sb(pfx + "eg%d" % i, [128, 8], F32)
        S_.A = [k.sb(pfx + "A%d_%d" % (i, q), [128, 256], F32) for q in range(2)]
        S_.B = [k.sb(pfx + "B%d_%d" % (i, q), [128, 256], F32) for q in range(2)]
        S_.P = k.sb(pfx + "P%d" % i, [128, 256], F32)
        S_.attnT = k.sb(pfx + "attnT%d" % i, [128, 256], BF16)
        S_.WT = k.sb(pfx + "WT%d" % i, [128, 256], BF16)
        S_.qdecT = k.sb(pfx + "qdecT%d" % i, [128, 256], BF16)
        S_.vtok = k.sb(pfx + "vtok%d" % i, [128, 256], BF16)
        S_.kdec = k.sb(pfx + "kdec%d" % i, [128, 256], BF16)
        sets.append(S_)
    acc = sets[0].decs.sub(sets[0].decs.t[:, 0:TB], pfx + "acc")
    ybs = [sets[1 % NS].decs.sub(sets[1 % NS].decs.t[:, 0:TB], pfx + "yb0"),
           sets[2 % NS].decs.sub(sets[2 % NS].decs.t[:, 0:TB], pfx + "yb1")]
    deferred = []
    R = k.sb(pfx + "R", [128, 256], BF16)
    vn = k.sb(pfx + "vn", [128, 256], BF16)
    S = k.sb(pfx + "S", [128, 256], F32)
    Sb = k.sb(pfx + "Sb", [128, 256], BF16)
    ssq = k.sb(pfx + "ssq", [128, 8], F32)
    og = k.sb(pfx + "og", [128, 256], BF16)

    k.dma("sp", cwt[:, :, :, :], cw[:, :, :, :], [cw], [cwt])
    k.dma("sp", hpt[:, :, :], hp[:, :, :], [hp], [hpt])
    k.dma("sp", onwt[:, :], onw[:, :], [onw], [onwt])
    act(k, negA[:, :, :], hpt[:, :, 2:4], AF.Exp, [hpt], [negA])
    ts(k, "dve", negA[:, :, :], negA[:, :, :], -1.0, ALU.mult, [negA], [negA])

    def load_w(jj):
        for q4 in range(4):
            k.dma("pool", w_sb[:, q4 * 8:(q4 + 1) * 8, :],
                  win[jj, q4 * 1024:(q4 + 1) * 1024, :].rearrange("(c p) n -> p c n", p=128),
                  [win], [w_sb])

    bi = 0
    for j in range(nheads):
        if j == 0:
            load_w(0)
        for fc in range(4):
            k.op("dve", lambda en: en.memset(rawb[fc][:, 0:3], 0.0), [], [rawb[fc]])
        for tb in range(nb):
            hb = hbk[bi % 2]
            bi += 1
            k.dma("sp", hb[:, :, :], hnT[tb], [hnT], [hb])
            for fc in range(4):
                ps = mmA if fc % 2 == 0 else mmB
                for cc in range(NC_):
                    mm(k, ps[:, 0:TB], w_sb[:, cc, fc * 128:(fc + 1) * 128], hb[:, cc, :],
                       cc == 0, cc == NC_ - 1, [w_sb, hb], [ps])
                while deferred:
                    deferred.pop(0)()
                rb = rawb[fc]
                cp(k, "act", rb[:, 3:TB + 3], ps[:, 0:TB], [ps], [rb])
                ts(k, "dve", acc[:, :], rb[:, 3:TB + 3], cwt[:, j, fc, 3:4], ALU.mult, [rb, cwt], [acc])
                for tap in (2, 1, 0):
                    stt(k, acc[:, :], rb[:, tap:tap + TB], cwt[:, j, fc, tap:tap + 1], acc[:, :],
                        ALU.mult, ALU.add, [rb, cwt, acc], [acc])
                cp(k, "dve", rb[:, 0:3], rb[:, TB:TB + 3], [rb], [rb])
                if fc >= 2:
                    act(k, vT[:, fc - 2, tb * TB:(tb + 1) * TB], acc[:, :], AF.Silu, [acc], [vT])
                else:
                    ybf = ybs[fc]
                    act(k, ybf[:, :], acc[:, :], AF.Silu, [acc], [ybf])

                    def norm_part(fc=fc, tb=tb, ybf=ybf):
                        act(k, sqb[:, :], ybf[:, :], AF.Square, [ybf], [sqb])
                        mm(k, DD[:, 0:TB], c["onesb"](), sqb[:, :], True, True, [cb, sqb], [DD])
                        act(k, sb_[:, :], DD[:, 0:TB], AF.Sqrt, [DD], [sb_], bias=EPS)
                        k.op("dve", lambda en: en.reciprocal(out=rn[:, :], in_=sb_[:, :]), [sb_], [rn])
                        if fc == 0:
                            stt(k, kq[:, 3 * tb:3 * tb + 3, 128:256], v3(ybf[:, :]), 128.0 ** -0.5, v3(rn[:, :]),
                                ALU.mult, ALU.mult, [ybf, rn], [kq])
                        else:
                            tt(k, "dve", kq[:, 3 * tb:3 * tb + 3, 0:128], v3(ybf[:, :]), v3(rn[:, :]), ALU.mult,
                               [ybf, rn], [kq])
                    deferred.append(norm_part)
            for tl in range(3):
                i = 3 * tb + tl
                ps = mmA if tl % 2 == 0 else mmB
                for cc in range(NC_):
                    mm(k, ps[:, 0:260], hb[:, cc, tl * 128:(tl + 1) * 128], w_sb[:, cc, 512:772],
                       cc == 0, cc == NC_ - 1, [w_sb, hb], [ps])
                while deferred:
                    deferred.pop(0)()
                act(k, zt[:, :], ps[:, 0:256], AF.Silu, [ps], [zt])
                tt(k, "dve", zsw[:, i, :], zt[:, :], onwt[:, :], ALU.mult, [zt, onwt], [zsw])
                act(k, betas[:, i, :], ps[:, 256:258], AF.Sigmoid, [ps], [betas])
                tt(k, "dve", alog[:, i, :], ps[:, 258:260], hpt[:, j, 0:2], ALU.add, [ps, hpt], [alog])
        act(k, alog[:, :, :], alog[:, :, :], AF.Exp, [alog], [alog])
        act(k, alog[:, :, :], alog[:, :, :], AF.Ln, [alog], [alog], bias=1.0)
        for h in range(2):
            ts(k, "dve", gstep[:, :, h], alog[:, :, h], negA[:, j, h:h + 1], ALU.mult, [alog, negA], [gstep])
        ts(k, "dve", negb[:, :, :], betas[:, :, :], -1.0, ALU.mult, [betas], [negb])
        k.op("dve", lambda en: en.memset(S[:, :], 0.0), [], [S])
        k.op("dve", lambda en: en.memset(Sb[:, :], 0.0), [], [Sb])

        if j + 1 < nheads:
            load_w(j + 1)
        if bg_cb is not None:
            bg_cb(j)
        def pre(n, j=j):
            st_ = sets[n % NS]
            X, Y = st_.X, st_.Y
            Xb = X[:, :].bitcast(BF16)
            Yb = Y[:, :].bitcast(BF16)
            kT = kq[:, n, 0:128]
            qT = kq[:, n, 128:256]
            RG1, RG2, decs, EB, eg = st_.X1, st_.X2, st_.decs, st_.EB, st_.eg
            for h in range(2):
                ts(k, "dve", RG1[:, hs_[h]], c["U"](), gstep[:, n, h:h + 1], ALU.mult, [cf, gstep], [RG1])
                ts(k, "pool", RG2[:, hs_[h]], c["Ls"](), gstep[:, n, h:h + 1], ALU.mult, [cf, gstep], [RG2])
            yield
            mm(k, X[:, 0:256], c["U"](), RG2[:, :], True, True, [cf, RG2], [X])
            mm(k, X[:, 256:512], c["Ls"](), RG1[:, :], True, True, [cf, RG1], [X])
            mm(k, Y[:, 0:256], c["ones"](), RG1[:, :], True, True, [cf, RG1], [Y])
            mm(k, Y[:, 256:258], c["U"](), gstep[:, n, 0:2], True, True, [cf, gstep], [Y])
            yield
            act(k, decs[:, :], X[:, :], AF.Exp, [X], [decs])
            act(k, EB[:, :], Y[:, 0:256], AF.Exp, [Y], [EB])
            act(k, eg[:, 0:2], Y[:, 256:258], AF.Exp, [Y], [eg])
            yield
            ts(k, "dve", eg[:, 2:4], eg[:, 0:2], -1.0, ALU.mult, [eg], [eg])
            for h in range(2):
                col = h * 128 + 127
                cp(k, "dve", eg[:, 4 + h:5 + h], EB[:, col:col + 1], [EB], [eg])
            mm(k, X[:, 0:256], kT, kq[:, n, :], True, True, [kq], [X])
            yield
            A0, B0 = st_.A[0], st_.B[0]
            t1, t2 = RG1, RG2
            for h in range(2):
                tt(k, "dve", t1[:, hs_[h]], X[:, 0:128], decs[:, h * 128:(h + 1) * 128], ALU.mult, [X, decs], [t1])
                stt(k, A0[:, hs_[h]], t1[:, hs_[h]], negb[:, n, h:h + 1], c["Ls"](), ALU.mult, ALU.mult,
                    [t1, negb, cf], [A0])
            yield
            for h in range(2):
                tr(k, Y[:, hs_[h]], A0[:, hs_[h]], c["I"](), [A0, cf], [Y])
            for h in range(2):
                tt(k, "dve", t2[:, hs_[h]], X[:, 128:256], decs[:, 256 + h * 128:256 + (h + 1) * 128],
                   ALU.mult, [X, decs], [t2])
                tt(k, "pool", st_.attnT[:, hs_[h]], t2[:, hs_[h]], c["U"](), ALU.mult, [t2, cf], [st_.attnT])
            yield
            cp(k, "act", B0[:, :], Y[:, 0:256], [Y], [B0])
            yield
            Pm = st_.P
            tt(k, "pool", Pm[:, :], c["I2"](), B0[:, :], ALU.add, [cf, B0], [Pm])
            ai = 0
            for m in range(1, 7):
                Ac, Bc = st_.A[ai], st_.B[ai]
                An, Bn = st_.A[1 - ai], st_.B[1 - ai]
                for h in range(2):
                    mm(k, X[:, hs_[h]], Bc[:, hs_[h]], Ac[:, hs_[h]], True, True, [Ac, Bc], [X])
                if m <= 5:
                    for h in range(2):
                        mm(k, Y[:, hs_[h]], Ac[:, hs_[h]], Bc[:, hs_[h]], True, True, [Ac, Bc], [Y])
                yield
                cp(k, "act", An[:, :], X[:, 0:256], [X], [An])
                if m <= 5:
                    cp(k, "dve", Bn[:, :], Y[:, 0:256], [Y], [Bn])
                yield
                for h in range(2):
                    mm(k, X[:, 256 + h * 128:384 + h * 128], An[:, hs_[h]], Pm[:, hs_[h]], True, True, [An, Pm], [X])
                yield
                tt(k, "dve", Pm[:, :], Pm[:, :], X[:, 256:512], ALU.add, [Pm, X], [Pm])
                ai = 1 - ai
            yield
            for h in range(2):
                ts(k, "dve", st_.WT[:, hs_[h]], Pm[:, hs_[h]], betas[:, n, h:h + 1], ALU.mult, [Pm, betas], [st_.WT])
                tt(k, "pool", st_.qdecT[:, hs_[h]], qT, EB[:, hs_[h]], ALU.mult, [kq, EB], [st_.qdecT])
            for h in range(2):
                tr(k, Yb[:, hs_[h]], vT[:, h, n * 128:(n + 1) * 128], c["Ib"](), [vT, cb], [Y])
            tr(k, Xb[:, 0:128], kT, c["Ib"](), [kq, cb], [X])
            yield
            cp(k, "act", st_.vtok[:, :], Yb[:, 0:256], [Y], [st_.vtok])
            for h in range(2):
                col = 256 + h * 128 + 127
                ts(k, "dve", st_.kdec[:, hs_[h]], Xb[:, 0:128], decs[:, col:col + 1], ALU.mult, [X, decs], [st_.kdec])
            yield

        def state(n, j=j):
            st_ = sets[n % NS]
            eg = st_.eg
            kT = kq[:, n, 0:128]
            Z1b = Z1[:, :].bitcast(BF16)
            mm(k, Z1[:, 0:256], kT, Sb[:, :], True, True, [kq, Sb], [Z1])
            yield
            for h in range(2):
                stt(k, R[:, hs_[h]], Z1[:, hs_[h]], eg[:, 2 + h:3 + h], st_.vtok[:, hs_[h]], ALU.mult, ALU.add,
                    [Z1, eg, st_.vtok], [R])
            yield
            for h in range(2):
                mm(k, Z2[:, hs_[h]], st_.WT[:, hs_[h]], R[:, hs_[h]], True, True, [st_.WT, R], [Z2])
            yield
            cp(k, "act", vn[:, :], Z2[:, 0:256], [Z2], [vn])
            yield
            for h in range(2):
                mm(k, Z2[:, 256 + h * 128:384 + h * 128], st_.kdec[:, hs_[h]], vn[:, hs_[h]], True, True,
                   [st_.kdec, vn], [Z2])
            for h in range(2):
                mm(k, Z1[:, 256 + h * 128:384 + h * 128], st_.qdecT[:, hs_[h]], Sb[:, hs_[h]], True, False,
                   [st_.qdecT, Sb], [Z1])
                mm(k, Z1[:, 256 + h * 128:384 + h * 128], st_.attnT[:, hs_[h]], vn[:, hs_[h]], False, True,
                   [st_.attnT, vn], [Z1])
            yield
            for h in range(2):
                stt(k, S[:, hs_[h]], S[:, hs_[h]], eg[:, 4 + h:5 + h], Z2[:, 256 + h * 128:384 + h * 128],
                    ALU.mult, ALU.add, [S, eg, Z2], [S])
            yield
            cp(k, "act", Sb[:, :], S[:, :], [S], [Sb])
            for h in range(2):
                act(k, junk[:, hs_[h]], Z1[:, 256 + h * 128:384 + h * 128], AF.Square, [Z1], [junk, ssq],
                    accum=ssq[:, h:h + 1])
            yield
            act(k, ssq[:, 2:4], ssq[:, 0:2], AF.Sqrt, [ssq], [ssq], bias=EPS, scale=1.0 / 128)
            yield
            k.op("dve", lambda en: en.reciprocal(out=ssq[:, 4:6], in_=ssq[:, 2:4]), [ssq], [ssq])
            for h in range(2):
                stt(k, og[:, hs_[h]], Z1[:, 256 + h * 128:384 + h * 128], ssq[:, 4 + h:5 + h], zsw[:, n, hs_[h]],
                    ALU.mult, ALU.mult, [Z1, ssq, zsw], [og])
            yield
            for h in range(2):
                tr(k, Z1b[:, hs_[h]], og[:, hs_[h]], c["Ib"](), [og, cb], [Z1])
            yield
            stg = ostg[(n // 3) % 2]
            tl = n % 3
            cp(k, "act", stg[:, :, tl * 128:(tl + 1) * 128], v3(Z1b[:, 0:256]), [Z1], [stg])
            if tl == 2:
                out_cb(j, n // 3, stg)
            yield

        run_chains(pre, state, nt if not DBG_SKIP else 0, NS)


def phase_outproj(k, P, xTs, kcs, wout, hres, out, nb, pfx, ssq_out=None, out_rows=None, load_cb=None,
                  wq="pool"):
    KC = sum(kcs)
    nt = nb * 3
    mmA, mmB, DD = P.full
    w_sb = k.sb(pfx + "w", [128, KC, 512], BF16)
    xb = [k.sb(pfx + "xb%d" % i, [128, KC, TB], BF16) for i in range(2)]
    res = [k.sb(pfx + "res%d" % i, [128, 512], F32) for i in range(2)]
    ot = [k.sb(pfx + "ot%d" % i, [128, 512], F32) for i in range(2)]
    junk = k.sb(pfx + "junk", [128, 512], BF16)
    ssa = k.sb(pfx + "ssa", [128, 2, nt], F32)
    bi = 0
    ti = 0
    for cbk in range(2):
        cs = slice(cbk * 512, (cbk + 1) * 512)
        for q8 in range(KC // 8):
            k.dma(wq, w_sb[:, q8 * 8:(q8 + 1) * 8, :],
                  wout[q8 * 1024:(q8 + 1) * 1024, cs].rearrange("(c p) n -> p c n", p=128), [wout], [w_sb])
        for tb in range(nb):
            xbb = xb[bi % 2]
            bi += 1
            off = 0
            if load_cb is not None:
                load_cb(tb, xbb)
            else:
                for xT, kc in zip(xTs, kcs):
                    k.dma("sp", xbb[:, off:off + kc, :], xT[tb], [xT], [xbb])
                    off += kc
            for tl in range(3):
                i = 3 * tb + tl
                ps = mmA if ti % 2 == 0 else mmB
                rs, o_ = res[ti % 2], ot[ti % 2]
                ti += 1
                k.dma("sp", rs[:, :], hres[i * 128:(i + 1) * 128, cs], [hres], [rs])
                for cc in range(KC):
                    mm(k, ps[:, :], xbb[:, cc, tl * 128:(tl + 1) * 128], w_sb[:, cc, :], cc == 0, cc == KC - 1,
                       [xbb, w_sb], [ps])
                tt(k, "dve", o_[:, :], ps[:, :], rs[:, :], ALU.add, [ps, rs], [o_])
                if ssq_out is not None:
                    act(k, junk[:, :], o_[:, :], AF.Square, [o_], [junk, ssa], accum=ssa[:, cbk, i:i + 1])
                if out_rows is None:
                    k.dma("sp", out[i * 128:(i + 1) * 128, cs], o_[:, :], [o_], [out])
                else:
                    for dst_rows, src_rows in out_rows(i):
                        k.dma("sp", out[dst_rows, cs], o_[src_rows, :], [o_], [out])
    if ssq_out is not None:
        tt(k, "dve", ssa[:, 0, :], ssa[:, 0, :], ssa[:, 1, :], ALU.add, [ssa], [ssa])
        k.dma("sp", ssq_out[:, :], ssa[:, 0, :], [ssa], [ssq_out])


def phase_norm2(k, c, P, ssq_all, h1, nw, dst, nb, pfx, ssq_view=None, out_cb=None):
    nt = nb * 3
    TR = P.half[9]
    TRb = TR[:, :].bitcast(BF16)
    cb = c["b"]
    sq = k.sb(pfx + "sq", [128, 4, nt], F32)
    rstd = k.sb(pfx + "rstd", [128, nt], F32)
    wB = k.sb(pfx + "wB", [128, 1024], F32)
    xts = [k.sb(pfx + "xt%d" % i, [128, 1024], F32) for i in range(2)]
    hnb = [k.sb(pfx + "hn%d" % i, [128, 1024], BF16) for i in range(2)]
    st = [k.sb(pfx + "st%d" % i, [128, 8, TB], BF16) for i in range(2)]
    k.dma("sp", sq[:, :, :], ssq_view if ssq_view is not None else ssq_all[:, :, :].rearrange("g p n -> p g n"),
          [ssq_all], [sq])
    k.dma("sp", wB[:, :], nw[:, :], [nw], [wB])
    for g in range(1, 4):
        tt(k, "dve", sq[:, 0, :], sq[:, 0, :], sq[:, g, :], ALU.add, [sq], [sq])
    act(k, sq[:, 1, :], sq[:, 0, :], AF.Sqrt, [sq], [sq], bias=EPS, scale=1.0 / D)
    k.op("dve", lambda en: en.reciprocal(out=rstd[:, :], in_=sq[:, 1, :]), [sq], [rstd])
    for blk in range(nb):
        stg = st[blk % 2]
        for tl in range(3):
            i = blk * 3 + tl
            xt, hb = xts[i % 2], hnb[i % 2]
            k.dma("sp", xt[:, :], h1[i * 128:(i + 1) * 128, :], [h1], [xt])
            stt(k, hb[:, :], xt[:, :], rstd[:, i:i + 1], wB[:, :], ALU.mult, ALU.mult, [xt, rstd, wB], [hb])
            for cc in range(8):
                tr(k, TRb[:, (cc % 4) * 128:(cc % 4 + 1) * 128], hb[:, cc * 128:(cc + 1) * 128], c["Ib"](),
                   [hb, cb], [TR])
                if cc % 4 == 3:
                    g4 = cc // 4
                    cp(k, "act", stg[:, g4 * 4:(g4 + 1) * 4, tl * 128:(tl + 1) * 128], v3(TRb[:, 0:512]),
                       [TR], [stg])
        if out_cb is not None:
            out_cb(blk, stg)
        else:
            k.dma("sp", dst[blk], stg[:, :, :], [stg], [dst])


def phase_sb(k, c, P, hnTs, win, nwqk, o1T, nb, nheads, pfx, load_cb=None, out_cb=None):
    nt = nb * 3
    TT = nt * 128
    mmA, mmB, DD = P.full
    ZP = P.half[0].root
    TR = P.half[2]
    OO = P.half[4]
    TRb = TR[:, :].bitcast(BF16)
    cf, cb = c["f"], c["b"]
    w_sb = k.sb(pfx + "w", [128, NC_, 512], BF16)
    hbk = [k.sb(pfx + "hb%d" % i, [128, NC_, TB], BF16) for i in range(2)]
    qT = k.sb(pfx + "qT", [128, TT], BF16)
    kT = k.sb(pfx + "kT", [128, TT], BF16)
    vtok = k.sb(pfx + "vtok", [128, nt, 128], BF16)
    gs = k.sb(pfx + "gs", [128, nt, 128], BF16)
    oT = k.sb(pfx + "oT", [128, TT], BF16)
    nwt = k.sb(pfx + "nw", [128, 2], F32)
    sqb = k.sb(pfx + "sq", [128, TB], BF16)
    sb_ = k.sb(pfx + "s", [128, TB], F32)
    rn = k.sb(pfx + "rn", [128, TB], F32)
    ones512 = k.sb(pfx + "ones", [128, 512], F32)
    E1 = k.sb(pfx + "E1", [128, 512], F32)
    SP = k.sb(pfx + "SP", [128, 512], F32)
    PRE = k.sb(pfx + "PRE", [128, 512], F32)
    ARG = k.sb(pfx + "ARG", [128, 512], F32)
    Aw = k.sb(pfx + "Aw", [128, 512], BF16)
    AT = k.sb(pfx + "AT", [128, 512], BF16)
    car = k.sb(pfx + "car", [128, 4], F32)
    og = k.sb(pfx + "og", [128, 128], BF16)
    k.dma("sp", nwt[:, :], nwqk[:, :], [nwqk], [nwt])
    k.op("dve", lambda en: en.memset(ones512[:, :], 1.0), [], [ones512])
    bi = 0
    for j in range(nheads):
        for q4 in range(4):
            k.dma("pool", w_sb[:, q4 * 8:(q4 + 1) * 8, :],
                  win[j, q4 * 1024:(q4 + 1) * 1024, :].rearrange("(c p) n -> p c n", p=128), [win], [w_sb])
        for tb in range(nb):
            hb = hbk[bi % 2]
            bi += 1
            if load_cb is not None:
                load_cb(tb, hb)
            else:
                for g in range(4):
                    k.dma("sp", hb[:, g * 8:(g + 1) * 8, :], hnTs[g][tb], [hnTs[g]], [hb])
            for fc in range(2):
                ps = mmA if fc == 0 else mmB
                for cc in range(NC_):
                    mm(k, ps[:, 0:TB], w_sb[:, cc, fc * 128:(fc + 1) * 128], hb[:, cc, :], cc == 0, cc == NC_ - 1,
                       [w_sb, hb], [ps])
                act(k, sqb[:, :], ps[:, 0:TB], AF.Square, [ps], [sqb])
                mm(k, DD[:, 0:TB], c["onesb"](), sqb[:, :], True, True, [cb, sqb], [DD])
                act(k, sb_[:, :], DD[:, 0:TB], AF.Sqrt, [DD], [sb_], bias=EPS, scale=1.0 / 128)
                k.op("dve", lambda en: en.reciprocal(out=rn[:, :], in_=sb_[:, :]), [sb_], [rn])
                dstT = qT if fc == 0 else kT
                stt(k, rn[:, :], rn[:, :], nwt[:, fc:fc + 1], ps[:, 0:TB], ALU.mult, ALU.mult, [rn, nwt, ps], [rn])
                if fc == 0:
                    ts(k, "dve", dstT[:, tb * TB:(tb + 1) * TB], rn[:, :], 128.0 ** -0.5, ALU.mult, [rn], [dstT])
                else:
                    cp(k, "dve", dstT[:, tb * TB:(tb + 1) * TB], rn[:, :], [rn], [dstT])
            for tl in range(3):
                i = 3 * tb + tl
                ps = mmA if tl % 2 == 0 else mmB
                for cc in range(NC_):
                    mm(k, ps[:, 0:256], hb[:, cc, tl * 128:(tl + 1) * 128], w_sb[:, cc, 256:512], cc == 0,
                       cc == NC_ - 1, [w_sb, hb], [ps])
                cp(k, "dve", vtok[:, i, :], ps[:, 0:128], [ps], [vtok])
                act(k, gs[:, i, :], ps[:, 128:256], AF.Silu, [ps], [gs])
        for tq in range(nt):
            k.op("dve", lambda en: en.memset(car[:, 0:1], 0.0), [], [car])
            blocks = []
            t1 = tq + 1
            while t1 > 0:
                t0 = max(0, t1 - 4)
                blocks.append((t0, t1))
                t1 = t0
            nmm = tq + 1
            mi = 0
            for (t0, t1) in blocks:
                n = t1 - t0
                W = n * 128
                diag = (t1 - 1 == tq)
                mm(k, ZP[:, 0:W], qT[:, tq * 128:(tq + 1) * 128], kT[:, t0 * 128:t1 * 128], True, True,
                   [qT, kT], [ZP])
                act(k, E1[:, 0:W], ZP[:, 0:W], AF.Exp, [ZP], [E1])
                act(k, SP[:, 0:W], E1[:, 0:W], AF.Ln, [E1], [SP], bias=1.0)
                if diag:
                    tt(k, "pool", SP[:, W - 128:W], SP[:, W - 128:W], c["Ls"](), ALU.mult, [SP, cf], [SP])
                k.op("dve", lambda en: en.tensor_tensor_scan(out=PRE[:, 0:W], data0=ones512[:, 0:W],
                                                             data1=SP[:, 0:W], initial=0.0, op0=ALU.mult,
                                                             op1=ALU.add), [ones512, SP], [PRE])
                tt(k, "dve", ARG[:, 0:W], ZP[:, 0:W], SP[:, 0:W], ALU.subtract, [ZP, SP], [ARG])
                tt(k, "pool", ARG[:, 0:W], ARG[:, 0:W], PRE[:, 0:W], ALU.add, [ARG, PRE], [ARG])
                tt(k, "dve", car[:, 1:2], car[:, 0:1], PRE[:, W - 1:W], ALU.add, [car, PRE], [car])
                ts(k, "dve", car[:, 2:3], car[:, 1:2], -1.0, ALU.mult, [car], [car])
                act(k, Aw[:, 0:W], ARG[:, 0:W], AF.Exp, [ARG, car], [Aw], bias=car[:, 2:3])
                cp(k, "dve", car[:, 0:1], car[:, 1:2], [car], [car])
                if diag:
                    tt(k, "pool", Aw[:, W - 128:W], Aw[:, W - 128:W], c["b"][:, 384:512], ALU.mult, [Aw, cb], [Aw])
                for s_ in range(n):
                    tr(k, TRb[:, s_ * 128:(s_ + 1) * 128], Aw[:, s_ * 128:(s_ + 1) * 128], c["Ib"](), [Aw, cb], [TR])
                cp(k, "act", AT[:, 0:W], TRb[:, 0:W], [TR], [AT])
                for s_ in range(n):
                    mm(k, OO[:, 0:128], AT[:, s_ * 128:(s_ + 1) * 128], vtok[:, t0 + s_, :], mi == 0, mi == nmm - 1,
                       [AT, vtok], [OO])
                    mi += 1
            tt(k, "dve", og[:, :], OO[:, 0:128], gs[:, tq, :], ALU.mult, [OO, gs], [og])
            tr(k, TRb[:, 0:128], og[:, :], c["Ib"](), [og, cb], [TR])
            cp(k, "act", oT[:, tq * 128:(tq + 1) * 128], TRb[:, 0:128], [TR], [oT])
        if out_cb is not None:
            out_cb(j, oT)
        else:
            k.dma("sp", o1T[:, :, j, :].rearrange("t p x -> p t x"), v3(oT[:, :], TB), [oT], [o1T])


def _new(with_consts=True):
    nc = bass.Bass("TRN2", target_bir_lowering=False)
    k = K(nc)
    cd = k.dram("consts", [128, 640], F32, kind="ExternalInput") if with_consts else None
    return nc, k, cd


def build_l1(nb=NB, nh=8):
    nc, k, cd = _new()
    h0 = k.dram("h0", [nb * TB, D], F32, kind="ExternalInput")
    nw = k.dram("nw", [128, D], F32, kind="ExternalInput")
    win = k.dram("win", [nh, D, DNW], F32, kind="ExternalInput")
    cw = k.dram("cw", [128, nh, 4, 4], F32, kind="ExternalInput")
    hp = k.dram("hp", [128, nh, 4], F32, kind="ExternalInput")
    onw = k.dram("onw", [128, 256], F32, kind="ExternalInput")
    hnT = k.dram("hnT", [nb, 128, NC_, TB], BF16, kind="Internal")
    o0T = k.dram("o0T", [nb, 128, nh * 2, TB], BF16, kind="ExternalOutput")
    k.phase_begin()
    c = load_consts(k, cd, "a")
    phase_norm_T(k, c, h0, nw, hnT, nb, "n")
    k.phase_end()
    k.phase_begin()
    c = load_consts(k, cd, "b")
    def dn_out(j, tb, stg):
        k.dma("sp", o0T[tb, :, 2 * j:2 * j + 2, :], stg[:, :, :], [stg], [o0T])
    phase_dn2(k, c, hnT, win, cw, hp, onw, nb, nh, "d", dn_out)
    k.phase_end()
    k.finish()
    return nc


def build_outproj(kc_each, final, nb=NB):
    nc, k, cd = _new(False)
    xTs = [k.dram("xT%d" % g, [nb, 128, kc_each, TB], BF16, kind="ExternalInput") for g in range(4)]
    wout = k.dram("wout", [4 * kc_each * 128, 1024], F32, kind="ExternalInput")
    hres = k.dram("hres", [nb * TB, 1024], F32, kind="ExternalInput")
    P = PsumSet(k, "p")
    if not final:
        out = k.dram("h1", [nb * TB, 1024], F32, kind="ExternalOutput")
        ssq = k.dram("ssq", [128, nb * 3], F32, kind="ExternalOutput")
        phase_outproj(k, P, xTs, [kc_each] * 4, wout, hres, out, nb, "o", ssq_out=ssq)
    else:
        out = k.dram("out", [SEQ, 1024], F32, kind="ExternalOutput")

        def rows(i):
            lo = max(i * 128, NMETA)
            hi = min((i + 1) * 128, NMETA + SEQ)
            if hi <= lo:
                return []
            return [(slice(lo - NMETA, hi - NMETA), slice(lo - i * 128, hi - i * 128))]
        phase_outproj(k, P, xTs, [kc_each] * 4, wout, hres, out, nb, "o", out_rows=rows)
    k.finish()
    return nc


def build_l3(nb=NB):
    nc, k, cd = _new()
    ssq_all = k.dram("ssq_all", [4, 128, nb * 3], F32, kind="ExternalInput")
    h1 = k.dram("h1", [nb * TB, 1024], F32, kind="ExternalInput")
    nw = k.dram("nw", [128, 1024], F32, kind="ExternalInput")
    dst = k.dram("hn1T", [nb, 128, 8, TB], BF16, kind="ExternalOutput")
    c = load_consts(k, cd)
    P = PsumSet(k, "p")
    phase_norm2(k, c, P, ssq_all, h1, nw, dst, nb, "m")
    k.finish()
    return nc


def build_l4(nb=NB, nh=8):
    nc, k, cd = _new()
    hnTs = [k.dram("hnT%d" % g, [nb, 128, 8, TB], BF16, kind="ExternalInput") for g in range(4)]
    win = k.dram("win", [nh, D, 512], F32, kind="ExternalInput")
    nwqk = k.dram("nwqk", [128, 2], F32, kind="ExternalInput")
    o1T = k.dram("o1T", [nb, 128, nh, TB], BF16, kind="ExternalOutput")
    c = load_consts(k, cd)
    def load_h(tb, hb):
        for g in range(4):
            k.dma("sp", hb[:, g * 8:(g + 1) * 8, :], hnTs[g][tb], [hnTs[g]], [hb])

    def sb_out(j, oT):
        k.dma("sp", o1T[:, :, j, :].rearrange("t p x -> p t x"), v3(oT[:, :], TB), [oT], [o1T])
    phase_sb2(k, c, win, nwqk, nb, nh, "s", load_h, sb_out)
    k.finish()
    return nc


def run_rr(gens, width, bg=None):
    pending = list(gens)
    active = []
    while pending or active:
        while pending and len(active) < width:
            active.append(pending.pop(0)())
        for g in list(active):
            try:
                next(g)
            except StopIteration:
                active.remove(g)
        if bg is not None:
            try:
                next(bg)
            except StopIteration:
                bg = None
    if bg is not None:
        for _ in bg:
            pass


def phase_sb2(k, c, win, nwqk, nb, nheads, pfx, load_cb, out_cb, NS=3):
    nt = nb * 3
    TT = nt * 128
    banks = [k.ps(pfx + "bk%d" % i, [128, 512], F32) for i in range(8)]
    mmA, mmB = banks[0:2]
    cf, cb = c["f"], c["b"]
    hnTs = None
    w_sb = k.sb(pfx + "w", [128, NC_, 512], BF16)
    hbk = [k.sb(pfx + "hb%d" % i, [128, NC_, TB], BF16) for i in range(2)]
    arrs = [(k.sb(pfx + "qT%d" % i, [128, TT], BF16), k.sb(pfx + "kT%d" % i, [128, TT], BF16),
             k.sb(pfx + "vtok%d" % i, [128, nt, 128], BF16), k.sb(pfx + "gs%d" % i, [128, nt, 128], BF16))
            for i in range(2)]
    oT = k.sb(pfx + "oT", [128, TT], BF16)
    nwt = k.sb(pfx + "nw", [128, 2], F32)
    sqb = k.sb(pfx + "sq", [128, TB], BF16)
    sb_ = k.sb(pfx + "s", [128, TB], F32)
    rn = k.sb(pfx + "rn", [128, TB], F32)
    ones512 = k.sb(pfx + "ones", [128, 512], F32)

    class Set:
        pass
    sets = []
    for i in range(NS):
        S_ = Set()
        S_.ZP, S_.TA = banks[2 + 2 * i], banks[3 + 2 * i]
        S_.E1 = k.sb(pfx + "E1_%d" % i, [128, 512], F32)
        S_.SP = k.sb(pfx + "SP_%d" % i, [128, 512], F32)
        S_.PRE = k.sb(pfx + "PRE_%d" % i, [128, 512], F32)
        S_.Aw = k.sb(pfx + "Aw_%d" % i, [128, 512], BF16)
        S_.AT = k.sb(pfx + "AT_%d" % i, [128, 512], BF16)
        S_.car = k.sb(pfx + "car_%d" % i, [128, 4], F32)
        S_.Oa = k.sb(pfx + "Oa_%d" % i, [128, 128], F32)
        S_.og = k.sb(pfx + "og_%d" % i, [128, 128], BF16)
        sets.append(S_)
    def load_w(jj):
        for q4 in range(4):
            k.dma("pool", w_sb[:, q4 * 8:(q4 + 1) * 8, :],
                  win[jj, q4 * 1024:(q4 + 1) * 1024, :].rearrange("(c p) n -> p c n", p=128), [win], [w_sb])

    bi = [0]

    def inproj(j):
        qT, kT, vtok, gs = arrs[j % 2]
        for tb in range(nb):
            hb = hbk[bi[0] % 2]
            bi[0] += 1
            load_cb(tb, hb)
            for fc in range(2):
                ps, DD = (mmA, mmB) if fc == 0 else (mmB, mmA)
                for cc in range(NC_):
                    mm(k, ps[:, 0:TB], w_sb[:, cc, fc * 128:(fc + 1) * 128], hb[:, cc, :], cc == 0, cc == NC_ - 1,
                       [w_sb, hb], [ps])
                yield
                act(k, sqb[:, :], ps[:, 0:TB], AF.Square, [ps], [sqb])
                yield
                mm(k, DD[:, 0:TB], c["onesb"](), sqb[:, :], True, True, [cb, sqb], [DD])
                yield
                act(k, sb_[:, :], DD[:, 0:TB], AF.Sqrt, [DD], [sb_], bias=EPS, scale=1.0 / 128)
                yield
                k.op("dve", lambda en: en.reciprocal(out=rn[:, :], in_=sb_[:, :]), [sb_], [rn])
                dstT = qT if fc == 0 else kT
                stt(k, rn[:, :], rn[:, :], nwt[:, fc:fc + 1], ps[:, 0:TB], ALU.mult, ALU.mult, [rn, nwt, ps], [rn])
                if fc == 0:
                    ts(k, "dve", dstT[:, tb * TB:(tb + 1) * TB], rn[:, :], 128.0 ** -0.5, ALU.mult, [rn], [dstT])
                else:
                    cp(k, "dve", dstT[:, tb * TB:(tb + 1) * TB], rn[:, :], [rn], [dstT])
                yield
            for tl in range(3):
                i = 3 * tb + tl
                ps = mmA if tl % 2 == 0 else mmB
                for cc in range(NC_):
                    mm(k, ps[:, 0:256], hb[:, cc, tl * 128:(tl + 1) * 128], w_sb[:, cc, 256:512], cc == 0,
                       cc == NC_ - 1, [w_sb, hb], [ps])
                yield
                cp(k, "dve", vtok[:, i, :], ps[:, 0:128], [ps], [vtok])
                act(k, gs[:, i, :], ps[:, 128:256], AF.Silu, [ps], [gs])
                yield
        if j + 1 < nheads:
            load_w(j + 1)

    k.dma("sp", nwt[:, :], nwqk[:, :], [nwqk], [nwt])
    k.op("dve", lambda en: en.memset(ones512[:, :], 1.0), [], [ones512])
    load_w(0)
    for _ in inproj(0):
        pass
    for j in range(nheads):
        qT, kT, vtok, gs = arrs[j % 2]
        def chain(tq, j=j):
            st_ = sets[tq % NS]
            ZP, TA = st_.ZP, st_.TA
            TRb = TA[:, 0:256].bitcast(BF16)
            AV = TA[:, 256:384]
            E1, SP, PRE, ARG, Aw, AT, car, Oa = st_.E1, st_.SP, st_.PRE, st_.E1, st_.Aw, st_.AT, st_.car, st_.Oa
            k.op("dve", lambda en: en.memset(car[:, 0:1], 0.0), [], [car])
            t1 = tq + 1
            first = True
            while t1 > 0:
                t0 = max(0, t1 - 4)
                n = t1 - t0
                W = n * 128
                diag = (t1 - 1 == tq)
                mm(k, ZP[:, 0:W], qT[:, tq * 128:(tq + 1) * 128], kT[:, t0 * 128:t1 * 128], True, True,
                   [qT, kT], [ZP])
                yield
                act(k, E1[:, 0:W], ZP[:, 0:W], AF.Exp, [ZP], [E1])
                yield
                act(k, SP[:, 0:W], E1[:, 0:W], AF.Ln, [E1], [SP], bias=1.0)
                yield
                if diag:
                    tt(k, "pool", SP[:, W - 128:W], SP[:, W - 128:W], c["Ls"](), ALU.mult, [SP, cf], [SP])
                    yield
                k.op("dve", lambda en: en.tensor_tensor_scan(out=PRE[:, 0:W], data0=ones512[:, 0:W],
                                                             data1=SP[:, 0:W], initial=0.0, op0=ALU.mult,
                                                             op1=ALU.add), [ones512, SP], [PRE])
                tt(k, "dve", ARG[:, 0:W], ZP[:, 0:W], SP[:, 0:W], ALU.subtract, [ZP, SP], [ARG])
                yield
                tt(k, "pool", ARG[:, 0:W], ARG[:, 0:W], PRE[:, 0:W], ALU.add, [ARG, PRE], [ARG])
                tt(k, "dve", car[:, 1:2], car[:, 0:1], PRE[:, W - 1:W], ALU.add, [car, PRE], [car])
                ts(k, "dve", car[:, 2:3], car[:, 1:2], -1.0, ALU.mult, [car], [car])
                yield
                act(k, Aw[:, 0:W], ARG[:, 0:W], AF.Exp, [ARG, car], [Aw], bias=car[:, 2:3])
                yield
                cp(k, "dve", car[:, 0:1], car[:, 1:2], [car], [car])
                if diag:
                    tt(k, "pool", Aw[:, W - 128:W], Aw[:, W - 128:W], c["b"][:, 384:512], ALU.mult, [Aw, cb], [Aw])
                    yield
                for s_ in range(n):
                    tr(k, TRb[:, s_ * 128:(s_ + 1) * 128], Aw[:, s_ * 128:(s_ + 1) * 128], c["Ib"](), [Aw, cb], [TA])
                yield
                cp(k, "dve", AT[:, 0:W], TRb[:, 0:W], [TA], [AT])
                yield
                for s_ in range(n):
                    mm(k, AV, AT[:, s_ * 128:(s_ + 1) * 128], vtok[:, t0 + s_, :], s_ == 0, s_ == n - 1,
                       [AT, vtok], [TA])
                yield
                if first:
                    cp(k, "dve", Oa[:, :], AV, [TA], [Oa])
                else:
                    tt(k, "dve", Oa[:, :], Oa[:, :], AV, ALU.add, [Oa, TA], [Oa])
                first = False
                t1 = t0
                yield
            tt(k, "pool", st_.og[:, :], Oa[:, :], gs[:, tq, :], ALU.mult, [Oa, gs], [st_.og])
            yield
            tr(k, TRb[:, 0:128], st_.og[:, :], c["Ib"](), [st_.og, cb], [TA])
            yield
            cp(k, "act", oT[:, tq * 128:(tq + 1) * 128], TRb[:, 0:128], [TA], [oT])
            yield

        bg = inproj(j + 1) if j + 1 < nheads else None
        run_rr([(lambda tq=tq: chain(tq)) for tq in range(nt if not DBG_SKIP else 0)], NS, bg)
        out_cb(j, oT)


PCS1 = [(0, 4), (4, 8), (8, 11)]
PCS3 = [(0, 6), (6, 11)]


def _piece_of(pcs, tb):
    for i, (a, b) in enumerate(pcs):
        if a <= tb < b:
            return i, tb - a
    raise ValueError(tb)


def build_fused(nb=NB, nh=8):
    nc, k, cd = _new()
    nt = nb * 3
    h0 = k.dram("h0", [nb * TB, D], F32, kind="ExternalInput")
    h0c = k.dram("h0c", [nb * TB, 1024], F32, kind="ExternalInput")
    nw0 = k.dram("nw0", [128, D], F32, kind="ExternalInput")
    win0 = k.dram("win0", [nh, D, DNW], F32, kind="ExternalInput")
    cw = k.dram("cw", [128, nh, 4, 4], F32, kind="ExternalInput")
    hp = k.dram("hp", [128, nh, 4], F32, kind="ExternalInput")
    onw = k.dram("onw", [128, 256], F32, kind="ExternalInput")
    wout0 = k.dram("wout0", [nh * 4 * 256, 1024], F32, kind="ExternalInput")
    nw1 = k.dram("nw1", [128, 1024], F32, kind="ExternalInput")
    win1 = k.dram("win1", [nh, D, 512], F32, kind="ExternalInput")
    nwqk = k.dram("nwqk", [128, 2], F32, kind="ExternalInput")
    wout1 = k.dram("wout1", [nh * 4 * 128, 1024], F32, kind="ExternalInput")
    out = k.dram("out", [SEQ, 1024], F32, kind="ExternalOutput")
    hnT = k.dram("hnT", [nb, 128, NC_, TB], BF16)
    h1 = k.dram("h1", [nb * TB, 1024], F32)
    pcs1 = [(a, min(b, nb)) for (a, b) in PCS1 if a < nb]
    pcs3 = [(a, min(b, nb)) for (a, b) in PCS3 if a < nb]
    A1, G1, A2, G2, A3, G3 = [Buf(k, None, n) for n in ("A1", "G1", "A2", "G2", "A3", "G3")]
    a1 = [[nc.dram_tensor("a1_%d_%d" % (j, i), [(b - a) * 128, 768], BF16, kind="Internal").ap()
           for i, (a, b) in enumerate(pcs1)] for j in range(nh)]
    g1 = [[nc.dram_tensor("g1_%d_%d" % (j, i), [4 * (b - a) * 128, 768], BF16, kind="Internal").ap()
           for i, (a, b) in enumerate(pcs1)] for j in range(nh)]
    a2 = [nc.dram_tensor("a2_%d" % t, [128, 8 * TB], BF16, kind="Internal").ap() for t in range(nb)]
    g2 = [nc.dram_tensor("g2_%d" % t, [4 * 128, 8 * TB], BF16, kind="Internal").ap() for t in range(nb)]
    a3 = [[nc.dram_tensor("a3_%d_%d" % (j, i), [(b - a) * 128, TB], BF16, kind="Internal").ap()
           for i, (a, b) in enumerate(pcs3)] for j in range(nh)]
    g3 = [[nc.dram_tensor("g3_%d_%d" % (j, i), [4 * (b - a) * 128, TB], BF16, kind="Internal").ap()
           for i, (a, b) in enumerate(pcs3)] for j in range(nh)]
    ssq_in = k.dram("ssq_in", [128, nt], F32)
    ssq_all = k.dram("ssq_all", [4 * 128, nt], F32)

    wb0 = k.dram("wb0", [nh * 4 * 256, 1024], BF16)
    wb1 = k.dram("wb1", [nh * 4 * 128, 1024], BF16)
    def precast(j):
        for (src, dst) in ((wout0, wb0), (wout1, wb1)):
            per = src.t.shape[0] // nh
            r0 = j * per
            for q in range(r0, r0 + per, 1024):
                q1 = min(r0 + per, q + 1024)
                k.dma("pool", dst[q:q1, :], src[q:q1, :], [src], [dst])

    k.phase_begin()
    c = load_consts(k, cd, "a")
    phase_norm_T(k, c, h0, nw0, hnT, nb, "n")
    k.phase_end()

    def dn_out(j, tb, stg):
        i, tl = _piece_of(pcs1, tb)
        dst = a1[j][i].rearrange("(t p) (h x) -> p t h x", p=128, h=2)
        k.dma("sp", dst[:, tl, :, :], stg[:, :, :], [stg], [A1])
        if tb == pcs1[i][1] - 1:
            k.cc(a1[j][i][:, :], g1[j][i][:, :], [A1], [G1], G1)

    k.phase_begin()
    c = load_consts(k, cd, "b")
    phase_dn2(k, c, hnT, win0, cw, hp, onw, nb, nh, "d", dn_out, bg_cb=precast)
    k.phase_end()

    def load_x2(tb, xbb):
        i, tl = _piece_of(pcs1, tb)
        dstv = xbb[:, :, :].rearrange("p (r j h) x -> p r j h x", r=4, h=2)
        for j in range(nh):
            src = g1[j][i].rearrange("(r t p) (h x) -> p r t h x", r=4, p=128, h=2)
            k.dma("sp", dstv[:, :, j, :, :], src[:, :, tl, :, :], [G1], [xbb])

    k.phase_begin()
    P = PsumSet(k, "p2")
    phase_outproj(k, P, None, [nh * 2] * 4, wb0, h0c, h1, nb, "o", ssq_out=ssq_in, load_cb=load_x2, wq="sp")
    k.cc(ssq_in[:, :], ssq_all[:, :], [ssq_in], [ssq_all], ssq_all)
    k.phase_end()

    G2p = [Buf(k, None, "G2p%d" % t) for t in range(nb)]

    def n2_out(blk, stg):
        k.dma("sp", a2[blk].rearrange("p (c x) -> p c x", c=8), stg[:, :, :], [stg], [A2])
        k.cc(a2[blk][:, :], g2[blk][:, :], [A2], [G2], G2)

    k.phase_begin()
    c = load_consts(k, cd, "c")
    P = PsumSet(k, "p3")
    phase_norm2(k, c, P, ssq_all, h1, nw1, None, nb, "m",
                ssq_view=ssq_all[:, :].rearrange("(g p) n -> p g n", p=128), out_cb=n2_out)
    k.phase_end()

    def load_h3(tb, hb):
        k.dma("sp", hb[:, :, :].rearrange("p (r c) x -> p r c x", r=4),
              g2[tb].rearrange("(r p) (c x) -> p r c x", p=128, c=8), [G2], [hb])

    def sb_out(j, oT):
        for i, (a, b) in enumerate(pcs3):
            k.dma("sp", a3[j][i].rearrange("(t p) x -> p t x", p=128), v3(oT[:, a * TB:b * TB], TB), [oT], [A3])
        for i in range(len(pcs3)):
            k.cc(a3[j][i][:, :], g3[j][i][:, :], [A3], [G3], G3)

    k.phase_begin()
    c = load_consts(k, cd, "d")
    phase_sb2(k, c, win1, nwqk, nb, nh, "s", load_h3, sb_out)
    k.phase_end()

    def load_x4(tb, xbb):
        i, tl = _piece_of(pcs3, tb)
        dstv = xbb[:, :, :].rearrange("p (r j) x -> p r j x", r=4)
        for j in range(nh):
            src = g3[j][i].rearrange("(r t p) x -> p r t x", r=4, p=128)
            k.dma("sp", dstv[:, :, j, :], src[:, :, tl, :], [G3], [xbb])

    def rows(i):
        lo = max(i * 128, NMETA)
        hi = min((i + 1) * 128, NMETA + SEQ)
        if hi <= lo:
            return []
        return [(slice(lo - NMETA, hi - NMETA), slice(lo - i * 128, hi - i * 128))]

    k.phase_begin()
    P = PsumSet(k, "p5")
    phase_outproj(k, P, None, [nh] * 4, wb1, h1, out, nb, "q", out_rows=rows if nb == NB else None,
                  load_cb=load_x4, wq="sp")
    k.phase_end()
    k.finish()
    return nc


_DBG = None
FUSED = True


def _run(nc, in_maps):
    res = run_bass_kernel_spmd(nc, in_maps, core_ids=list(range(8)))
    return res.results


def _rep(v, n=128):
    return np.ascontiguousarray(np.tile(np.asarray(v, np.float32)[None], (n, 1)))


def kernel(x, meta_tokens, dn_norm_w, dn_w_in, dn_conv_w, dn_a_log, dn_dt_bias, dn_out_norm_w, dn_w_out,
           sb_norm_w, sb_w_in, sb_q_norm_w, sb_k_norm_w, sb_w_out):
    f32 = np.float32
    x = np.asarray(x, f32)
    cst = consts_np()
    h0p = []
    for b in range(2):
        h = np.zeros((T, D), f32)
        h[:NMETA] = np.asarray(meta_tokens, f32)
        h[NMETA:NMETA + SEQ] = x[b]
        h0p.append(h)
    w_in = np.asarray(dn_w_in, f32)[0]
    convw = np.asarray(dn_conv_w, f32)[0]
    a_log = np.asarray(dn_a_log, f32)[0]
    dtb = np.asarray(dn_dt_bias, f32)[0]
    onw = np.asarray(dn_out_norm_w, f32)[0]
    KQ, KV = 4096, 8192
    dn_win, dn_cw, dn_hp = [], [], []
    for g in range(4):
        ws, cws, hps = [], [], []
        for j in range(8):
            J = 8 * g + j
            cols = np.concatenate([np.arange(J * 128, (J + 1) * 128), KQ + np.arange(J * 128, (J + 1) * 128),
                                   2 * KQ + np.arange(2 * J * 128, (2 * J + 2) * 128),
                                   2 * KQ + KV + np.arange(2 * J * 128, (2 * J + 2) * 128),
                                   2 * KQ + 2 * KV + np.arange(2 * J, 2 * J + 2),
                                   2 * KQ + 2 * KV + 64 + np.arange(2 * J, 2 * J + 2)])
            ws.append(w_in[:, cols])
            cws.append(convw[:, cols[:512]].reshape(4, 4, 128).transpose(2, 1, 0))
            hps.append(np.concatenate([dtb[2 * J:2 * J + 2], a_log[2 * J:2 * J + 2]]))
        dn_win.append(np.ascontiguousarray(np.stack(ws)))
        dn_cw.append(np.ascontiguousarray(np.stack(cws, axis=1)))
        dn_hp.append(np.ascontiguousarray(np.tile(np.stack(hps)[None], (128, 1, 1))))
    dn_nw_r = _rep(np.asarray(dn_norm_w, f32)[0])
    onw_r = _rep(np.concatenate([onw, onw]))
    w_out0 = np.asarray(dn_w_out, f32)[0]
    sbw = np.asarray(sb_w_in, f32)[0]
    sb_win = []
    for g in range(4):
        ws = []
        for j in range(8):
            H = 8 * g + j
            cols = np.concatenate([q * 4096 + np.arange(H * 128, (H + 1) * 128) for q in range(4)])
            ws.append(sbw[:, cols])
        sb_win.append(np.ascontiguousarray(np.stack(ws)))
    sb_nw = np.asarray(sb_norm_w, f32)[0]
    nwqk = np.ascontiguousarray(np.stack([np.asarray(sb_q_norm_w, f32)[0], np.asarray(sb_k_norm_w, f32)[0]], 1))
    w_out1 = np.asarray(sb_w_out, f32)[0]

    cores = [(c // 4, c % 4) for c in range(8)]
    if FUSED:
        rf = _run(build_fused(), [{
            "consts": cst, "h0": h0p[b], "h0c": np.ascontiguousarray(h0p[b][:, g * 1024:(g + 1) * 1024]),
            "nw0": dn_nw_r, "win0": dn_win[g], "cw": dn_cw[g], "hp": dn_hp[g], "onw": onw_r,
            "wout0": np.ascontiguousarray(w_out0[:, g * 1024:(g + 1) * 1024]),
            "nw1": _rep(sb_nw[g * 1024:(g + 1) * 1024]), "win1": sb_win[g], "nwqk": nwqk,
            "wout1": np.ascontiguousarray(w_out1[:, g * 1024:(g + 1) * 1024])} for (b, g) in cores])
        out = np.empty((2, SEQ, D), f32)
        for ci, (b, g) in enumerate(cores):
            out[b][:, g * 1024:(g + 1) * 1024] = np.asarray(rf[ci]["out"])
        return out
    r1 = _run(build_l1(), [{"consts": cst, "h0": h0p[b], "nw": dn_nw_r, "win": dn_win[g], "cw": dn_cw[g],
                            "hp": dn_hp[g], "onw": onw_r} for (b, g) in cores])
    o0T = [np.asarray(r["o0T"]) for r in r1]
    r2 = _run(build_outproj(16, False),
              [dict({"wout": np.ascontiguousarray(w_out0[:, g * 1024:(g + 1) * 1024]),
                     "hres": np.ascontiguousarray(h0p[b][:, g * 1024:(g + 1) * 1024])},
                    **{"xT%d" % gg: o0T[4 * b + gg] for gg in range(4)}) for (b, g) in cores])
    h1 = [np.asarray(r["h1"]) for r in r2]
    ssq = [np.asarray(r["ssq"]) for r in r2]
    r3 = _run(build_l3(), [{"consts": cst, "ssq_all": np.ascontiguousarray(np.stack(ssq[4 * b:4 * b + 4])),
                            "h1": h1[4 * b + g], "nw": _rep(sb_nw[g * 1024:(g + 1) * 1024])} for (b, g) in cores])
    hn1T = [np.asarray(r["hn1T"]) for r in r3]
    r4 = _run(build_l4(), [dict({"consts": cst, "win": sb_win[g], "nwqk": nwqk},
                                **{"hnT%d" % gg: hn1T[4 * b + gg] for gg in range(4)}) for (b, g) in cores])
    o1T = [np.asarray(r["o1T"]) for r in r4]
    r5 = _run(build_outproj(8, True),
              [dict({"wout": np.ascontiguousarray(w_out1[:, g * 1024:(g + 1) * 1024]),
                     "hres": h1[4 * b + g]},
                    **{"xT%d" % gg: o1T[4 * b + gg] for gg in range(4)}) for (b, g) in cores])
    if _DBG is not None:
        _DBG.update(o0T=o0T, h1=h1, ssq=ssq, hn1T=hn1T, o1T=o1T)
    out = np.empty((2, SEQ, D), f32)
    for ci, (b, g) in enumerate(cores):
        out[b][:, g * 1024:(g + 1) * 1024] = np.asarray(r5[ci]["out"])
    return out
```

```python
import contextlib
import numpy as np
import ml_dtypes
import concourse.bass as bass
import concourse.mybir as mybir
from concourse.bass_utils import run_bass_kernel_spmd

F32 = mybir.dt.float32
BF16 = mybir.dt.bfloat16
AF = mybir.ActivationFunctionType
ALU = mybir.AluOpType

D = 4096
NC_ = 32
SEQ = 4096
NMETA = 16
T = 4224
NT = 33
TB = 384
NB = 11
EPS = 1e-6
RGROUPS = [[0, 1, 2, 3], [4, 5, 6, 7]]
DNW = 772


class Tok:
    __slots__ = ("sem", "val")

    def __init__(self, sem, val):
        self.sem = sem
        self.val = val


class Buf:
    def __init__(self, k, t, name):
        self.k = k
        self.t = t
        self.name = name
        self.w = None
        self.r = {}
        self.dsem = None
        self.dcnt = 0
        self.root = self
        self.excl = False

    def sub(self, ap, name):
        b = Buf(self.k, ap, name)
        b.root = self.root
        return b

    def __getitem__(self, idx):
        return self.t[idx]


class K:
    def __init__(self, nc):
        self.nc = nc
        self.es = contextlib.ExitStack()
        self.eng = {"pe": nc.tensor, "act": nc.scalar, "dve": nc.vector,
                    "pool": nc.gpsimd, "sp": nc.sync}
        self.esem = {}
        self.ecnt = {}
        self.seen = {}
        for e in self.eng:
            self.esem[e] = self.es.enter_context(nc.semaphore("es_" + e))
            self.ecnt[e] = 0
            self.seen[e] = {}
        self.dbufs = []
        self.n = 0
        self.pst = None

    def phase_begin(self):
        self.pst = contextlib.ExitStack()

    def phase_end(self):
        self.barrier()
        self.pst.close()
        self.pst = None

    def sb(self, name, shape, dt):
        st = self.pst if self.pst is not None else self.es
        t = st.enter_context(self.nc.sbuf_tensor(name, list(shape), dt))
        return Buf(self, t, name)

    def ps(self, name, shape, dt):
        st = self.pst if self.pst is not None else self.es
        t = st.enter_context(self.nc.psum_tensor(name, list(shape), dt))
        b = Buf(self, t, name)
        b.excl = True
        return b

    def dram(self, name, shape, dt, kind="Internal"):
        t = self.nc.dram_tensor(name, list(shape), dt, kind=kind)
        return Buf(self, t.ap(), name)

    def _wait(self, e, tok):
        if tok is None:
            return
        if e == "pe" and tok.sem is self.esem["pe"]:
            return
        key = id(tok.sem)
        if self.seen[e].get(key, 0) >= tok.val:
            return
        self.seen[e][key] = tok.val
        self.eng[e].wait_ge(tok.sem, tok.val)

    def _deps(self, e, reads, writes):
        for b in reads:
            self._wait(e, b.w)
        for b in writes:
            self._wait(e, b.w)
            for t in b.r.values():
                self._wait(e, t)

    def _mark(self, tok, reads, writes):
        for b in reads:
            b.r[id(tok.sem)] = tok
        for b in writes:
            b.w = tok
            b.r = {}

    def _norm(self, reads, writes):
        rs = [b.root for b in reads]
        ws = [b.root for b in writes]
        ws = ws + [b for b in rs if b.excl]
        rs = [b for b in rs if not b.excl]
        return rs, ws

    def op(self, e, fn, reads, writes):
        reads, writes = self._norm(reads, writes)
        self._deps(e, reads, writes)
        ins = fn(self.eng[e])
        self.ecnt[e] += 1
        ins.then_inc(self.esem[e], 1)
        self._mark(Tok(self.esem[e], self.ecnt[e]), reads, writes)
        self.n += 1

    def dma(self, q, out_ap, in_ap, reads, writes, owner=None):
        ow = (owner if owner is not None else writes[0]).root
        reads, writes = self._norm(reads, writes)
        if ow.dsem is None:
            ow.dsem = self.es.enter_context(self.nc.semaphore("ds_" + ow.name))
            self.dbufs.append(ow)
        self._deps(q, reads, writes)
        ow.dcnt += 16
        self.eng[q].dma_start(out=out_ap, in_=in_ap).then_inc(ow.dsem, 16)
        self._mark(Tok(ow.dsem, ow.dcnt), reads, writes)
        self.n += 1

    def cc(self, in_ap, out_ap, reads, writes, owner):
        ow = owner.root
        reads, writes = self._norm(reads, writes)
        if ow.dsem is None:
            ow.dsem = self.es.enter_context(self.nc.semaphore("cs_" + ow.name))
            self.dbufs.append(ow)
        for b in reads:
            self._wait("pool", b.w)
        for b in writes:
            for t in b.r.values():
                self._wait("pool", t)
        ow.dcnt += 1
        self.eng["pool"].collective_compute("AllGather", ALU.bypass, replica_groups=RGROUPS,
                                            ins=[in_ap], outs=[out_ap]).then_inc(ow.dsem)
        self._mark(Tok(ow.dsem, ow.dcnt), reads, writes)
        self.n += 1

    def barrier(self):
        for e in self.eng:
            for e2 in self.eng:
                if e2 != e and self.ecnt[e2] > 0:
                    self._wait(e, Tok(self.esem[e2], self.ecnt[e2]))
            for b in self.dbufs:
                if b.dcnt:
                    self._wait(e, Tok(b.dsem, b.dcnt))

    def finish(self):
        self.barrier()
        self.es.close()


def act(k, out, in_, func, reads, writes, bias=None, scale=None, accum=None, e="act"):
    kw = {}
    if bias is not None:
        kw["bias"] = bias
    if scale is not None:
        kw["scale"] = scale
    if accum is not None:
        kw["accum_out"] = accum
    k.op(e, lambda en: en.activation(out=out, in_=in_, func=func, **kw), reads, writes)


def mm(k, out, lhsT, rhs, start, stop, reads, writes):
    k.op("pe", lambda en: en.matmul(out, lhsT, rhs, start=start, stop=stop), reads, writes)


def tr(k, out, in_, ident, reads, writes):
    k.op("pe", lambda en: en.transpose(out, in_, ident), reads, writes)


def tt(k, e, out, in0, in1, op, reads, writes):
    k.op(e, lambda en: en.tensor_tensor(out=out, in0=in0, in1=in1, op=op), reads, writes)


def ts(k, e, out, in0, s1, op0, reads, writes, s2=None, op1=None):
    if op1 is None:
        k.op(e, lambda en: en.tensor_scalar(out=out, in0=in0, scalar1=s1, scalar2=None, op0=op0),
             reads, writes)
    else:
        k.op(e, lambda en: en.tensor_scalar(out=out, in0=in0, scalar1=s1, scalar2=s2, op0=op0,
                                            op1=op1), reads, writes)


def stt(k, out, in0, scalar, in1, op0, op1, reads, writes):
    k.op("dve", lambda en: en.scalar_tensor_tensor(out=out, in0=in0, scalar=scalar, in1=in1,
                                                   op0=op0, op1=op1), reads, writes)


def cp(k, e, out, in_, reads, writes):
    if e == "act":
        k.op("act", lambda en: en.copy(out=out, in_=in_), reads, writes)
    else:
        k.op(e, lambda en: en.tensor_copy(out=out, in_=in_), reads, writes)


def load_consts(k, cdram, pfx=""):
    c = {}
    cf = k.sb(pfx + "c_f32", [128, 5 * 128], F32)
    k.dma("sp", cf[:, :], cdram[:, :], [], [cf])
    c["f"] = cf
    cb = k.sb(pfx + "c_bf", [128, 5 * 128], BF16)
    cp(k, "dve", cb[:, :], cf[:, :], [cf], [cb])
    c["b"] = cb
    c["I"] = lambda t=cf: t[:, 0:128]
    c["I2"] = lambda t=cf: t[:, 0:256]
    c["U"] = lambda t=cf: t[:, 256:384]
    c["Ls"] = lambda t=cf: t[:, 384:512]
    c["ones"] = lambda t=cf: t[:, 512:640]
    c["Ib"] = lambda t=cb: t[:, 0:128]
    c["onesb"] = lambda t=cb: t[:, 512:640]
    return c


def consts_np():
    I = np.eye(128, dtype=np.float32)
    r = np.arange(128)
    U = (r[:, None] <= r[None, :]).astype(np.float32)
    Ls = (r[:, None] > r[None, :]).astype(np.float32)
    ones = np.ones((128, 128), np.float32)
    return np.ascontiguousarray(np.concatenate([I, I, U, Ls, ones], axis=1))


def phase_norm_T(k, c, src, nw, dst, nb, pfx):
    wB = k.sb(pfx + "wB", [128, D], F32)
    k.dma("sp", wB[:, :], nw[:, :], [nw], [wB])
    xts = [k.sb(pfx + "xt%d" % i, [128, D], F32) for i in range(2)]
    hnb = [k.sb(pfx + "hnb%d" % i, [128, D], BF16) for i in range(2)]
    junk = k.sb(pfx + "junk", [128, D], BF16)
    st = [k.sb(pfx + "st%d" % i, [128, NC_, TB], BF16) for i in range(2)]
    ss = k.sb(pfx + "ss", [128, 4], F32)
    tp = [k.ps(pfx + "tp%d" % i, [128, 8, 128], BF16) for i in range(2)]
    ev = 0
    for blk in range(nb):
        stg = st[blk % 2]
        for tl in range(3):
            i = blk * 3 + tl
            xt = xts[i % 2]
            hb = hnb[i % 2]
            k.dma("sp", xt[:, :], src[i * 128:(i + 1) * 128, :], [src], [xt])
            act(k, junk[:, :], xt[:, :], AF.Square, [xt], [junk, ss], accum=ss[:, 0:1])
            act(k, ss[:, 1:2], ss[:, 0:1], AF.Sqrt, [ss], [ss], bias=EPS, scale=1.0 / D)
            k.op("dve", lambda en: en.reciprocal(out=ss[:, 2:3], in_=ss[:, 1:2]), [ss], [ss])
            stt(k, hb[:, :], xt[:, :], ss[:, 2:3], wB[:, :], ALU.mult, ALU.mult, [xt, ss, wB], [hb])
            for g in range(4):
                tpp = tp[g % 2]
                for cc in range(8):
                    ch = g * 8 + cc
                    tr(k, tpp[:, cc, :], hb[:, ch * 128:(ch + 1) * 128], c["Ib"](), [hb, c["b"]], [tpp])
                e = "act" if ev % 2 == 0 else "dve"
                ev += 1
                cp(k, e, stg[:, g * 8:(g + 1) * 8, tl * 128:(tl + 1) * 128], tpp[:, :, :], [tpp], [stg])
        k.dma("sp", dst[blk], stg[:, :, :], [stg], [dst])


def v3(ap, b=128):
    return ap.rearrange("p (a b) -> p a b", b=b)


class PsumSet:
    def __init__(self, k, pfx):
        self.full = [k.ps(pfx + "pf%d" % i, [128, 512], F32) for i in range(3)]
        self.half = []
        for i in range(5):
            b = k.ps(pfx + "pb%d" % i, [128, 512], F32)
            self.half.append(b.sub(b.t[:, 0:256], pfx + "ph%da" % i))
            self.half.append(b.sub(b.t[:, 256:512], pfx + "ph%db" % i))


DN_STAGE = 99
DBG_SKIP = False


def phase_dn(k, c, P, hnT, win, cw, hp, onw, o0T, nb, nheads, pfx, dbg=None, out_cb=None):
    nt = nb * 3
    TT = nt * 128
    mmA, mmB, DD = P.full
    GBg, KQ, KS, CHA, CHB, PP, VN, OO, SU, TR = P.half
    cf, cb = c["f"], c["b"]
    hs_ = [slice(0, 128), slice(128, 256)]

    w_sb = k.sb(pfx + "w", [128, NC_, DNW], BF16)
    hbk = [k.sb(pfx + "hb%d" % i, [128, NC_, TB], BF16) for i in range(2)]
    kq = k.sb(pfx + "kq", [128, nt, 256], BF16)
    vT = k.sb(pfx + "vT", [128, 2, TT], BF16)
    zsw = k.sb(pfx + "zsw", [128, nt, 256], BF16)
    oT = k.sb(pfx + "oT", [128, 2, TT], BF16)
    betas = k.sb(pfx + "beta", [128, nt, 2], F32)
    negb = k.sb(pfx + "negb", [128, nt, 2], F32)
    alog = k.sb(pfx + "alog", [128, nt, 2], F32)
    gstep = k.sb(pfx + "gstep", [128, nt, 2], F32)
    cwt = k.sb(pfx + "cw", [128, nheads, 4, 4], F32)
    hpt = k.sb(pfx + "hp", [128, nheads, 4], F32)
    negA = k.sb(pfx + "negA", [128, nheads, 2], F32)
    onwt = k.sb(pfx + "onw", [128, 256], F32)
    rawb = [k.sb(pfx + "raw%d" % i, [128, TB + 3], F32) for i in range(4)]
    acc = k.sb(pfx + "acc", [128, TB], F32)
    yb = k.sb(pfx + "y", [128, TB], F32)
    sqb = k.sb(pfx + "sq", [128, TB], BF16)
    sb_ = k.sb(pfx + "s", [128, TB], F32)
    rn = k.sb(pfx + "rn", [128, TB], F32)
    zt = k.sb(pfx + "zt", [128, 256], BF16)
    RG1 = k.sb(pfx + "RG1", [128, 256], F32)
    RG2 = k.sb(pfx + "RG2", [128, 256], F32)
    decs = k.sb(pfx + "decs", [128, 512], F32)
    EB = k.sb(pfx + "EB", [128, 256], F32)
    eg = k.sb(pfx + "eg", [128, 4], F32)
    t1 = k.sb(pfx + "t1", [128, 256], F32)
    t2 = k.sb(pfx + "t2", [128, 256], F32)
    Ab = [k.sb(pfx + "A%d" % i, [128, 256], F32) for i in range(2)]
    Bb = [k.sb(pfx + "B%d" % i, [128, 256], F32) for i in range(2)]
    Pb = [k.sb(pfx + "P%d" % i, [128, 256], F32) for i in range(2)]
    attnT = k.sb(pfx + "attnT", [128, 256], BF16)
    WT = k.sb(pfx + "WT", [128, 256], BF16)
    qdecT = k.sb(pfx + "qdecT", [128, 256], BF16)
    vtok = k.sb(pfx + "vtok", [128, 256], BF16)
    kdec = k.sb(pfx + "kdec", [128, 256], BF16)
    R = k.sb(pfx + "R", [128, 256], BF16)
    vn = k.sb(pfx + "vn", [128, 256], BF16)
    S = k.sb(pfx + "S", [128, 256], F32)
    Sb = k.sb(pfx + "Sb", [128, 256], BF16)
    ssq = k.sb(pfx + "ssq", [128, 8], F32)
    og = k.sb(pfx + "og", [128, 256], BF16)
    junk = k.sb(pfx + "junk", [128, 256], BF16)
    TRb = TR[:, :].bitcast(BF16)

    k.dma("sp", cwt[:, :, :, :], cw[:, :, :, :], [cw], [cwt])
    k.dma("sp", hpt[:, :, :], hp[:, :, :], [hp], [hpt])
    k.dma("sp", onwt[:, :], onw[:, :], [onw], [onwt])
    act(k, negA[:, :, :], hpt[:, :, 2:4], AF.Exp, [hpt], [negA])
    ts(k, "dve", negA[:, :, :], negA[:, :, :], -1.0, ALU.mult, [negA], [negA])

    bi = 0
    for j in range(nheads):
        for q4 in range(4):
            k.dma("pool", w_sb[:, q4 * 8:(q4 + 1) * 8, :],
                  win[j, q4 * 1024:(q4 + 1) * 1024, :].rearrange("(c p) n -> p c n", p=128),
                  [win], [w_sb])
        for fc in range(4):
            k.op("dve", lambda en: en.memset(rawb[fc][:, 0:3], 0.0), [], [rawb[fc]])
        for tb in range(nb):
            hb = hbk[bi % 2]
            bi += 1
            k.dma("sp", hb[:, :, :], hnT[tb], [hnT], [hb])
            for fc in range(4):
                ps = mmA if fc % 2 == 0 else mmB
                for cc in range(NC_):
                    mm(k, ps[:, 0:TB], w_sb[:, cc, fc * 128:(fc + 1) * 128], hb[:, cc, :],
                       cc == 0, cc == NC_ - 1, [w_sb, hb], [ps])
                rb = rawb[fc]
                cp(k, "act", rb[:, 3:TB + 3], ps[:, 0:TB], [ps], [rb])
                ts(k, "dve", acc[:, :], rb[:, 3:TB + 3], cwt[:, j, fc, 3:4], ALU.mult, [rb, cwt], [acc])
                for tap in (2, 1, 0):
                    stt(k, acc[:, :], rb[:, tap:tap + TB], cwt[:, j, fc, tap:tap + 1], acc[:, :],
                        ALU.mult, ALU.add, [rb, cwt, acc], [acc])
                cp(k, "dve", rb[:, 0:3], rb[:, TB:TB + 3], [rb], [rb])
                if fc >= 2:
                    act(k, vT[:, fc - 2, tb * TB:(tb + 1) * TB], acc[:, :], AF.Silu, [acc], [vT])
                else:
                    act(k, yb[:, :], acc[:, :], AF.Silu, [acc], [yb])
                    act(k, sqb[:, :], yb[:, :], AF.Square, [yb], [sqb])
                    mm(k, DD[:, 0:TB], c["onesb"](), sqb[:, :], True, True, [cb, sqb], [DD])
                    act(k, sb_[:, :], DD[:, 0:TB], AF.Sqrt, [DD], [sb_], bias=EPS)
                    k.op("dve", lambda en: en.reciprocal(out=rn[:, :], in_=sb_[:, :]), [sb_], [rn])
                    if fc == 0:
                        stt(k, kq[:, 3 * tb:3 * tb + 3, 128:256], v3(yb[:, :]), 128.0 ** -0.5, v3(rn[:, :]),
                            ALU.mult, ALU.mult, [yb, rn], [kq])
                    else:
                        tt(k, "dve", kq[:, 3 * tb:3 * tb + 3, 0:128], v3(yb[:, :]), v3(rn[:, :]), ALU.mult,
                           [yb, rn], [kq])
            for tl in range(3):
                i = 3 * tb + tl
                ps = mmA if tl % 2 == 0 else mmB
                for cc in range(NC_):
                    mm(k, ps[:, 0:260], hb[:, cc, tl * 128:(tl + 1) * 128], w_sb[:, cc, 512:772],
                       cc == 0, cc == NC_ - 1, [w_sb, hb], [ps])
                act(k, zt[:, :], ps[:, 0:256], AF.Silu, [ps], [zt])
                tt(k, "dve", zsw[:, i, :], zt[:, :], onwt[:, :], ALU.mult, [zt, onwt], [zsw])
                act(k, betas[:, i, :], ps[:, 256:258], AF.Sigmoid, [ps], [betas])
                tt(k, "dve", alog[:, i, :], ps[:, 258:260], hpt[:, j, 0:2], ALU.add, [ps, hpt], [alog])
        act(k, alog[:, :, :], alog[:, :, :], AF.Exp, [alog], [alog])
        act(k, alog[:, :, :], alog[:, :, :], AF.Ln, [alog], [alog], bias=1.0)
        for h in range(2):
            ts(k, "dve", gstep[:, :, h], alog[:, :, h], negA[:, j, h:h + 1], ALU.mult, [alog, negA], [gstep])
        ts(k, "dve", negb[:, :, :], betas[:, :, :], -1.0, ALU.mult, [betas], [negb])
        k.op("dve", lambda en: en.memset(S[:, :], 0.0), [], [S])
        k.op("dve", lambda en: en.memset(Sb[:, :], 0.0), [], [Sb])

        if dbg is not None:
            k.dma("sp", dbg["kq"][:, :, :], kq[:, :, :], [kq], [dbg["kq"]])
            k.dma("sp", dbg["vT"][:, :, :], vT[:, :, :], [vT], [dbg["vT"]])
            k.dma("sp", dbg["zsw"][:, :, :], zsw[:, :, :], [zsw], [dbg["zsw"]])
            k.dma("sp", dbg["beta"][:, :, :], betas[:, :, :], [betas], [dbg["beta"]])
            k.dma("sp", dbg["gstep"][:, :, :], gstep[:, :, :], [gstep], [dbg["gstep"]])
        for n in range(nt if DN_STAGE >= 2 else 0):
            ST = DN_STAGE
            kT = kq[:, n, 0:128]
            qT = kq[:, n, 128:256]
            for h in range(2):
                ts(k, "dve", RG1[:, hs_[h]], c["U"](), gstep[:, n, h:h + 1], ALU.mult, [cf, gstep], [RG1])
                ts(k, "pool", RG2[:, hs_[h]], c["Ls"](), gstep[:, n, h:h + 1], ALU.mult, [cf, gstep], [RG2])
            mm(k, DD[:, 0:256], c["U"](), RG2[:, :], True, True, [cf, RG2], [DD])
            mm(k, DD[:, 256:512], c["Ls"](), RG1[:, :], True, True, [cf, RG1], [DD])
            mm(k, GBg[:, :], c["ones"](), RG1[:, :], True, True, [cf, RG1], [GBg])
            mm(k, TR[:, 0:2], c["U"](), gstep[:, n, 0:2], True, True, [cf, gstep], [TR])
            act(k, decs[:, :], DD[:, :], AF.Exp, [DD], [decs])
            act(k, EB[:, :], GBg[:, :], AF.Exp, [GBg], [EB])
            act(k, eg[:, 0:2], TR[:, 0:2], AF.Exp, [TR], [eg])
            ts(k, "dve", eg[:, 2:4], eg[:, 0:2], -1.0, ALU.mult, [eg], [eg])
            if ST < 3:
                continue
            mm(k, KQ[:, :], kT, kq[:, n, :], True, True, [kq], [KQ])
            A0, B0 = Ab[0], Bb[0]
            for h in range(2):
                tt(k, "dve", t1[:, hs_[h]], KQ[:, 0:128], decs[:, h * 128:(h + 1) * 128], ALU.mult,
                   [KQ, decs], [t1])
                stt(k, A0[:, hs_[h]], t1[:, hs_[h]], negb[:, n, h:h + 1], c["Ls"](), ALU.mult, ALU.mult,
                    [t1, negb, cf], [A0])
                tt(k, "dve", t2[:, hs_[h]], KQ[:, 128:256], decs[:, 256 + h * 128:256 + (h + 1) * 128],
                   ALU.mult, [KQ, decs], [t2])
                tt(k, "pool", attnT[:, hs_[h]], t2[:, hs_[h]], c["U"](), ALU.mult, [t2, cf], [attnT])
            if ST < 4:
                continue
            for h in range(2):
                tr(k, TR[:, hs_[h]], A0[:, hs_[h]], c["I"](), [A0, cf], [TR])
            cp(k, "act", B0[:, :], TR[:, :], [TR], [B0])
            if ST < 5:
                continue
            ai, pi = 0, 0
            tt(k, "pool", Pb[0][:, :], c["I2"](), B0[:, :], ALU.add, [cf, B0], [Pb[0]])
            for m in range(1, 7):
                Ac, Bc = Ab[ai], Bb[ai]
                An, Bn = Ab[1 - ai], Bb[1 - ai]
                for h in range(2):
                    mm(k, CHA[:, hs_[h]], Bc[:, hs_[h]], Ac[:, hs_[h]], True, True, [Ac, Bc], [CHA])
                if m <= 5:
                    for h in range(2):
                        mm(k, CHB[:, hs_[h]], Ac[:, hs_[h]], Bc[:, hs_[h]], True, True, [Ac, Bc], [CHB])
                cp(k, "act", An[:, :], CHA[:, :], [CHA], [An])
                if m <= 5:
                    cp(k, "dve", Bn[:, :], CHB[:, :], [CHB], [Bn])
                Pc, Pn = Pb[pi], Pb[1 - pi]
                for h in range(2):
                    mm(k, PP[:, hs_[h]], An[:, hs_[h]], Pc[:, hs_[h]], True, True, [An, Pc], [PP])
                tt(k, "dve", Pn[:, :], Pc[:, :], PP[:, :], ALU.add, [Pc, PP], [Pn])
                ai, pi = 1 - ai, 1 - pi
            Pf = Pb[pi]
            for h in range(2):
                ts(k, "dve", WT[:, hs_[h]], Pf[:, hs_[h]], betas[:, n, h:h + 1], ALU.mult, [Pf, betas], [WT])
                tt(k, "pool", qdecT[:, hs_[h]], qT, EB[:, hs_[h]], ALU.mult, [kq, EB], [qdecT])
            if ST < 6:
                continue
            for h in range(2):
                tr(k, TRb[:, hs_[h]], vT[:, h, n * 128:(n + 1) * 128], c["Ib"](), [vT, cb], [TR])
            cp(k, "act", vtok[:, :], TRb[:, 0:256], [TR], [vtok])
            tr(k, TRb[:, 0:128], kT, c["Ib"](), [kq, cb], [TR])
            for h in range(2):
                col = 256 + h * 128 + 127
                ts(k, "dve", kdec[:, hs_[h]], TRb[:, 0:128], decs[:, col:col + 1], ALU.mult, [TR, decs], [kdec])
            if ST < 7:
                continue
            mm(k, KS[:, :], kT, Sb[:, :], True, True, [kq, Sb], [KS])
            for h in range(2):
                stt(k, R[:, hs_[h]], KS[:, hs_[h]], eg[:, 2 + h:3 + h], vtok[:, hs_[h]], ALU.mult, ALU.add,
                    [KS, eg, vtok], [R])
            for h in range(2):
                mm(k, VN[:, hs_[h]], WT[:, hs_[h]], R[:, hs_[h]], True, True, [WT, R], [VN])
            cp(k, "act", vn[:, :], VN[:, :], [VN], [vn])
            for h in range(2):
                mm(k, OO[:, hs_[h]], qdecT[:, hs_[h]], Sb[:, hs_[h]], True, False, [qdecT, Sb], [OO])
                mm(k, OO[:, hs_[h]], attnT[:, hs_[h]], vn[:, hs_[h]], False, True, [attnT, vn], [OO])
            for h in range(2):
                mm(k, SU[:, hs_[h]], kdec[:, hs_[h]], vn[:, hs_[h]], True, True, [kdec, vn], [SU])
            for h in range(2):
                col = h * 128 + 127
                stt(k, S[:, hs_[h]], S[:, hs_[h]], EB[:, col:col + 1], SU[:, hs_[h]], ALU.mult, ALU.add,
                    [S, EB, SU], [S])
            cp(k, "act", Sb[:, :], S[:, :], [S], [Sb])
            for h in range(2):
                act(k, junk[:, hs_[h]], OO[:, hs_[h]], AF.Square, [OO], [junk, ssq], accum=ssq[:, h:h + 1])
            act(k, ssq[:, 2:4], ssq[:, 0:2], AF.Sqrt, [ssq], [ssq], bias=EPS, scale=1.0 / 128)
            k.op("dve", lambda en: en.reciprocal(out=ssq[:, 4:6], in_=ssq[:, 2:4]), [ssq], [ssq])
            for h in range(2):
                stt(k, og[:, hs_[h]], OO[:, hs_[h]], ssq[:, 4 + h:5 + h], zsw[:, n, hs_[h]], ALU.mult, ALU.mult,
                    [OO, ssq, zsw], [og])
            for h in range(2):
                tr(k, TRb[:, hs_[h]], og[:, hs_[h]], c["Ib"](), [og, cb], [TR])
            cp(k, "act", oT[:, :, n * 128:(n + 1) * 128], v3(TRb[:, 0:256]), [TR], [oT])
        if out_cb is not None:
            out_cb(j, oT)
        else:
            for h in range(2):
                k.dma("sp", o0T[:, :, 2 * j + h, :].rearrange("t p x -> p t x"), v3(oT[:, h, :], TB), [oT], [o0T])


def run_chains(pre_gens, state_gen_fn, n, width):
    pres = {}
    nxt = 0
    done_pre = set()
    st = None
    st_i = 0
    while st_i < n:
        while nxt < n and len(pres) < width and nxt < st_i + width:
            pres[nxt] = pre_gens(nxt)
            nxt += 1
        for i in sorted(pres):
            try:
                next(pres[i])
            except StopIteration:
                done_pre.add(i)
                del pres[i]
        if st is None and st_i in done_pre:
            st = state_gen_fn(st_i)
        if st is not None:
            try:
                next(st)
            except StopIteration:
                st = None
                st_i += 1


def phase_dn2(k, c, hnT, win, cw, hp, onw, nb, nheads, pfx, out_cb, NS=3, bg_cb=None):
    nt = nb * 3
    TT = nt * 128
    banks = [k.ps(pfx + "bk%d" % i, [128, 512], F32) for i in range(8)]
    mmA, mmB, Z1, Z2 = banks[0:4]
    DD = Z2
    cf, cb = c["f"], c["b"]
    hs_ = [slice(0, 128), slice(128, 256)]

    w_sb = k.sb(pfx + "w", [128, NC_, DNW], BF16)
    hbk = [k.sb(pfx + "hb%d" % i, [128, NC_, TB], BF16) for i in range(2)]
    kq = k.sb(pfx + "kq", [128, nt, 256], BF16)
    vT = k.sb(pfx + "vT", [128, 2, TT], BF16)
    zsw = k.sb(pfx + "zsw", [128, nt, 256], BF16)
    ostg = [k.sb(pfx + "ostg%d" % i, [128, 2, TB], BF16) for i in range(2)]
    betas = k.sb(pfx + "beta", [128, nt, 2], F32)
    negb = k.sb(pfx + "negb", [128, nt, 2], F32)
    alog = k.sb(pfx + "alog", [128, nt, 2], F32)
    gstep = k.sb(pfx + "gstep", [128, nt, 2], F32)
    cwt = k.sb(pfx + "cw", [128, nheads, 4, 4], F32)
    hpt = k.sb(pfx + "hp", [128, nheads, 4], F32)
    negA = k.sb(pfx + "negA", [128, nheads, 2], F32)
    onwt = k.sb(pfx + "onw", [128, 256], F32)
    rawb = [k.sb(pfx + "raw%d" % i, [128, TB + 3], F32) for i in range(4)]
    sqb = k.sb(pfx + "sq", [128, TB], BF16)
    sb_ = k.sb(pfx + "s", [128, TB], F32)
    rn = k.sb(pfx + "rn", [128, TB], F32)
    zt = k.sb(pfx + "zt", [128, 256], BF16)
    junk = zt

    class Set:
        pass
    sets = []
    for i in range(NS):
        S_ = Set()
        S_.X, S_.Y = (banks[4 + 2 * i], banks[5 + 2 * i]) if i < 2 else (mmA, mmB)
        S_.X1 = k.sb(pfx + "X1_%d" % i, [128, 256], F32)
        S_.X2 = k.sb(pfx + "X2_%d" % i, [128, 256], F32)
        S_.decs = k.sb(pfx + "decs%d" % i, [128, 512], F32)
        S_.EB = k.sb(pfx + "EB%d" % i, [128, 256], F32)
        S_.eg = k.sb(pfx + "eg%d" % i, [128, 8], F32)
        S_.A = [k.sb(pfx + "A%d_%d" % (i, q), [128, 256], F32) for q in range(2)]
        S_.B = [k.sb(pfx + "B%d_%d" % (i, q), [128, 256], F32) for q in range(2)]
        S_.P = k.sb(pfx + "P%d" % i, [128, 256], F32)
        S_.attnT = k.sb(pfx + "attnT%d" % i, [128, 256], BF16)
        S_.WT = k.sb(pfx + "WT%d" % i, [128, 256], BF16)
        S_.qdecT = k.sb(pfx + "qdecT%d" % i, [128, 256], BF16)
        S_.vtok = k.sb(pfx + "vtok%d" % i, [128, 256], BF16)
        S_.kdec = k.sb(pfx + "kdec%d" % i, [128, 256], BF16)
        sets.append(S_)
    acc = sets[0].decs.sub(sets[0].decs.t[:, 0:TB], pfx + "acc")
    ybs = [sets[1 % NS].decs.sub(sets[1 % NS].decs.t[:, 0:TB], pfx + "yb0"),
           sets[2 % NS].decs.sub(sets[2 % NS].decs.t[:, 0:TB], pfx + "yb1")]
    deferred = []
    R = k.sb(pfx + "R", [128, 256], BF16)
    vn = k.sb(pfx + "vn", [128, 256], BF16)
    S = k.sb(pfx + "S", [128, 256], F32)
    Sb = k.sb(pfx + "Sb", [128, 256], BF16)
    ssq = k.sb(pfx + "ssq", [128, 8], F32)
    og = k.sb(pfx + "og", [128, 256], BF16)

    k.dma("sp", cwt[:, :, :, :], cw[:, :, :, :], [cw], [cwt])
    k.dma("sp", hpt[:, :, :], hp[:, :, :], [hp], [hpt])
    k.dma("sp", onwt[:, :], onw[:, :], [onw], [onwt])
    act(k, negA[:, :, :], hpt[:, :, 2:4], AF.Exp, [hpt], [negA])
    ts(k, "dve", negA[:, :, :], negA[:, :, :], -1.0, ALU.mult, [negA], [negA])

    def load_w(jj):
        for q4 in range(4):
            k.dma("pool", w_sb[:, q4 * 8:(q4 + 1) * 8, :],
                  win[jj, q4 * 1024:(q4 + 1) * 1024, :].rearrange("(c p) n -> p c n", p=128),
                  [win], [w_sb])

    bi = 0
    for j in range(nheads):
        if j == 0:
            load_w(0)
        for fc in range(4):
            k.op("dve", lambda en: en.memset(rawb[fc][:, 0:3], 0.0), [], [rawb[fc]])
        for tb in range(nb):
            hb = hbk[bi % 2]
            bi += 1
            k.dma("sp", hb[:, :, :], hnT[tb], [hnT], [hb])
            for fc in range(4):
                ps = mmA if fc % 2 == 0 else mmB
                for cc in range(NC_):
                    mm(k, ps[:, 0:TB], w_sb[:, cc, fc * 128:(fc + 1) * 128], hb[:, cc, :],
                       cc == 0, cc == NC_ - 1, [w_sb, hb], [ps])
                while deferred:
                    deferred.pop(0)()
                rb = rawb[fc]
                cp(k, "act", rb[:, 3:TB + 3], ps[:, 0:TB], [ps], [rb])
                ts(k, "dve", acc[:, :], rb[:, 3:TB + 3], cwt[:, j, fc, 3:4], ALU.mult, [rb, cwt], [acc])
                for tap in (2, 1, 0):
                    stt(k, acc[:, :], rb[:, tap:tap + TB], cwt[:, j, fc, tap:tap + 1], acc[:, :],
                        ALU.mult, ALU.add, [rb, cwt, acc], [acc])
                cp(k, "dve", rb[:, 0:3], rb[:, TB:TB + 3], [rb], [rb])
                if fc >= 2:
                    act(k, vT[:, fc - 2, tb * TB:(tb + 1) * TB], acc[:, :], AF.Silu, [acc], [vT])
                else:
                    ybf = ybs[fc]
                    act(k, ybf[:, :], acc[:, :], AF.Silu, [acc], [ybf])

                    def norm_part(fc=fc, tb=tb, ybf=ybf):
                        act(k, sqb[:, :], ybf[:, :], AF.Square, [ybf], [sqb])
                        mm(k, DD[:, 0:TB], c["onesb"](), sqb[:, :], True, True, [cb, sqb], [DD])
                        act(k, sb_[:, :], DD[:, 0:TB], AF.Sqrt, [DD], [sb_], bias=EPS)
                        k.op("dve", lambda en: en.reciprocal(out=rn[:, :], in_=sb_[:, :]), [sb_], [rn])
                        if fc == 0:
                            stt(k, kq[:, 3 * tb:3 * tb + 3, 128:256], v3(ybf[:, :]), 128.0 ** -0.5, v3(rn[:, :]),
                                ALU.mult, ALU.mult, [ybf, rn], [kq])
                        else:
                            tt(k, "dve", kq[:, 3 * tb:3 * tb + 3, 0:128], v3(ybf[:, :]), v3(rn[:, :]), ALU.mult,
                               [ybf, rn], [kq])
                    deferred.append(norm_part)
            for tl in range(3):
                i = 3 * tb + tl
                ps = mmA if tl % 2 == 0 else mmB
                for cc in range(NC_):
                    mm(k, ps[:, 0:260], hb[:, cc, tl * 128:(tl + 1) * 128], w_sb[:, cc, 512:772],
                       cc == 0, cc == NC_ - 1, [w_sb, hb], [ps])
                while deferred:
                    deferred.pop(0)()
                act(k, zt[:, :], ps[:, 0:256], AF.Silu, [ps], [zt])
                tt(k, "dve", zsw[:, i, :], zt[:, :], onwt[:, :], ALU.mult, [zt, onwt], [zsw])
                act(k, betas[:, i, :], ps[:, 256:258], AF.Sigmoid, [ps], [betas])
                tt(k, "dve", alog[:, i, :], ps[:, 258:260], hpt[:, j, 0:2], ALU.add, [ps, hpt], [alog])
        act(k, alog[:, :, :], alog[:, :, :], AF.Exp, [alog], [alog])
        act(k, alog[:, :, :], alog[:, :, :], AF.Ln, [alog], [alog], bias=1.0)
        for h in range(2):
            ts(k, "dve", gstep[:, :, h], alog[:, :, h], negA[:, j, h:h + 1], ALU.mult, [alog, negA], [gstep])
        ts(k, "dve", negb[:, :, :], betas[:, :, :], -1.0, ALU.mult, [betas], [negb])
        k.op("dve", lambda en: en.memset(S[:, :], 0.0), [], [S])
        k.op("dve", lambda en: en.memset(Sb[:, :], 0.0), [], [Sb])

        if j + 1 < nheads:
            load_w(j + 1)
        if bg_cb is not None:
            bg_cb(j)
        def pre(n, j=j):
            st_ = sets[n % NS]
            X, Y = st_.X, st_.Y
            Xb = X[:, :].bitcast(BF16)
            Yb = Y[:, :].bitcast(BF16)
            kT = kq[:, n, 0:128]
            qT = kq[:, n, 128:256]
            RG1, RG2, decs, EB, eg = st_.X1, st_.X2, st_.decs, st_.EB, st_.eg
            for h in range(2):
                ts(k, "dve", RG1[:, hs_[h]], c["U"](), gstep[:, n, h:h + 1], ALU.mult, [cf, gstep], [RG1])
                ts(k, "pool", RG2[:, hs_[h]], c["Ls"](), gstep[:, n, h:h + 1], ALU.mult, [cf, gstep], [RG2])
            yield
            mm(k, X[:, 0:256], c["U"](), RG2[:, :], True, True, [cf, RG2], [X])
            mm(k, X[:, 256:512], c["Ls"](), RG1[:, :], True, True, [cf, RG1], [X])
            mm(k, Y[:, 0:256], c["ones"](), RG1[:, :], True, True, [cf, RG1], [Y])
            mm(k, Y[:, 256:258], c["U"](), gstep[:, n, 0:2], True, True, [cf, gstep], [Y])
            yield
            act(k, decs[:, :], X[:, :], AF.Exp, [X], [decs])
            act(k, EB[:, :], Y[:, 0:256], AF.Exp, [Y], [EB])
            act(k, eg[:, 0:2], Y[:, 256:258], AF.Exp, [Y], [eg])
            yield
            ts(k, "dve", eg[:, 2:4], eg[:, 0:2], -1.0, ALU.mult, [eg], [eg])
            for h in range(2):
                col = h * 128 + 127
                cp(k, "dve", eg[:, 4 + h:5 + h], EB[:, col:col + 1], [EB], [eg])
            mm(k, X[:, 0:256], kT, kq[:, n, :], True, True, [kq], [X])
            yield
            A0, B0 = st_.A[0], st_.B[0]
            t1, t2 = RG1, RG2
            for h in range(2):
                tt(k, "dve", t1[:, hs_[h]], X[:, 0:128], decs[:, h * 128:(h + 1) * 128], ALU.mult, [X, decs], [t1])
                stt(k, A0[:, hs_[h]], t1[:, hs_[h]], negb[:, n, h:h + 1], c["Ls"](), ALU.mult, ALU.mult,
                    [t1, negb, cf], [A0])
            yield
            for h in range(2):
                tr(k, Y[:, hs_[h]], A0[:, hs_[h]], c["I"](), [A0, cf], [Y])
            for h in range(2):
                tt(k, "dve", t2[:, hs_[h]], X[:, 128:256], decs[:, 256 + h * 128:256 + (h + 1) * 128],
                   ALU.mult, [X, decs], [t2])
                tt(k, "pool", st_.attnT[:, hs_[h]], t2[:, hs_[h]], c["U"](), ALU.mult, [t2, cf], [st_.attnT])
            yield
            cp(k, "act", B0[:, :], Y[:, 0:256], [Y], [B0])
            yield
            Pm = st_.P
            tt(k, "pool", Pm[:, :], c["I2"](), B0[:, :], ALU.add, [cf, B0], [Pm])
            ai = 0
            for m in range(1, 7):
                Ac, Bc = st_.A[ai], st_.B[ai]
                An, Bn = st_.A[1 - ai], st_.B[1 - ai]
                for h in range(2):
                    mm(k, X[:, hs_[h]], Bc[:, hs_[h]], Ac[:, hs_[h]], True, True, [Ac, Bc], [X])
                if m <= 5:
                    for h in range(2):
                        mm(k, Y[:, hs_[h]], Ac[:, hs_[h]], Bc[:, hs_[h]], True, True, [Ac, Bc], [Y])
                yield
                cp(k, "act", An[:, :], X[:, 0:256], [X], [An])
                if m <= 5:
                    cp(k, "dve", Bn[:, :], Y[:, 0:256], [Y], [Bn])
                yield
                for h in range(2):
                    mm(k, X[:, 256 + h * 128:384 + h * 128], An[:, hs_[h]], Pm[:, hs_[h]], True, True, [An, Pm], [X])
                yield
                tt(k, "dve", Pm[:, :], Pm[:, :], X[:, 256:512], ALU.add, [Pm, X], [Pm])
                ai = 1 - ai
            yield
            for h in range(2):
                ts(k, "dve", st_.WT[:, hs_[h]], Pm[:, hs_[h]], betas[:, n, h:h + 1], ALU.mult, [Pm, betas], [st_.WT])
                tt(k, "pool", st_.qdecT[:, hs_[h]], qT, EB[:, hs_[h]], ALU.mult, [kq, EB], [st_.qdecT])
            for h in range(2):
                tr(k, Yb[:, hs_[h]], vT[:, h, n * 128:(n + 1) * 128], c["Ib"](), [vT, cb], [Y])
            tr(k, Xb[:, 0:128], kT, c["Ib"](), [kq, cb], [X])
            yield
            cp(k, "act", st_.vtok[:, :], Yb[:, 0:256], [Y], [st_.vtok])
            for h in range(2):
                col = 256 + h * 128 + 127
                ts(k, "dve", st_.kdec[:, hs_[h]], Xb[:, 0:128], decs[:, col:col + 1], ALU.mult, [X, decs], [st_.kdec])
            yield

        def state(n, j=j):
            st_ = sets[n % NS]
            eg = st_.eg
            kT = kq[:, n, 0:128]
            Z1b = Z1[:, :].bitcast(BF16)
            mm(k, Z1[:, 0:256], kT, Sb[:, :], True, True, [kq, Sb], [Z1])
            yield
            for h in range(2):
                stt(k, R[:, hs_[h]], Z1[:, hs_[h]], eg[:, 2 + h:3 + h], st_.vtok[:, hs_[h]], ALU.mult, ALU.add,
                    [Z1, eg, st_.vtok], [R])
            yield
            for h in range(2):
                mm(k, Z2[:, hs_[h]], st_.WT[:, hs_[h]], R[:, hs_[h]], True, True, [st_.WT, R], [Z2])
            yield
            cp(k, "act", vn[:, :], Z2[:, 0:256], [Z2], [vn])
            yield
            for h in range(2):
                mm(k, Z2[:, 256 + h * 128:384 + h * 128], st_.kdec[:, hs_[h]], vn[:, hs_[h]], True, True,
                   [st_.kdec, vn], [Z2])
            for h in range(2):
                mm(k, Z1[:, 256 + h * 128:384 + h * 128], st_.qdecT[:, hs_[h]], Sb[:, hs_[h]], True, False,
                   [st_.qdecT, Sb], [Z1])
                mm(k, Z1[:, 256 + h * 128:384 + h * 128], st_.attnT[:, hs_[h]], vn[:, hs_[h]], False, True,
                   [st_.attnT, vn], [Z1])
            yield
            for h in range(2):
                stt(k, S[:, hs_[h]], S[:, hs_[h]], eg[:, 4 + h:5 + h], Z2[:, 256 + h * 128:384 + h * 128],
                    ALU.mult, ALU.add, [S, eg, Z2], [S])
            yield
            cp(k, "act", Sb[:, :], S[:, :], [S], [Sb])
            for h in range(2):
                act(k, junk[:, hs_[h]], Z1[:, 256 + h * 128:384 + h * 128], AF.Square, [Z1], [junk, ssq],
                    accum=ssq[:, h:h + 1])
            yield
            act(k, ssq[:, 2:4], ssq[:, 0:2], AF.Sqrt, [ssq], [ssq], bias=EPS, scale=1.0 / 128)
            yield
            k.op("dve", lambda en: en.reciprocal(out=ssq[:, 4:6], in_=ssq[:, 2:4]), [ssq], [ssq])
            for h in range(2):
                stt(k, og[:, hs_[h]], Z1[:, 256 + h * 128:384 + h * 128], ssq[:, 4 + h:5 + h], zsw[:, n, hs_[h]],
                    ALU.mult, ALU.mult, [Z1, ssq, zsw], [og])
            yield
            for h in range(2):
                tr(k, Z1b[:, hs_[h]], og[:, hs_[h]], c["Ib"](), [og, cb], [Z1])
            yield
            stg = ostg[(n // 3) % 2]
            tl = n % 3
            cp(k, "act", stg[:, :, tl * 128:(tl + 1) * 128], v3(Z1b[:, 0:256]), [Z1], [stg])
            if tl == 2:
                out_cb(j, n // 3, stg)
            yield

        run_chains(pre, state, nt if not DBG_SKIP else 0, NS)


def phase_outproj(k, P, xTs, kcs, wout, hres, out, nb, pfx, ssq_out=None, out_rows=None, load_cb=None,
                  wq="pool", single=False):
    KC = sum(kcs)
    nt = nb * 3
    mmA, mmB, DD = P.full
    groups = [[0, 1]] if single else [[0], [1]]
    wcols = 512 * len(groups[0])
    w_sb = k.sb(pfx + "w", [128, KC, wcols], BF16)
    xb = [k.sb(pfx + "xb%d" % i, [128, KC, TB], BF16) for i in range(2)]
    res = [k.sb(pfx + "res%d" % i, [128, 512], F32) for i in range(2)]
    ot = [k.sb(pfx + "ot%d" % i, [128, 512], F32) for i in range(2)]
    junk = k.sb(pfx + "junk", [128, 512], BF16)
    ssa = k.sb(pfx + "ssa", [128, 2, nt], F32)
    bi = 0
    ti = 0
    for grp in groups:
        c0 = grp[0] * 512
        for q8 in range(KC // 8):
            k.dma(wq, w_sb[:, q8 * 8:(q8 + 1) * 8, :],
                  wout[q8 * 1024:(q8 + 1) * 1024, c0:c0 + wcols].rearrange("(c p) n -> p c n", p=128),
                  [wout], [w_sb])
        for tb in range(nb):
            xbb = xb[bi % 2]
            bi += 1
            off = 0
            if load_cb is not None:
                load_cb(tb, xbb)
            else:
                for xT, kc in zip(xTs, kcs):
                    k.dma("sp", xbb[:, off:off + kc, :], xT[tb], [xT], [xbb])
                    off += kc
            for tl in range(3):
              for gi, cbk in enumerate(grp):
                cs = slice(cbk * 512, (cbk + 1) * 512)
                i = 3 * tb + tl
                ps = mmA if ti % 2 == 0 else mmB
                rs, o_ = res[ti % 2], ot[ti % 2]
                ti += 1
                k.dma("sp", rs[:, :], hres[i * 128:(i + 1) * 128, cs], [hres], [rs])
                for cc in range(KC):
                    mm(k, ps[:, :], xbb[:, cc, tl * 128:(tl + 1) * 128], w_sb[:, cc, gi * 512:(gi + 1) * 512],
                       cc == 0, cc == KC - 1, [xbb, w_sb], [ps])
                tt(k, "dve", o_[:, :], ps[:, :], rs[:, :], ALU.add, [ps, rs], [o_])
                if ssq_out is not None:
                    act(k, junk[:, :], o_[:, :], AF.Square, [o_], [junk, ssa], accum=ssa[:, cbk, i:i + 1])
                if out_rows is None:
                    k.dma("sp", out[i * 128:(i + 1) * 128, cs], o_[:, :], [o_], [out])
                else:
                    for dst_rows, src_rows in out_rows(i):
                        k.dma("sp", out[dst_rows, cs], o_[src_rows, :], [o_], [out])
    if ssq_out is not None:
        tt(k, "dve", ssa[:, 0, :], ssa[:, 0, :], ssa[:, 1, :], ALU.add, [ssa], [ssa])
        k.dma("sp", ssq_out[:, :], ssa[:, 0, :], [ssa], [ssq_out])


def phase_norm2(k, c, P, ssq_all, h1, nw, dst, nb, pfx, ssq_view=None, out_cb=None):
    nt = nb * 3
    TR = P.half[9]
    TRb = TR[:, :].bitcast(BF16)
    cb = c["b"]
    sq = k.sb(pfx + "sq", [128, 4, nt], F32)
    rstd = k.sb(pfx + "rstd", [128, nt], F32)
    wB = k.sb(pfx + "wB", [128, 1024], F32)
    xts = [k.sb(pfx + "xt%d" % i, [128, 1024], F32) for i in range(2)]
    hnb = [k.sb(pfx + "hn%d" % i, [128, 1024], BF16) for i in range(2)]
    st = [k.sb(pfx + "st%d" % i, [128, 8, TB], BF16) for i in range(2)]
    k.dma("sp", sq[:, :, :], ssq_view if ssq_view is not None else ssq_all[:, :, :].rearrange("g p n -> p g n"),
          [ssq_all], [sq])
    k.dma("sp", wB[:, :], nw[:, :], [nw], [wB])
    for g in range(1, 4):
        tt(k, "dve", sq[:, 0, :], sq[:, 0, :], sq[:, g, :], ALU.add, [sq], [sq])
    act(k, sq[:, 1, :], sq[:, 0, :], AF.Sqrt, [sq], [sq], bias=EPS, scale=1.0 / D)
    k.op("dve", lambda en: en.reciprocal(out=rstd[:, :], in_=sq[:, 1, :]), [sq], [rstd])
    for blk in range(nb):
        stg = st[blk % 2]
        for tl in range(3):
            i = blk * 3 + tl
            xt, hb = xts[i % 2], hnb[i % 2]
            k.dma("sp", xt[:, :], h1[i * 128:(i + 1) * 128, :], [h1], [xt])
            stt(k, hb[:, :], xt[:, :], rstd[:, i:i + 1], wB[:, :], ALU.mult, ALU.mult, [xt, rstd, wB], [hb])
            for cc in range(8):
                tr(k, TRb[:, (cc % 4) * 128:(cc % 4 + 1) * 128], hb[:, cc * 128:(cc + 1) * 128], c["Ib"](),
                   [hb, cb], [TR])
                if cc % 4 == 3:
                    g4 = cc // 4
                    cp(k, "act", stg[:, g4 * 4:(g4 + 1) * 4, tl * 128:(tl + 1) * 128], v3(TRb[:, 0:512]),
                       [TR], [stg])
        if out_cb is not None:
            out_cb(blk, stg)
        else:
            k.dma("sp", dst[blk], stg[:, :, :], [stg], [dst])


def phase_sb(k, c, P, hnTs, win, nwqk, o1T, nb, nheads, pfx, load_cb=None, out_cb=None):
    nt = nb * 3
    TT = nt * 128
    mmA, mmB, DD = P.full
    ZP = P.half[0].root
    TR = P.half[2]
    OO = P.half[4]
    TRb = TR[:, :].bitcast(BF16)
    cf, cb = c["f"], c["b"]
    w_sb = k.sb(pfx + "w", [128, NC_, 512], BF16)
    hbk = [k.sb(pfx + "hb%d" % i, [128, NC_, TB], BF16) for i in range(2)]
    qT = k.sb(pfx + "qT", [128, TT], BF16)
    kT = k.sb(pfx + "kT", [128, TT], BF16)
    vtok = k.sb(pfx + "vtok", [128, nt, 128], BF16)
    gs = k.sb(pfx + "gs", [128, nt, 128], BF16)
    oT = k.sb(pfx + "oT", [128, TT], BF16)
    nwt = k.sb(pfx + "nw", [128, 2], F32)
    sqb = k.sb(pfx + "sq", [128, TB], BF16)
    sb_ = k.sb(pfx + "s", [128, TB], F32)
    rn = k.sb(pfx + "rn", [128, TB], F32)
    ones512 = k.sb(pfx + "ones", [128, 512], F32)
    E1 = k.sb(pfx + "E1", [128, 512], F32)
    SP = k.sb(pfx + "SP", [128, 512], F32)
    PRE = k.sb(pfx + "PRE", [128, 512], F32)
    ARG = k.sb(pfx + "ARG", [128, 512], F32)
    Aw = k.sb(pfx + "Aw", [128, 512], BF16)
    AT = k.sb(pfx + "AT", [128, 512], BF16)
    car = k.sb(pfx + "car", [128, 4], F32)
    og = k.sb(pfx + "og", [128, 128], BF16)
    k.dma("sp", nwt[:, :], nwqk[:, :], [nwqk], [nwt])
    k.op("dve", lambda en: en.memset(ones512[:, :], 1.0), [], [ones512])
    bi = 0
    for j in range(nheads):
        for q4 in range(4):
            k.dma("pool", w_sb[:, q4 * 8:(q4 + 1) * 8, :],
                  win[j, q4 * 1024:(q4 + 1) * 1024, :].rearrange("(c p) n -> p c n", p=128), [win], [w_sb])
        for tb in range(nb):
            hb = hbk[bi % 2]
            bi += 1
            if load_cb is not None:
                load_cb(tb, hb)
            else:
                for g in range(4):
                    k.dma("sp", hb[:, g * 8:(g + 1) * 8, :], hnTs[g][tb], [hnTs[g]], [hb])
            for fc in range(2):
                ps = mmA if fc == 0 else mmB
                for cc in range(NC_):
                    mm(k, ps[:, 0:TB], w_sb[:, cc, fc * 128:(fc + 1) * 128], hb[:, cc, :], cc == 0, cc == NC_ - 1,
                       [w_sb, hb], [ps])
                act(k, sqb[:, :], ps[:, 0:TB], AF.Square, [ps], [sqb])
                mm(k, DD[:, 0:TB], c["onesb"](), sqb[:, :], True, True, [cb, sqb], [DD])
                act(k, sb_[:, :], DD[:, 0:TB], AF.Sqrt, [DD], [sb_], bias=EPS, scale=1.0 / 128)
                k.op("dve", lambda en: en.reciprocal(out=rn[:, :], in_=sb_[:, :]), [sb_], [rn])
                dstT = qT if fc == 0 else kT
                stt(k, rn[:, :], rn[:, :], nwt[:, fc:fc + 1], ps[:, 0:TB], ALU.mult, ALU.mult, [rn, nwt, ps], [rn])
                if fc == 0:
                    ts(k, "dve", dstT[:, tb * TB:(tb + 1) * TB], rn[:, :], 128.0 ** -0.5, ALU.mult, [rn], [dstT])
                else:
                    cp(k, "dve", dstT[:, tb * TB:(tb + 1) * TB], rn[:, :], [rn], [dstT])
            for tl in range(3):
                i = 3 * tb + tl
                ps = mmA if tl % 2 == 0 else mmB
                for cc in range(NC_):
                    mm(k, ps[:, 0:256], hb[:, cc, tl * 128:(tl + 1) * 128], w_sb[:, cc, 256:512], cc == 0,
                       cc == NC_ - 1, [w_sb, hb], [ps])
                cp(k, "dve", vtok[:, i, :], ps[:, 0:128], [ps], [vtok])
                act(k, gs[:, i, :], ps[:, 128:256], AF.Silu, [ps], [gs])
        for tq in range(nt):
            k.op("dve", lambda en: en.memset(car[:, 0:1], 0.0), [], [car])
            blocks = []
            t1 = tq + 1
            while t1 > 0:
                t0 = max(0, t1 - 4)
                blocks.append((t0, t1))
                t1 = t0
            nmm = tq + 1
            mi = 0
            for (t0, t1) in blocks:
                n = t1 - t0
                W = n * 128
                diag = (t1 - 1 == tq)
                mm(k, ZP[:, 0:W], qT[:, tq * 128:(tq + 1) * 128], kT[:, t0 * 128:t1 * 128], True, True,
                   [qT, kT], [ZP])
                act(k, E1[:, 0:W], ZP[:, 0:W], AF.Exp, [ZP], [E1])
                act(k, SP[:, 0:W], E1[:, 0:W], AF.Ln, [E1], [SP], bias=1.0)
                if diag:
                    tt(k, "pool", SP[:, W - 128:W], SP[:, W - 128:W], c["Ls"](), ALU.mult, [SP, cf], [SP])
                k.op("dve", lambda en: en.tensor_tensor_scan(out=PRE[:, 0:W], data0=ones512[:, 0:W],
                                                             data1=SP[:, 0:W], initial=0.0, op0=ALU.mult,
                                                             op1=ALU.add), [ones512, SP], [PRE])
                tt(k, "dve", ARG[:, 0:W], ZP[:, 0:W], SP[:, 0:W], ALU.subtract, [ZP, SP], [ARG])
                tt(k, "pool", ARG[:, 0:W], ARG[:, 0:W], PRE[:, 0:W], ALU.add, [ARG, PRE], [ARG])
                tt(k, "dve", car[:, 1:2], car[:, 0:1], PRE[:, W - 1:W], ALU.add, [car, PRE], [car])
                ts(k, "dve", car[:, 2:3], car[:, 1:2], -1.0, ALU.mult, [car], [car])
                act(k, Aw[:, 0:W], ARG[:, 0:W], AF.Exp, [ARG, car], [Aw], bias=car[:, 2:3])
                cp(k, "dve", car[:, 0:1], car[:, 1:2], [car], [car])
                if diag:
                    tt(k, "pool", Aw[:, W - 128:W], Aw[:, W - 128:W], c["b"][:, 384:512], ALU.mult, [Aw, cb], [Aw])
                for s_ in range(n):
                    tr(k, TRb[:, s_ * 128:(s_ + 1) * 128], Aw[:, s_ * 128:(s_ + 1) * 128], c["Ib"](), [Aw, cb], [TR])
                cp(k, "act", AT[:, 0:W], TRb[:, 0:W], [TR], [AT])
                for s_ in range(n):
                    mm(k, OO[:, 0:128], AT[:, s_ * 128:(s_ + 1) * 128], vtok[:, t0 + s_, :], mi == 0, mi == nmm - 1,
                       [AT, vtok], [OO])
                    mi += 1
            tt(k, "dve", og[:, :], OO[:, 0:128], gs[:, tq, :], ALU.mult, [OO, gs], [og])
            tr(k, TRb[:, 0:128], og[:, :], c["Ib"](), [og, cb], [TR])
            cp(k, "act", oT[:, tq * 128:(tq + 1) * 128], TRb[:, 0:128], [TR], [oT])
        if out_cb is not None:
            out_cb(j, oT)
        else:
            k.dma("sp", o1T[:, :, j, :].rearrange("t p x -> p t x"), v3(oT[:, :], TB), [oT], [o1T])


def _new(with_consts=True):
    nc = bass.Bass("TRN2", target_bir_lowering=False)
    k = K(nc)
    cd = k.dram("consts", [128, 640], F32, kind="ExternalInput") if with_consts else None
    return nc, k, cd


def build_l1(nb=NB, nh=8):
    nc, k, cd = _new()
    h0 = k.dram("h0", [nb * TB, D], F32, kind="ExternalInput")
    nw = k.dram("nw", [128, D], F32, kind="ExternalInput")
    win = k.dram("win", [nh, D, DNW], F32, kind="ExternalInput")
    cw = k.dram("cw", [128, nh, 4, 4], F32, kind="ExternalInput")
    hp = k.dram("hp", [128, nh, 4], F32, kind="ExternalInput")
    onw = k.dram("onw", [128, 256], F32, kind="ExternalInput")
    hnT = k.dram("hnT", [nb, 128, NC_, TB], BF16, kind="Internal")
    o0T = k.dram("o0T", [nb, 128, nh * 2, TB], BF16, kind="ExternalOutput")
    k.phase_begin()
    c = load_consts(k, cd, "a")
    phase_norm_T(k, c, h0, nw, hnT, nb, "n")
    k.phase_end()
    k.phase_begin()
    c = load_consts(k, cd, "b")
    def dn_out(j, tb, stg):
        k.dma("sp", o0T[tb, :, 2 * j:2 * j + 2, :], stg[:, :, :], [stg], [o0T])
    phase_dn2(k, c, hnT, win, cw, hp, onw, nb, nh, "d", dn_out)
    k.phase_end()
    k.finish()
    return nc


def build_outproj(kc_each, final, nb=NB):
    nc, k, cd = _new(False)
    xTs = [k.dram("xT%d" % g, [nb, 128, kc_each, TB], BF16, kind="ExternalInput") for g in range(4)]
    wout = k.dram("wout", [4 * kc_each * 128, 1024], F32, kind="ExternalInput")
    hres = k.dram("hres", [nb * TB, 1024], F32, kind="ExternalInput")
    P = PsumSet(k, "p")
    if not final:
        out = k.dram("h1", [nb * TB, 1024], F32, kind="ExternalOutput")
        ssq = k.dram("ssq", [128, nb * 3], F32, kind="ExternalOutput")
        phase_outproj(k, P, xTs, [kc_each] * 4, wout, hres, out, nb, "o", ssq_out=ssq)
    else:
        out = k.dram("out", [SEQ, 1024], F32, kind="ExternalOutput")

        def rows(i):
            lo = max(i * 128, NMETA)
            hi = min((i + 1) * 128, NMETA + SEQ)
            if hi <= lo:
                return []
            return [(slice(lo - NMETA, hi - NMETA), slice(lo - i * 128, hi - i * 128))]
        phase_outproj(k, P, xTs, [kc_each] * 4, wout, hres, out, nb, "o", out_rows=rows)
    k.finish()
    return nc


def build_l3(nb=NB):
    nc, k, cd = _new()
    ssq_all = k.dram("ssq_all", [4, 128, nb * 3], F32, kind="ExternalInput")
    h1 = k.dram("h1", [nb * TB, 1024], F32, kind="ExternalInput")
    nw = k.dram("nw", [128, 1024], F32, kind="ExternalInput")
    dst = k.dram("hn1T", [nb, 128, 8, TB], BF16, kind="ExternalOutput")
    c = load_consts(k, cd)
    P = PsumSet(k, "p")
    phase_norm2(k, c, P, ssq_all, h1, nw, dst, nb, "m")
    k.finish()
    return nc


def build_l4(nb=NB, nh=8):
    nc, k, cd = _new()
    hnTs = [k.dram("hnT%d" % g, [nb, 128, 8, TB], BF16, kind="ExternalInput") for g in range(4)]
    win = k.dram("win", [nh, D, 512], F32, kind="ExternalInput")
    nwqk = k.dram("nwqk", [128, 2], F32, kind="ExternalInput")
    o1T = k.dram("o1T", [nb, 128, nh, TB], BF16, kind="ExternalOutput")
    c = load_consts(k, cd)
    def load_h(tb, hb):
        for g in range(4):
            k.dma("sp", hb[:, g * 8:(g + 1) * 8, :], hnTs[g][tb], [hnTs[g]], [hb])

    def sb_out(j, oT):
        k.dma("sp", o1T[:, :, j, :].rearrange("t p x -> p t x"), v3(oT[:, :], TB), [oT], [o1T])
    phase_sb2(k, c, win, nwqk, nb, nh, "s", load_h, sb_out)
    k.finish()
    return nc


def run_rr(gens, width, bg=None):
    pending = list(gens)
    active = []
    while pending or active:
        while pending and len(active) < width:
            active.append(pending.pop(0)())
        for g in list(active):
            try:
                next(g)
            except StopIteration:
                active.remove(g)
        if bg is not None:
            try:
                next(bg)
            except StopIteration:
                bg = None
    if bg is not None:
        for _ in bg:
            pass


def phase_sb2(k, c, win, nwqk, nb, nheads, pfx, load_cb, out_cb, NS=3):
    nt = nb * 3
    TT = nt * 128
    banks = [k.ps(pfx + "bk%d" % i, [128, 512], F32) for i in range(8)]
    mmA, mmB = banks[0:2]
    cf, cb = c["f"], c["b"]
    hnTs = None
    w_sb = k.sb(pfx + "w", [128, NC_, 512], BF16)
    hbk = [k.sb(pfx + "hb%d" % i, [128, NC_, TB], BF16) for i in range(2)]
    arrs = [(k.sb(pfx + "qT%d" % i, [128, TT], BF16), k.sb(pfx + "kT%d" % i, [128, TT], BF16),
             k.sb(pfx + "vtok%d" % i, [128, nt, 128], BF16), k.sb(pfx + "gs%d" % i, [128, nt, 128], BF16))
            for i in range(2)]
    oT = k.sb(pfx + "oT", [128, TT], BF16)
    nwt = k.sb(pfx + "nw", [128, 2], F32)
    sqb = k.sb(pfx + "sq", [128, TB], BF16)
    sb_ = k.sb(pfx + "s", [128, TB], F32)
    rn = k.sb(pfx + "rn", [128, TB], F32)
    ones512 = k.sb(pfx + "ones", [128, 512], F32)

    class Set:
        pass
    sets = []
    for i in range(NS):
        S_ = Set()
        S_.ZP, S_.TA = banks[2 + 2 * i], banks[3 + 2 * i]
        S_.E1 = k.sb(pfx + "E1_%d" % i, [128, 512], F32)
        S_.SP = k.sb(pfx + "SP_%d" % i, [128, 512], F32)
        S_.PRE = k.sb(pfx + "PRE_%d" % i, [128, 512], F32)
        S_.Aw = k.sb(pfx + "Aw_%d" % i, [128, 512], BF16)
        S_.AT = k.sb(pfx + "AT_%d" % i, [128, 512], BF16)
        S_.car = k.sb(pfx + "car_%d" % i, [128, 4], F32)
        S_.Oa = k.sb(pfx + "Oa_%d" % i, [128, 128], F32)
        S_.og = k.sb(pfx + "og_%d" % i, [128, 128], BF16)
        sets.append(S_)
    def load_w(jj):
        for q4 in range(4):
            k.dma("pool", w_sb[:, q4 * 8:(q4 + 1) * 8, :],
                  win[jj, q4 * 1024:(q4 + 1) * 1024, :].rearrange("(c p) n -> p c n", p=128), [win], [w_sb])

    bi = [0]

    def inproj(j):
        qT, kT, vtok, gs = arrs[j % 2]
        for tb in range(nb):
            hb = hbk[bi[0] % 2]
            bi[0] += 1
            load_cb(tb, hb)
            for fc in range(2):
                ps, DD = (mmA, mmB) if fc == 0 else (mmB, mmA)
                for cc in range(NC_):
                    mm(k, ps[:, 0:TB], w_sb[:, cc, fc * 128:(fc + 1) * 128], hb[:, cc, :], cc == 0, cc == NC_ - 1,
                       [w_sb, hb], [ps])
                yield
                act(k, sqb[:, :], ps[:, 0:TB], AF.Square, [ps], [sqb])
                yield
                mm(k, DD[:, 0:TB], c["onesb"](), sqb[:, :], True, True, [cb, sqb], [DD])
                yield
                act(k, sb_[:, :], DD[:, 0:TB], AF.Sqrt, [DD], [sb_], bias=EPS, scale=1.0 / 128)
                yield
                k.op("dve", lambda en: en.reciprocal(out=rn[:, :], in_=sb_[:, :]), [sb_], [rn])
                dstT = qT if fc == 0 else kT
                stt(k, rn[:, :], rn[:, :], nwt[:, fc:fc + 1], ps[:, 0:TB], ALU.mult, ALU.mult, [rn, nwt, ps], [rn])
                if fc == 0:
                    ts(k, "dve", dstT[:, tb * TB:(tb + 1) * TB], rn[:, :], 128.0 ** -0.5, ALU.mult, [rn], [dstT])
                else:
                    cp(k, "dve", dstT[:, tb * TB:(tb + 1) * TB], rn[:, :], [rn], [dstT])
                yield
            for tl in range(3):
                i = 3 * tb + tl
                ps = mmA if tl % 2 == 0 else mmB
                for cc in range(NC_):
                    mm(k, ps[:, 0:256], hb[:, cc, tl * 128:(tl + 1) * 128], w_sb[:, cc, 256:512], cc == 0,
                       cc == NC_ - 1, [w_sb, hb], [ps])
                yield
                cp(k, "dve", vtok[:, i, :], ps[:, 0:128], [ps], [vtok])
                act(k, gs[:, i, :], ps[:, 128:256], AF.Silu, [ps], [gs])
                yield
        if j + 1 < nheads:
            load_w(j + 1)

    k.dma("sp", nwt[:, :], nwqk[:, :], [nwqk], [nwt])
    k.op("dve", lambda en: en.memset(ones512[:, :], 1.0), [], [ones512])
    load_w(0)
    for _ in inproj(0):
        pass
    for j in range(nheads):
        qT, kT, vtok, gs = arrs[j % 2]
        def chain(tq, j=j):
            st_ = sets[tq % NS]
            ZP, TA = st_.ZP, st_.TA
            TRb = TA[:, 0:256].bitcast(BF16)
            AV = TA[:, 256:384]
            E1, SP, PRE, ARG, Aw, AT, car, Oa = st_.E1, st_.SP, st_.PRE, st_.E1, st_.Aw, st_.AT, st_.car, st_.Oa
            k.op("dve", lambda en: en.memset(car[:, 0:1], 0.0), [], [car])
            t1 = tq + 1
            first = True
            while t1 > 0:
                t0 = max(0, t1 - 4)
                n = t1 - t0
                W = n * 128
                diag = (t1 - 1 == tq)
                mm(k, ZP[:, 0:W], qT[:, tq * 128:(tq + 1) * 128], kT[:, t0 * 128:t1 * 128], True, True,
                   [qT, kT], [ZP])
                yield
                act(k, E1[:, 0:W], ZP[:, 0:W], AF.Exp, [ZP], [E1])
                yield
                act(k, SP[:, 0:W], E1[:, 0:W], AF.Ln, [E1], [SP], bias=1.0)
                yield
                if diag:
                    tt(k, "pool", SP[:, W - 128:W], SP[:, W - 128:W], c["Ls"](), ALU.mult, [SP, cf], [SP])
                    yield
                k.op("dve", lambda en: en.tensor_tensor_scan(out=PRE[:, 0:W], data0=ones512[:, 0:W],
                                                             data1=SP[:, 0:W], initial=0.0, op0=ALU.mult,
                                                             op1=ALU.add), [ones512, SP], [PRE])
                tt(k, "dve", ARG[:, 0:W], ZP[:, 0:W], SP[:, 0:W], ALU.subtract, [ZP, SP], [ARG])
                yield
                tt(k, "pool", ARG[:, 0:W], ARG[:, 0:W], PRE[:, 0:W], ALU.add, [ARG, PRE], [ARG])
                tt(k, "dve", car[:, 1:2], car[:, 0:1], PRE[:, W - 1:W], ALU.add, [car, PRE], [car])
                ts(k, "dve", car[:, 2:3], car[:, 1:2], -1.0, ALU.mult, [car], [car])
                yield
                act(k, Aw[:, 0:W], ARG[:, 0:W], AF.Exp, [ARG, car], [Aw], bias=car[:, 2:3])
                yield
                cp(k, "dve", car[:, 0:1], car[:, 1:2], [car], [car])
                if diag:
                    tt(k, "pool", Aw[:, W - 128:W], Aw[:, W - 128:W], c["b"][:, 384:512], ALU.mult, [Aw, cb], [Aw])
                    yield
                for s_ in range(n):
                    tr(k, TRb[:, s_ * 128:(s_ + 1) * 128], Aw[:, s_ * 128:(s_ + 1) * 128], c["Ib"](), [Aw, cb], [TA])
                yield
                cp(k, "dve", AT[:, 0:W], TRb[:, 0:W], [TA], [AT])
                yield
                for s_ in range(n):
                    mm(k, AV, AT[:, s_ * 128:(s_ + 1) * 128], vtok[:, t0 + s_, :], s_ == 0, s_ == n - 1,
                       [AT, vtok], [TA])
                yield
                if first:
                    cp(k, "dve", Oa[:, :], AV, [TA], [Oa])
                else:
                    tt(k, "dve", Oa[:, :], Oa[:, :], AV, ALU.add, [Oa, TA], [Oa])
                first = False
                t1 = t0
                yield
            tt(k, "pool", st_.og[:, :], Oa[:, :], gs[:, tq, :], ALU.mult, [Oa, gs], [st_.og])
            yield
            tr(k, TRb[:, 0:128], st_.og[:, :], c["Ib"](), [st_.og, cb], [TA])
            yield
            cp(k, "act", oT[:, tq * 128:(tq + 1) * 128], TRb[:, 0:128], [TA], [oT])
            yield

        bg = inproj(j + 1) if j + 1 < nheads else None
        run_rr([(lambda tq=tq: chain(tq)) for tq in range(nt if not DBG_SKIP else 0)], NS, bg)
        out_cb(j, oT)


PCS1 = [(0, 4), (4, 8), (8, 11)]
PCS3 = [(0, 6), (6, 11)]


def _piece_of(pcs, tb):
    for i, (a, b) in enumerate(pcs):
        if a <= tb < b:
            return i, tb - a
    raise ValueError(tb)


def build_fused(nb=NB, nh=8):
    nc, k, cd = _new()
    nt = nb * 3
    h0 = k.dram("h0", [nb * TB, D], F32, kind="ExternalInput")
    h0c = k.dram("h0c", [nb * TB, 1024], F32, kind="ExternalInput")
    nw0 = k.dram("nw0", [128, D], F32, kind="ExternalInput")
    win0 = k.dram("win0", [nh, D, DNW], F32, kind="ExternalInput")
    cw = k.dram("cw", [128, nh, 4, 4], F32, kind="ExternalInput")
    hp = k.dram("hp", [128, nh, 4], F32, kind="ExternalInput")
    onw = k.dram("onw", [128, 256], F32, kind="ExternalInput")
    wout0 = k.dram("wout0", [nh * 4 * 256, 1024], F32, kind="ExternalInput")
    nw1 = k.dram("nw1", [128, 1024], F32, kind="ExternalInput")
    win1 = k.dram("win1", [nh, D, 512], F32, kind="ExternalInput")
    nwqk = k.dram("nwqk", [128, 2], F32, kind="ExternalInput")
    wout1 = k.dram("wout1", [nh * 4 * 128, 1024], F32, kind="ExternalInput")
    out = k.dram("out", [SEQ, 1024], F32, kind="ExternalOutput")
    hnT = k.dram("hnT", [nb, 128, NC_, TB], BF16)
    h1 = k.dram("h1", [nb * TB, 1024], F32)
    pcs1 = [(a, min(b, nb)) for (a, b) in PCS1 if a < nb]
    pcs3 = [(a, min(b, nb)) for (a, b) in PCS3 if a < nb]
    A1, G1, A2, G2, A3, G3 = [Buf(k, None, n) for n in ("A1", "G1", "A2", "G2", "A3", "G3")]
    a1 = [[nc.dram_tensor("a1_%d_%d" % (j, i), [(b - a) * 128, 768], BF16, kind="Internal").ap()
           for i, (a, b) in enumerate(pcs1)] for j in range(nh)]
    g1 = [[nc.dram_tensor("g1_%d_%d" % (j, i), [4 * (b - a) * 128, 768], BF16, kind="Internal").ap()
           for i, (a, b) in enumerate(pcs1)] for j in range(nh)]
    a2 = [nc.dram_tensor("a2_%d" % t, [128, 8 * TB], BF16, kind="Internal").ap() for t in range(nb)]
    g2 = [nc.dram_tensor("g2_%d" % t, [4 * 128, 8 * TB], BF16, kind="Internal").ap() for t in range(nb)]
    a3 = [[nc.dram_tensor("a3_%d_%d" % (j, i), [(b - a) * 128, TB], BF16, kind="Internal").ap()
           for i, (a, b) in enumerate(pcs3)] for j in range(nh)]
    g3 = [[nc.dram_tensor("g3_%d_%d" % (j, i), [4 * (b - a) * 128, TB], BF16, kind="Internal").ap()
           for i, (a, b) in enumerate(pcs3)] for j in range(nh)]
    ssq_in = k.dram("ssq_in", [128, nt], F32)
    ssq_all = k.dram("ssq_all", [4 * 128, nt], F32)

    wb0 = k.dram("wb0", [nh * 4 * 256, 1024], BF16)
    wb1 = k.dram("wb1", [nh * 4 * 128, 1024], BF16)
    def precast(j):
        for (src, dst) in ((wout0, wb0), (wout1, wb1)):
            per = src.t.shape[0] // nh
            r0 = j * per
            for q in range(r0, r0 + per, 1024):
                q1 = min(r0 + per, q + 1024)
                k.dma("pool", dst[q:q1, :], src[q:q1, :], [src], [dst])

    k.phase_begin()
    c = load_consts(k, cd, "a")
    phase_norm_T(k, c, h0, nw0, hnT, nb, "n")
    k.phase_end()

    def dn_out(j, tb, stg):
        i, tl = _piece_of(pcs1, tb)
        dst = a1[j][i].rearrange("(t p) (h x) -> p t h x", p=128, h=2)
        k.dma("sp", dst[:, tl, :, :], stg[:, :, :], [stg], [A1])
        if tb == pcs1[i][1] - 1:
            k.cc(a1[j][i][:, :], g1[j][i][:, :], [A1], [G1], G1)

    k.phase_begin()
    c = load_consts(k, cd, "b")
    phase_dn2(k, c, hnT, win0, cw, hp, onw, nb, nh, "d", dn_out, bg_cb=precast)
    k.phase_end()

    def load_x2(tb, xbb):
        i, tl = _piece_of(pcs1, tb)
        dstv = xbb[:, :, :].rearrange("p (r j h) x -> p r j h x", r=4, h=2)
        for j in range(nh):
            src = g1[j][i].rearrange("(r t p) (h x) -> p r t h x", r=4, p=128, h=2)
            k.dma("sp", dstv[:, :, j, :, :], src[:, :, tl, :, :], [G1], [xbb])

    k.phase_begin()
    P = PsumSet(k, "p2")
    phase_outproj(k, P, None, [nh * 2] * 4, wb0, h0c, h1, nb, "o", ssq_out=ssq_in, load_cb=load_x2, wq="sp")
    k.cc(ssq_in[:, :], ssq_all[:, :], [ssq_in], [ssq_all], ssq_all)
    k.phase_end()

    G2p = [Buf(k, None, "G2p%d" % t) for t in range(nb)]

    def n2_out(blk, stg):
        k.dma("sp", a2[blk].rearrange("p (c x) -> p c x", c=8), stg[:, :, :], [stg], [A2])
        k.cc(a2[blk][:, :], g2[blk][:, :], [A2], [G2], G2)

    k.phase_begin()
    c = load_consts(k, cd, "c")
    P = PsumSet(k, "p3")
    phase_norm2(k, c, P, ssq_all, h1, nw1, None, nb, "m",
                ssq_view=ssq_all[:, :].rearrange("(g p) n -> p g n", p=128), out_cb=n2_out)
    k.phase_end()

    def load_h3(tb, hb):
        k.dma("sp", hb[:, :, :].rearrange("p (r c) x -> p r c x", r=4),
              g2[tb].rearrange("(r p) (c x) -> p r c x", p=128, c=8), [G2], [hb])

    def sb_out(j, oT):
        for i, (a, b) in enumerate(pcs3):
            k.dma("sp", a3[j][i].rearrange("(t p) x -> p t x", p=128), v3(oT[:, a * TB:b * TB], TB), [oT], [A3])
        for i in range(len(pcs3)):
            k.cc(a3[j][i][:, :], g3[j][i][:, :], [A3], [G3], G3)

    k.phase_begin()
    c = load_consts(k, cd, "d")
    phase_sb2(k, c, win1, nwqk, nb, nh, "s", load_h3, sb_out)
    k.phase_end()

    def load_x4(tb, xbb):
        i, tl = _piece_of(pcs3, tb)
        dstv = xbb[:, :, :].rearrange("p (r j) x -> p r j x", r=4)
        for j in range(nh):
            src = g3[j][i].rearrange("(r t p) x -> p r t x", r=4, p=128)
            k.dma("sp", dstv[:, :, j, :], src[:, :, tl, :], [G3], [xbb])

    def rows(i):
        lo = max(i * 128, NMETA)
        hi = min((i + 1) * 128, NMETA + SEQ)
        if hi <= lo:
            return []
        return [(slice(lo - NMETA, hi - NMETA), slice(lo - i * 128, hi - i * 128))]

    k.phase_begin()
    P = PsumSet(k, "p5")
    phase_outproj(k, P, None, [nh] * 4, wb1, h1, out, nb, "q", out_rows=rows if nb == NB else None,
                  load_cb=load_x4, wq="sp", single=True)
    k.phase_end()
    k.finish()
    return nc


_DBG = None
FUSED = True


def _run(nc, in_maps):
    res = run_bass_kernel_spmd(nc, in_maps, core_ids=list(range(8)))
    return res.results


def _rep(v, n=128):
    return np.ascontiguousarray(np.tile(np.asarray(v, np.float32)[None], (n, 1)))


def kernel(x, meta_tokens, dn_norm_w, dn_w_in, dn_conv_w, dn_a_log, dn_dt_bias, dn_out_norm_w, dn_w_out,
           sb_norm_w, sb_w_in, sb_q_norm_w, sb_k_norm_w, sb_w_out):
    f32 = np.float32
    x = np.asarray(x, f32)
    cst = consts_np()
    h0p = []
    for b in range(2):
        h = np.zeros((T, D), f32)
        h[:NMETA] = np.asarray(meta_tokens, f32)
        h[NMETA:NMETA + SEQ] = x[b]
        h0p.append(h)
    w_in = np.asarray(dn_w_in, f32)[0]
    convw = np.asarray(dn_conv_w, f32)[0]
    a_log = np.asarray(dn_a_log, f32)[0]
    dtb = np.asarray(dn_dt_bias, f32)[0]
    onw = np.asarray(dn_out_norm_w, f32)[0]
    KQ, KV = 4096, 8192
    dn_win, dn_cw, dn_hp = [], [], []
    for g in range(4):
        ws, cws, hps = [], [], []
        for j in range(8):
            J = 8 * g + j
            cols = np.concatenate([np.arange(J * 128, (J + 1) * 128), KQ + np.arange(J * 128, (J + 1) * 128),
                                   2 * KQ + np.arange(2 * J * 128, (2 * J + 2) * 128),
                                   2 * KQ + KV + np.arange(2 * J * 128, (2 * J + 2) * 128),
                                   2 * KQ + 2 * KV + np.arange(2 * J, 2 * J + 2),
                                   2 * KQ + 2 * KV + 64 + np.arange(2 * J, 2 * J + 2)])
            ws.append(w_in[:, cols])
            cws.append(convw[:, cols[:512]].reshape(4, 4, 128).transpose(2, 1, 0))
            hps.append(np.concatenate([dtb[2 * J:2 * J + 2], a_log[2 * J:2 * J + 2]]))
        dn_win.append(np.ascontiguousarray(np.stack(ws)))
        dn_cw.append(np.ascontiguousarray(np.stack(cws, axis=1)))
        dn_hp.append(np.ascontiguousarray(np.tile(np.stack(hps)[None], (128, 1, 1))))
    dn_nw_r = _rep(np.asarray(dn_norm_w, f32)[0])
    onw_r = _rep(np.concatenate([onw, onw]))
    w_out0 = np.asarray(dn_w_out, f32)[0]
    sbw = np.asarray(sb_w_in, f32)[0]
    sb_win = []
    for g in range(4):
        ws = []
        for j in range(8):
            H = 8 * g + j
            cols = np.concatenate([q * 4096 + np.arange(H * 128, (H + 1) * 128) for q in range(4)])
            ws.append(sbw[:, cols])
        sb_win.append(np.ascontiguousarray(np.stack(ws)))
    sb_nw = np.asarray(sb_norm_w, f32)[0]
    nwqk = np.ascontiguousarray(np.stack([np.asarray(sb_q_norm_w, f32)[0], np.asarray(sb_k_norm_w, f32)[0]], 1))
    w_out1 = np.asarray(sb_w_out, f32)[0]

    cores = [(c // 4, c % 4) for c in range(8)]
    if FUSED:
        rf = _run(build_fused(), [{
            "consts": cst, "h0": h0p[b], "h0c": np.ascontiguousarray(h0p[b][:, g * 1024:(g + 1) * 1024]),
            "nw0": dn_nw_r, "win0": dn_win[g], "cw": dn_cw[g], "hp": dn_hp[g], "onw": onw_r,
            "wout0": np.ascontiguousarray(w_out0[:, g * 1024:(g + 1) * 1024]),
            "nw1": _rep(sb_nw[g * 1024:(g + 1) * 1024]), "win1": sb_win[g], "nwqk": nwqk,
            "wout1": np.ascontiguousarray(w_out1[:, g * 1024:(g + 1) * 1024])} for (b, g) in cores])
        out = np.empty((2, SEQ, D), f32)
        for ci, (b, g) in enumerate(cores):
            out[b][:, g * 1024:(g + 1) * 1024] = np.asarray(rf[ci]["out"])
        return out
    r1 = _run(build_l1(), [{"consts": cst, "h0": h0p[b], "nw": dn_nw_r, "win": dn_win[g], "cw": dn_cw[g],
                            "hp": dn_hp[g], "onw": onw_r} for (b, g) in cores])
    o0T = [np.asarray(r["o0T"]) for r in r1]
    r2 = _run(build_outproj(16, False),
              [dict({"wout": np.ascontiguousarray(w_out0[:, g * 1024:(g + 1) * 1024]),
                     "hres": np.ascontiguousarray(h0p[b][:, g * 1024:(g + 1) * 1024])},
                    **{"xT%d" % gg: o0T[4 * b + gg] for gg in range(4)}) for (b, g) in cores])
    h1 = [np.asarray(r["h1"]) for r in r2]
    ssq = [np.asarray(r["ssq"]) for r in r2]
    r3 = _run(build_l3(), [{"consts": cst, "ssq_all": np.ascontiguousarray(np.stack(ssq[4 * b:4 * b + 4])),
                            "h1": h1[4 * b + g], "nw": _rep(sb_nw[g * 1024:(g + 1) * 1024])} for (b, g) in cores])
    hn1T = [np.asarray(r["hn1T"]) for r in r3]
    r4 = _run(build_l4(), [dict({"consts": cst, "win": sb_win[g], "nwqk": nwqk},
                                **{"hnT%d" % gg: hn1T[4 * b + gg] for gg in range(4)}) for (b, g) in cores])
    o1T = [np.asarray(r["o1T"]) for r in r4]
    r5 = _run(build_outproj(8, True),
              [dict({"wout": np.ascontiguousarray(w_out1[:, g * 1024:(g + 1) * 1024]),
                     "hres": h1[4 * b + g]},
                    **{"xT%d" % gg: o1T[4 * b + gg] for gg in range(4)}) for (b, g) in cores])
    if _DBG is not None:
        _DBG.update(o0T=o0T, h1=h1, ssq=ssq, hn1T=hn1T, o1T=o1T)
    out = np.empty((2, SEQ, D), f32)
    for ci, (b, g) in enumerate(cores):
        out[b][:, g * 1024:(g + 1) * 1024] = np.asarray(r5[ci]["out"])
    return out
```

```python
import contextlib
import numpy as np
import ml_dtypes
import concourse.bass as bass
import concourse.mybir as mybir
from concourse.bass_utils import run_bass_kernel_spmd

F32 = mybir.dt.float32
BF16 = mybir.dt.bfloat16
AF = mybir.ActivationFunctionType
ALU = mybir.AluOpType

D = 4096
NC_ = 32
SEQ = 4096
NMETA = 16
T = 4224
NT = 33
TB = 384
NB = 11
EPS = 1e-6
RGROUPS = [[0, 1, 2, 3], [4, 5, 6, 7]]
DNW = 772


class Tok:
    __slots__ = ("sem", "val")

    def __init__(self, sem, val):
        self.sem = sem
        self.val = val


class Buf:
    def __init__(self, k, t, name):
        self.k = k
        self.t = t
        self.name = name
        self.w = None
        self.r = {}
        self.dsem = None
        self.dcnt = 0
        self.root = self
        self.excl = False

    def sub(self, ap, name):
        b = Buf(self.k, ap, name)
        b.root = self.root
        return b

    def __getitem__(self, idx):
        return self.t[idx]


class K:
    def __init__(self, nc):
        self.nc = nc
        self.es = contextlib.ExitStack()
        self.eng = {"pe": nc.tensor, "act": nc.scalar, "dve": nc.vector,
                    "pool": nc.gpsimd, "sp": nc.sync}
        self.esem = {}
        self.ecnt = {}
        self.seen = {}
        for e in self.eng:
            self.esem[e] = self.es.enter_context(nc.semaphore("es_" + e))
            self.ecnt[e] = 0
            self.seen[e] = {}
        self.dbufs = []
        self.n = 0
        self.pst = None

    def phase_begin(self):
        self.pst = contextlib.ExitStack()

    def phase_end(self):
        self.barrier()
        self.pst.close()
        self.pst = None

    def sb(self, name, shape, dt):
        st = self.pst if self.pst is not None else self.es
        t = st.enter_context(self.nc.sbuf_tensor(name, list(shape), dt))
        return Buf(self, t, name)

    def ps(self, name, shape, dt):
        st = self.pst if self.pst is not None else self.es
        t = st.enter_context(self.nc.psum_tensor(name, list(shape), dt))
        b = Buf(self, t, name)
        b.excl = True
        return b

    def dram(self, name, shape, dt, kind="Internal"):
        t = self.nc.dram_tensor(name, list(shape), dt, kind=kind)
        return Buf(self, t.ap(), name)

    def _wait(self, e, tok):
        if tok is None:
            return
        if e == "pe" and tok.sem is self.esem["pe"]:
            return
        key = id(tok.sem)
        if self.seen[e].get(key, 0) >= tok.val:
            return
        self.seen[e][key] = tok.val
        self.eng[e].wait_ge(tok.sem, tok.val)

    def _deps(self, e, reads, writes):
        for b in reads:
            self._wait(e, b.w)
        for b in writes:
            self._wait(e, b.w)
            for t in b.r.values():
                self._wait(e, t)

    def _mark(self, tok, reads, writes):
        for b in reads:
            b.r[id(tok.sem)] = tok
        for b in writes:
            b.w = tok
            b.r = {}

    def _norm(self, reads, writes):
        rs = [b.root for b in reads]
        ws = [b.root for b in writes]
        ws = ws + [b for b in rs if b.excl]
        rs = [b for b in rs if not b.excl]
        return rs, ws

    def op(self, e, fn, reads, writes):
        reads, writes = self._norm(reads, writes)
        self._deps(e, reads, writes)
        ins = fn(self.eng[e])
        self.ecnt[e] += 1
        ins.then_inc(self.esem[e], 1)
        self._mark(Tok(self.esem[e], self.ecnt[e]), reads, writes)
        self.n += 1

    def dma(self, q, out_ap, in_ap, reads, writes, owner=None):
        ow = (owner if owner is not None else writes[0]).root
        reads, writes = self._norm(reads, writes)
        if ow.dsem is None:
            ow.dsem = self.es.enter_context(self.nc.semaphore("ds_" + ow.name))
            self.dbufs.append(ow)
        self._deps(q, reads, writes)
        ow.dcnt += 16
        self.eng[q].dma_start(out=out_ap, in_=in_ap).then_inc(ow.dsem, 16)
        self._mark(Tok(ow.dsem, ow.dcnt), reads, writes)
        self.n += 1

    def cc(self, in_ap, out_ap, reads, writes, owner):
        ow = owner.root
        reads, writes = self._norm(reads, writes)
        if ow.dsem is None:
            ow.dsem = self.es.enter_context(self.nc.semaphore("cs_" + ow.name))
            self.dbufs.append(ow)
        for b in reads:
            self._wait("pool", b.w)
        for b in writes:
            for t in b.r.values():
                self._wait("pool", t)
        ow.dcnt += 1
        self.eng["pool"].collective_compute("AllGather", ALU.bypass, replica_groups=RGROUPS,
                                            ins=[in_ap], outs=[out_ap]).then_inc(ow.dsem)
        self._mark(Tok(ow.dsem, ow.dcnt), reads, writes)
        self.n += 1

    def barrier(self):
        for e in self.eng:
            for e2 in self.eng:
                if e2 != e and self.ecnt[e2] > 0:
                    self._wait(e, Tok(self.esem[e2], self.ecnt[e2]))
            for b in self.dbufs:
                if b.dcnt:
                    self._wait(e, Tok(b.dsem, b.dcnt))

    def finish(self):
        self.barrier()
        self.es.close()


def act(k, out, in_, func, reads, writes, bias=None, scale=None, accum=None, e="act"):
    kw = {}
    if bias is not None:
        kw["bias"] = bias
    if scale is not None:
        kw["scale"] = scale
    if accum is not None:
        kw["accum_out"] = accum
    k.op(e, lambda en: en.activation(out=out, in_=in_, func=func, **kw), reads, writes)


def mm(k, out, lhsT, rhs, start, stop, reads, writes):
    k.op("pe", lambda en: en.matmul(out, lhsT, rhs, start=start, stop=stop), reads, writes)


def tr(k, out, in_, ident, reads, writes):
    k.op("pe", lambda en: en.transpose(out, in_, ident), reads, writes)


def tt(k, e, out, in0, in1, op, reads, writes):
    k.op(e, lambda en: en.tensor_tensor(out=out, in0=in0, in1=in1, op=op), reads, writes)


def ts(k, e, out, in0, s1, op0, reads, writes, s2=None, op1=None):
    if op1 is None:
        k.op(e, lambda en: en.tensor_scalar(out=out, in0=in0, scalar1=s1, scalar2=None, op0=op0),
             reads, writes)
    else:
        k.op(e, lambda en: en.tensor_scalar(out=out, in0=in0, scalar1=s1, scalar2=s2, op0=op0,
                                            op1=op1), reads, writes)


def stt(k, out, in0, scalar, in1, op0, op1, reads, writes):
    k.op("dve", lambda en: en.scalar_tensor_tensor(out=out, in0=in0, scalar=scalar, in1=in1,
                                                   op0=op0, op1=op1), reads, writes)


def cp(k, e, out, in_, reads, writes):
    if e == "act":
        k.op("act", lambda en: en.copy(out=out, in_=in_), reads, writes)
    else:
        k.op(e, lambda en: en.tensor_copy(out=out, in_=in_), reads, writes)


def load_consts(k, cdram, pfx=""):
    c = {}
    cf = k.sb(pfx + "c_f32", [128, 5 * 128], F32)
    k.dma("sp", cf[:, :], cdram[:, :], [], [cf])
    c["f"] = cf
    cb = k.sb(pfx + "c_bf", [128, 5 * 128], BF16)
    cp(k, "dve", cb[:, :], cf[:, :], [cf], [cb])
    c["b"] = cb
    c["I"] = lambda t=cf: t[:, 0:128]
    c["I2"] = lambda t=cf: t[:, 0:256]
    c["U"] = lambda t=cf: t[:, 256:384]
    c["Ls"] = lambda t=cf: t[:, 384:512]
    c["ones"] = lambda t=cf: t[:, 512:640]
    c["Ib"] = lambda t=cb: t[:, 0:128]
    c["onesb"] = lambda t=cb: t[:, 512:640]
    return c


def consts_np():
    I = np.eye(128, dtype=np.float32)
    r = np.arange(128)
    U = (r[:, None] <= r[None, :]).astype(np.float32)
    Ls = (r[:, None] > r[None, :]).astype(np.float32)
    ones = np.ones((128, 128), np.float32)
    return np.ascontiguousarray(np.concatenate([I, I, U, Ls, ones], axis=1))


def phase_norm_T(k, c, src, nw, dst, nb, pfx):
    wB = k.sb(pfx + "wB", [128, D], F32)
    k.dma("sp", wB[:, :], nw[:, :], [nw], [wB])
    xts = [k.sb(pfx + "xt%d" % i, [128, D], F32) for i in range(2)]
    hnb = [k.sb(pfx + "hnb%d" % i, [128, D], BF16) for i in range(2)]
    junk = k.sb(pfx + "junk", [128, D], BF16)
    st = [k.sb(pfx + "st%d" % i, [128, NC_, TB], BF16) for i in range(2)]
    ss = k.sb(pfx + "ss", [128, 4], F32)
    tp = [k.ps(pfx + "tp%d" % i, [128, 8, 128], BF16) for i in range(2)]
    ev = 0
    for blk in range(nb):
        stg = st[blk % 2]
        for tl in range(3):
            i = blk * 3 + tl
            xt = xts[i % 2]
            hb = hnb[i % 2]
            k.dma("sp", xt[:, :], src[i * 128:(i + 1) * 128, :], [src], [xt])
            act(k, junk[:, :], xt[:, :], AF.Square, [xt], [junk, ss], accum=ss[:, 0:1])
            act(k, ss[:, 1:2], ss[:, 0:1], AF.Sqrt, [ss], [ss], bias=EPS, scale=1.0 / D)
            k.op("dve", lambda en: en.reciprocal(out=ss[:, 2:3], in_=ss[:, 1:2]), [ss], [ss])
            stt(k, hb[:, :], xt[:, :], ss[:, 2:3], wB[:, :], ALU.mult, ALU.mult, [xt, ss, wB], [hb])
            for g in range(4):
                tpp = tp[g % 2]
                for cc in range(8):
                    ch = g * 8 + cc
                    tr(k, tpp[:, cc, :], hb[:, ch * 128:(ch + 1) * 128], c["Ib"](), [hb, c["b"]], [tpp])
                e = "act" if ev % 2 == 0 else "dve"
                ev += 1
                cp(k, e, stg[:, g * 8:(g + 1) * 8, tl * 128:(tl + 1) * 128], tpp[:, :, :], [tpp], [stg])
        k.dma("sp", dst[blk], stg[:, :, :], [stg], [dst])


def v3(ap, b=128):
    return ap.rearrange("p (a b) -> p a b", b=b)


class PsumSet:
    def __init__(self, k, pfx):
        self.full = [k.ps(pfx + "pf%d" % i, [128, 512], F32) for i in range(3)]
        self.half = []
        for i in range(5):
            b = k.ps(pfx + "pb%d" % i, [128, 512], F32)
            self.half.append(b.sub(b.t[:, 0:256], pfx + "ph%da" % i))
            self.half.append(b.sub(b.t[:, 256:512], pfx + "ph%db" % i))


DN_STAGE = 99
DBG_SKIP = False


def phase_dn(k, c, P, hnT, win, cw, hp, onw, o0T, nb, nheads, pfx, dbg=None, out_cb=None):
    nt = nb * 3
    TT = nt * 128
    mmA, mmB, DD = P.full
    GBg, KQ, KS, CHA, CHB, PP, VN, OO, SU, TR = P.half
    cf, cb = c["f"], c["b"]
    hs_ = [slice(0, 128), slice(128, 256)]

    w_sb = k.sb(pfx + "w", [128, NC_, DNW], BF16)
    hbk = [k.sb(pfx + "hb%d" % i, [128, NC_, TB], BF16) for i in range(2)]
    kq = k.sb(pfx + "kq", [128, nt, 256], BF16)
    vT = k.sb(pfx + "vT", [128, 2, TT], BF16)
    zsw = k.sb(pfx + "zsw", [128, nt, 256], BF16)
    oT = k.sb(pfx + "oT", [128, 2, TT], BF16)
    betas = k.sb(pfx + "beta", [128, nt, 2], F32)
    negb = k.sb(pfx + "negb", [128, nt, 2], F32)
    alog = k.sb(pfx + "alog", [128, nt, 2], F32)
    gstep = k.sb(pfx + "gstep", [128, nt, 2], F32)
    cwt = k.sb(pfx + "cw", [128, nheads, 4, 4], F32)
    hpt = k.sb(pfx + "hp", [128, nheads, 4], F32)
    negA = k.sb(pfx + "negA", [128, nheads, 2], F32)
    onwt = k.sb(pfx + "onw", [128, 256], F32)
    rawb = [k.sb(pfx + "raw%d" % i, [128, TB + 3], F32) for i in range(4)]
    acc = k.sb(pfx + "acc", [128, TB], F32)
    yb = k.sb(pfx + "y", [128, TB], F32)
    sqb = k.sb(pfx + "sq", [128, TB], BF16)
    sb_ = k.sb(pfx + "s", [128, TB], F32)
    rn = k.sb(pfx + "rn", [128, TB], F32)
    zt = k.sb(pfx + "zt", [128, 256], BF16)
    RG1 = k.sb(pfx + "RG1", [128, 256], F32)
    RG2 = k.sb(pfx + "RG2", [128, 256], F32)
    decs = k.sb(pfx + "decs", [128, 512], F32)
    EB = k.sb(pfx + "EB", [128, 256], F32)
    eg = k.sb(pfx + "eg", [128, 4], F32)
    t1 = k.sb(pfx + "t1", [128, 256], F32)
    t2 = k.sb(pfx + "t2", [128, 256], F32)
    Ab = [k.sb(pfx + "A%d" % i, [128, 256], F32) for i in range(2)]
    Bb = [k.sb(pfx + "B%d" % i, [128, 256], F32) for i in range(2)]
    Pb = [k.sb(pfx + "P%d" % i, [128, 256], F32) for i in range(2)]
    attnT = k.sb(pfx + "attnT", [128, 256], BF16)
    WT = k.sb(pfx + "WT", [128, 256], BF16)
    qdecT = k.sb(pfx + "qdecT", [128, 256], BF16)
    vtok = k.sb(pfx + "vtok", [128, 256], BF16)
    kdec = k.sb(pfx + "kdec", [128, 256], BF16)
    R = k.sb(pfx + "R", [128, 256], BF16)
    vn = k.sb(pfx + "vn", [128, 256], BF16)
    S = k.sb(pfx + "S", [128, 256], F32)
    Sb = k.sb(pfx + "Sb", [128, 256], BF16)
    ssq = k.sb(pfx + "ssq", [128, 8], F32)
    og = k.sb(pfx + "og", [128, 256], BF16)
    junk = k.sb(pfx + "junk", [128, 256], BF16)
    TRb = TR[:, :].bitcast(BF16)

    k.dma("sp", cwt[:, :, :, :], cw[:, :, :, :], [cw], [cwt])
    k.dma("sp", hpt[:, :, :], hp[:, :, :], [hp], [hpt])
    k.dma("sp", onwt[:, :], onw[:, :], [onw], [onwt])
    act(k, negA[:, :, :], hpt[:, :, 2:4], AF.Exp, [hpt], [negA])
    ts(k, "dve", negA[:, :, :], negA[:, :, :], -1.0, ALU.mult, [negA], [negA])

    bi = 0
    for j in range(nheads):
        for q4 in range(4):
            k.dma("pool", w_sb[:, q4 * 8:(q4 + 1) * 8, :],
                  win[j, q4 * 1024:(q4 + 1) * 1024, :].rearrange("(c p) n -> p c n", p=128),
                  [win], [w_sb])
        for fc in range(4):
            k.op("dve", lambda en: en.memset(rawb[fc][:, 0:3], 0.0), [], [rawb[fc]])
        for tb in range(nb):
            hb = hbk[bi % 2]
            bi += 1
            k.dma("sp", hb[:, :, :], hnT[tb], [hnT], [hb])
            for fc in range(4):
                ps = mmA if fc % 2 == 0 else mmB
                for cc in range(NC_):
                    mm(k, ps[:, 0:TB], w_sb[:, cc, fc * 128:(fc + 1) * 128], hb[:, cc, :],
                       cc == 0, cc == NC_ - 1, [w_sb, hb], [ps])
                rb = rawb[fc]
                cp(k, "act", rb[:, 3:TB + 3], ps[:, 0:TB], [ps], [rb])
                ts(k, "dve", acc[:, :], rb[:, 3:TB + 3], cwt[:, j, fc, 3:4], ALU.mult, [rb, cwt], [acc])
                for tap in (2, 1, 0):
                    stt(k, acc[:, :], rb[:, tap:tap + TB], cwt[:, j, fc, tap:tap + 1], acc[:, :],
                        ALU.mult, ALU.add, [rb, cwt, acc], [acc])
                cp(k, "dve", rb[:, 0:3], rb[:, TB:TB + 3], [rb], [rb])
                if fc >= 2:
                    act(k, vT[:, fc - 2, tb * TB:(tb + 1) * TB], acc[:, :], AF.Silu, [acc], [vT])
                else:
                    act(k, yb[:, :], acc[:, :], AF.Silu, [acc], [yb])
                    act(k, sqb[:, :], yb[:, :], AF.Square, [yb], [sqb])
                    mm(k, DD[:, 0:TB], c["onesb"](), sqb[:, :], True, True, [cb, sqb], [DD])
                    act(k, sb_[:, :], DD[:, 0:TB], AF.Sqrt, [DD], [sb_], bias=EPS)
                    k.op("dve", lambda en: en.reciprocal(out=rn[:, :], in_=sb_[:, :]), [sb_], [rn])
                    if fc == 0:
                        stt(k, kq[:, 3 * tb:3 * tb + 3, 128:256], v3(yb[:, :]), 128.0 ** -0.5, v3(rn[:, :]),
                            ALU.mult, ALU.mult, [yb, rn], [kq])
                    else:
                        tt(k, "dve", kq[:, 3 * tb:3 * tb + 3, 0:128], v3(yb[:, :]), v3(rn[:, :]), ALU.mult,
                           [yb, rn], [kq])
            for tl in range(3):
                i = 3 * tb + tl
                ps = mmA if tl % 2 == 0 else mmB
                for cc in range(NC_):
                    mm(k, ps[:, 0:260], hb[:, cc, tl * 128:(tl + 1) * 128], w_sb[:, cc, 512:772],
                       cc == 0, cc == NC_ - 1, [w_sb, hb], [ps])
                act(k, zt[:, :], ps[:, 0:256], AF.Silu, [ps], [zt])
                tt(k, "dve", zsw[:, i, :], zt[:, :], onwt[:, :], ALU.mult, [zt, onwt], [zsw])
                act(k, betas[:, i, :], ps[:, 256:258], AF.Sigmoid, [ps], [betas])
                tt(k, "dve", alog[:, i, :], ps[:, 258:260], hpt[:, j, 0:2], ALU.add, [ps, hpt], [alog])
        act(k, alog[:, :, :], alog[:, :, :], AF.Exp, [alog], [alog])
        act(k, alog[:, :, :], alog[:, :, :], AF.Ln, [alog], [alog], bias=1.0)
        for h in range(2):
            ts(k, "dve", gstep[:, :, h], alog[:, :, h], negA[:, j, h:h + 1], ALU.mult, [alog, negA], [gstep])
        ts(k, "dve", negb[:, :, :], betas[:, :, :], -1.0, ALU.mult, [betas], [negb])
        k.op("dve", lambda en: en.memset(S[:, :], 0.0), [], [S])
        k.op("dve", lambda en: en.memset(Sb[:, :], 0.0), [], [Sb])

        if dbg is not None:
            k.dma("sp", dbg["kq"][:, :, :], kq[:, :, :], [kq], [dbg["kq"]])
            k.dma("sp", dbg["vT"][:, :, :], vT[:, :, :], [vT], [dbg["vT"]])
            k.dma("sp", dbg["zsw"][:, :, :], zsw[:, :, :], [zsw], [dbg["zsw"]])
            k.dma("sp", dbg["beta"][:, :, :], betas[:, :, :], [betas], [dbg["beta"]])
            k.dma("sp", dbg["gstep"][:, :, :], gstep[:, :, :], [gstep], [dbg["gstep"]])
        for n in range(nt if DN_STAGE >= 2 else 0):
            ST = DN_STAGE
            kT = kq[:, n, 0:128]
            qT = kq[:, n, 128:256]
            for h in range(2):
                ts(k, "dve", RG1[:, hs_[h]], c["U"](), gstep[:, n, h:h + 1], ALU.mult, [cf, gstep], [RG1])
                ts(k, "pool", RG2[:, hs_[h]], c["Ls"](), gstep[:, n, h:h + 1], ALU.mult, [cf, gstep], [RG2])
            mm(k, DD[:, 0:256], c["U"](), RG2[:, :], True, True, [cf, RG2], [DD])
            mm(k, DD[:, 256:512], c["Ls"](), RG1[:, :], True, True, [cf, RG1], [DD])
            mm(k, GBg[:, :], c["ones"](), RG1[:, :], True, True, [cf, RG1], [GBg])
            mm(k, TR[:, 0:2], c["U"](), gstep[:, n, 0:2], True, True, [cf, gstep], [TR])
            act(k, decs[:, :], DD[:, :], AF.Exp, [DD], [decs])
            act(k, EB[:, :], GBg[:, :], AF.Exp, [GBg], [EB])
            act(k, eg[:, 0:2], TR[:, 0:2], AF.Exp, [TR], [eg])
            ts(k, "dve", eg[:, 2:4], eg[:, 0:2], -1.0, ALU.mult, [eg], [eg])
            if ST < 3:
                continue
            mm(k, KQ[:, :], kT, kq[:, n, :], True, True, [kq], [KQ])
            A0, B0 = Ab[0], Bb[0]
            for h in range(2):
                tt(k, "dve", t1[:, hs_[h]], KQ[:, 0:128], decs[:, h * 128:(h + 1) * 128], ALU.mult,
                   [KQ, decs], [t1])
                stt(k, A0[:, hs_[h]], t1[:, hs_[h]], negb[:, n, h:h + 1], c["Ls"](), ALU.mult, ALU.mult,
                    [t1, negb, cf], [A0])
                tt(k, "dve", t2[:, hs_[h]], KQ[:, 128:256], decs[:, 256 + h * 128:256 + (h + 1) * 128],
                   ALU.mult, [KQ, decs], [t2])
                tt(k, "pool", attnT[:, hs_[h]], t2[:, hs_[h]], c["U"](), ALU.mult, [t2, cf], [attnT])
            if ST < 4:
                continue
            for h in range(2):
                tr(k, TR[:, hs_[h]], A0[:, hs_[h]], c["I"](), [A0, cf], [TR])
            cp(k, "act", B0[:, :], TR[:, :], [TR], [B0])
            if ST < 5:
                continue
            ai, pi = 0, 0
            tt(k, "pool", Pb[0][:, :], c["I2"](), B0[:, :], ALU.add, [cf, B0], [Pb[0]])
            for m in range(1, 7):
                Ac, Bc = Ab[ai], Bb[ai]
                An, Bn = Ab[1 - ai], Bb[1 - ai]
                for h in range(2):
                    mm(k, CHA[:, hs_[h]], Bc[:, hs_[h]], Ac[:, hs_[h]], True, True, [Ac, Bc], [CHA])
                if m <= 5:
                    for h in range(2):
                        mm(k, CHB[:, hs_[h]], Ac[:, hs_[h]], Bc[:, hs_[h]], True, True, [Ac, Bc], [CHB])
                cp(k, "act", An[:, :], CHA[:, :], [CHA], [An])
                if m <= 5:
                    cp(k, "dve", Bn[:, :], CHB[:, :], [CHB], [Bn])
                Pc, Pn = Pb[pi], Pb[1 - pi]
                for h in range(2):
                    mm(k, PP[:, hs_[h]], An[:, hs_[h]], Pc[:, hs_[h]], True, True, [An, Pc], [PP])
                tt(k, "dve", Pn[:, :], Pc[:, :], PP[:, :], ALU.add, [Pc, PP], [Pn])
                ai, pi = 1 - ai, 1 - pi
            Pf = Pb[pi]
            for h in range(2):
                ts(k, "dve", WT[:, hs_[h]], Pf[:, hs_[h]], betas[:, n, h:h + 1], ALU.mult, [Pf, betas], [WT])
                tt(k, "pool", qdecT[:, hs_[h]], qT, EB[:, hs_[h]], ALU.mult, [kq, EB], [qdecT])
            if ST < 6:
                continue
            for h in range(2):
                tr(k, TRb[:, hs_[h]], vT[:, h, n * 128:(n + 1) * 128], c["Ib"](), [vT, cb], [TR])
            cp(k, "act", vtok[:, :], TRb[:, 0:256], [TR], [vtok])
            tr(k, TRb[:, 0:128], kT, c["Ib"](), [kq, cb], [TR])
            for h in range(2):
                col = 256 + h * 128 + 127
                ts(k, "dve", kdec[:, hs_[h]], TRb[:, 0:128], decs[:, col:col + 1], ALU.mult, [TR, decs], [kdec])
            if ST < 7:
                continue
            mm(k, KS[:, :], kT, Sb[:, :], True, True, [kq, Sb], [KS])
            for h in range(2):
                stt(k, R[:, hs_[h]], KS[:, hs_[h]], eg[:, 2 + h:3 + h], vtok[:, hs_[h]], ALU.mult, ALU.add,
                    [KS, eg, vtok], [R])
            for h in range(2):
                mm(k, VN[:, hs_[h]], WT[:, hs_[h]], R[:, hs_[h]], True, True, [WT, R], [VN])
            cp(k, "act", vn[:, :], VN[:, :], [VN], [vn])
            for h in range(2):
                mm(k, OO[:, hs_[h]], qdecT[:, hs_[h]], Sb[:, hs_[h]], True, False, [qdecT, Sb], [OO])
                mm(k, OO[:, hs_[h]], attnT[:, hs_[h]], vn[:, hs_[h]], False, True, [attnT, vn], [OO])
            for h in range(2):
                mm(k, SU[:, hs_[h]], kdec[:, hs_[h]], vn[:, hs_[h]], True, True, [kdec, vn], [SU])
            for h in range(2):
                col = h * 128 + 127
                stt(k, S[:, hs_[h]], S[:, hs_[h]], EB[:, col:col + 1], SU[:, hs_[h]], ALU.mult, ALU.add,
                    [S, EB, SU], [S])
            cp(k, "act", Sb[:, :], S[:, :], [S], [Sb])
            for h in range(2):
                act(k, junk[:, hs_[h]], OO[:, hs_[h]], AF.Square, [OO], [junk, ssq], accum=ssq[:, h:h + 1])
            act(k, ssq[:, 2:4], ssq[:, 0:2], AF.Sqrt, [ssq], [ssq], bias=EPS, scale=1.0 / 128)
            k.op("dve", lambda en: en.reciprocal(out=ssq[:, 4:6], in_=ssq[:, 2:4]), [ssq], [ssq])
            for h in range(2):
                stt(k, og[:, hs_[h]], OO[:, hs_[h]], ssq[:, 4 + h:5 + h], zsw[:, n, hs_[h]], ALU.mult, ALU.mult,
                    [OO, ssq, zsw], [og])
            for h in range(2):
                tr(k, TRb[:, hs_[h]], og[:, hs_[h]], c["Ib"](), [og, cb], [TR])
            cp(k, "act", oT[:, :, n * 128:(n + 1) * 128], v3(TRb[:, 0:256]), [TR], [oT])
        if out_cb is not None:
            out_cb(j, oT)
        else:
            for h in range(2):
                k.dma("sp", o0T[:, :, 2 * j + h, :].rearrange("t p x -> p t x"), v3(oT[:, h, :], TB), [oT], [o0T])


def run_chains(pre_gens, state_gen_fn, n, width):
    pres = {}
    nxt = 0
    done_pre = set()
    st = None
    st_i = 0
    while st_i < n:
        while nxt < n and len(pres) < width and nxt < st_i + width:
            pres[nxt] = pre_gens(nxt)
            nxt += 1
        for i in sorted(pres):
            try:
                next(pres[i])
            except StopIteration:
                done_pre.add(i)
                del pres[i]
        if st is None and st_i in done_pre:
            st = state_gen_fn(st_i)
        if st is not None:
            try:
                next(st)
            except StopIteration:
                st = None
                st_i += 1


def phase_dn2(k, c, hnT, win, cw, hp, onw, nb, nheads, pfx, out_cb, NS=3, bg_cb=None):
    nt = nb * 3
    TT = nt * 128
    banks = [k.ps(pfx + "bk%d" % i, [128, 512], F32) for i in range(8)]
    mmA, mmB, Z1, Z2 = banks[0:4]
    DD = Z2
    cf, cb = c["f"], c["b"]
    hs_ = [slice(0, 128), slice(128, 256)]

    w_sb = k.sb(pfx + "w", [128, NC_, DNW], BF16)
    hbk = [k.sb(pfx + "hb%d" % i, [128, NC_, TB], BF16) for i in range(2)]
    kq = k.sb(pfx + "kq", [128, nt, 256], BF16)
    vT = k.sb(pfx + "vT", [128, 2, TT], BF16)
    zsw = k.sb(pfx + "zsw", [128, nt, 256], BF16)
    ostg = [k.sb(pfx + "ostg%d" % i, [128, 2, TB], BF16) for i in range(2)]
    betas = k.sb(pfx + "beta", [128, nt, 2], F32)
    negb = k.sb(pfx + "negb", [128, nt, 2], F32)
    alog = k.sb(pfx + "alog", [128, nt, 2], F32)
    gstep = k.sb(pfx + "gstep", [128, nt, 2], F32)
    cwt = k.sb(pfx + "cw", [128, nheads, 4, 4], F32)
    hpt = k.sb(pfx + "hp", [128, nheads, 4], F32)
    negA = k.sb(pfx + "negA", [128, nheads, 2], F32)
    onwt = k.sb(pfx + "onw", [128, 256], F32)
    rawb = [k.sb(pfx + "raw%d" % i, [128, TB + 3], F32) for i in range(4)]
    sqb = k.sb(pfx + "sq", [128, TB], BF16)
    sb_ = k.sb(pfx + "s", [128, TB], F32)
    rn = k.sb(pfx + "rn", [128, TB], F32)
    zt = k.sb(pfx + "zt", [128, 256], BF16)
    junk = zt

    class Set:
        pass
    sets = []
    for i in range(NS):
        S_ = Set()
        S_.X, S_.Y = (banks[4 + 2 * i], banks[5 + 2 * i]) if i < 2 else (mmA, mmB)
        S_.X1 = k.sb(pfx + "X1_%d" % i, [128, 256], F32)
        S_.X2 = k.sb(pfx + "X2_%d" % i, [128, 256], F32)
        S_.decs = k.sb(pfx + "decs%d" % i, [128, 512], F32)
        S_.EB = k.sb(pfx + "EB%d" % i, [128, 256], F32)
        S_.eg = k.sb(pfx + "eg%d" % i, [128, 8], F32)
        S_.A = [k.sb(pfx + "A%d_%d" % (i, q), [128, 256], F32) for q in range(2)]
        S_.B = [k.sb(pfx + "B%d_%d" % (i, q), [128, 256], F32) for q in range(2)]
        S_.P = k.sb(pfx + "P%d" % i, [128, 256], F32)
        S_.attnT = k.sb(pfx + "attnT%d" % i, [128, 256], BF16)
        S_.WT = k.sb(pfx + "WT%d" % i, [128, 256], BF16)
        S_.qdecT = k.sb(pfx + "qdecT%d" % i, [128, 256], BF16)
        S_.vtok = k.sb(pfx + "vtok%d" % i, [128, 256], BF16)
        S_.kdec = k.sb(pfx + "kdec%d" % i, [128, 256], BF16)
        sets.append(S_)
    acc = sets[0].decs.sub(sets[0].decs.t[:, 0:TB], pfx + "acc")
    ybs = [sets[1 % NS].decs.sub(sets[1 % NS].decs.t[:, 0:TB], pfx + "yb0"),
           sets[2 % NS].decs.sub(sets[2 % NS].decs.t[:, 0:TB], pfx + "yb1")]
    deferred = []
    R = k.sb(pfx + "R", [128, 256], BF16)
    vn = k.sb(pfx + "vn", [128, 256], BF16)
    S = k.sb(pfx + "S", [128, 256], F32)
    Sb = k.sb(pfx + "Sb", [128, 256], BF16)
    ssq = k.sb(pfx + "ssq", [128, 8], F32)
    og = k.sb(pfx + "og", [128, 256], BF16)

    k.dma("sp", cwt[:, :, :, :], cw[:, :, :, :], [cw], [cwt])
    k.dma("sp", hpt[:, :, :], hp[:, :, :], [hp], [hpt])
    k.dma("sp", onwt[:, :], onw[:, :], [onw], [onwt])
    act(k, negA[:, :, :], hpt[:, :, 2:4], AF.Exp, [hpt], [negA])
    ts(k, "dve", negA[:, :, :], negA[:, :, :], -1.0, ALU.mult, [negA], [negA])

    def load_w(jj):
        for q4 in range(4):
            k.dma("pool", w_sb[:, q4 * 8:(q4 + 1) * 8, :],
                  win[jj, q4 * 1024:(q4 + 1) * 1024, :].rearrange("(c p) n -> p c n", p=128),
                  [win], [w_sb])

    bi = 0
    for j in range(nheads):
        if j == 0:
            load_w(0)
        for fc in range(4):
            k.op("dve", lambda en: en.memset(rawb[fc][:, 0:3], 0.0), [], [rawb[fc]])
        for tb in range(nb):
            hb = hbk[bi % 2]
            bi += 1
            k.dma("sp", hb[:, :, :], hnT[tb], [hnT], [hb])
            for fc in range(4):
                ps = mmA if fc % 2 == 0 else mmB
                for cc in range(NC_):
                    mm(k, ps[:, 0:TB], w_sb[:, cc, fc * 128:(fc + 1) * 128], hb[:, cc, :],
                       cc == 0, cc == NC_ - 1, [w_sb, hb], [ps])
                while deferred:
                    deferred.pop(0)()
                rb = rawb[fc]
                cp(k, "act", rb[:, 3:TB + 3], ps[:, 0:TB], [ps], [rb])
                ts(k, "dve", acc[:, :], rb[:, 3:TB + 3], cwt[:, j, fc, 3:4], ALU.mult, [rb, cwt], [acc])
                for tap in (2, 1, 0):
                    stt(k, acc[:, :], rb[:, tap:tap + TB], cwt[:, j, fc, tap:tap + 1], acc[:, :],
                        ALU.mult, ALU.add, [rb, cwt, acc], [acc])
                cp(k, "dve", rb[:, 0:3], rb[:, TB:TB + 3], [rb], [rb])
                if fc >= 2:
                    act(k, vT[:, fc - 2, tb * TB:(tb + 1) * TB], acc[:, :], AF.Silu, [acc], [vT])
                else:
                    ybf = ybs[fc]
                    act(k, ybf[:, :], acc[:, :], AF.Silu, [acc], [ybf])

                    def norm_part(fc=fc, tb=tb, ybf=ybf):
                        act(k, sqb[:, :], ybf[:, :], AF.Square, [ybf], [sqb])
                        mm(k, DD[:, 0:TB], c["onesb"](), sqb[:, :], True, True, [cb, sqb], [DD])
                        act(k, sb_[:, :], DD[:, 0:TB], AF.Sqrt, [DD], [sb_], bias=EPS)
                        k.op("dve", lambda en: en.reciprocal(out=rn[:, :], in_=sb_[:, :]), [sb_], [rn])
                        if fc == 0:
                            stt(k, kq[:, 3 * tb:3 * tb + 3, 128:256], v3(ybf[:, :]), 128.0 ** -0.5, v3(rn[:, :]),
                                ALU.mult, ALU.mult, [ybf, rn], [kq])
                        else:
                            tt(k, "dve", kq[:, 3 * tb:3 * tb + 3, 0:128], v3(ybf[:, :]), v3(rn[:, :]), ALU.mult,
                               [ybf, rn], [kq])
                    deferred.append(norm_part)
            for tl in range(3):
                i = 3 * tb + tl
                ps = mmA if tl % 2 == 0 else mmB
                for cc in range(NC_):
                    mm(k, ps[:, 0:260], hb[:, cc, tl * 128:(tl + 1) * 128], w_sb[:, cc, 512:772],
                       cc == 0, cc == NC_ - 1, [w_sb, hb], [ps])
                while deferred:
                    deferred.pop(0)()
                act(k, zt[:, :], ps[:, 0:256], AF.Silu, [ps], [zt])
                tt(k, "dve", zsw[:, i, :], zt[:, :], onwt[:, :], ALU.mult, [zt, onwt], [zsw])
                act(k, betas[:, i, :], ps[:, 256:258], AF.Sigmoid, [ps], [betas])
                tt(k, "dve", alog[:, i, :], ps[:, 258:260], hpt[:, j, 0:2], ALU.add, [ps, hpt], [alog])
        act(k, alog[:, :, :], alog[:, :, :], AF.Exp, [alog], [alog])
        act(k, alog[:, :, :], alog[:, :, :], AF.Ln, [alog], [alog], bias=1.0)
        for h in range(2):
            ts(k, "dve", gstep[:, :, h], alog[:, :, h], negA[:, j, h:h + 1], ALU.mult, [alog, negA], [gstep])
        ts(k, "dve", negb[:, :, :], betas[:, :, :], -1.0, ALU.mult, [betas], [negb])
        k.op("dve", lambda en: en.memset(S[:, :], 0.0), [], [S])
        k.op("dve", lambda en: en.memset(Sb[:, :], 0.0), [], [Sb])

        if j + 1 < nheads:
            load_w(j + 1)
        if bg_cb is not None:
            bg_cb(j)
        def pre(n, j=j):
            st_ = sets[n % NS]
            X, Y = st_.X, st_.Y
            Xb = X[:, :].bitcast(BF16)
            Yb = Y[:, :].bitcast(BF16)
            kT = kq[:, n, 0:128]
            qT = kq[:, n, 128:256]
            RG1, RG2, decs, EB, eg = st_.X1, st_.X2, st_.decs, st_.EB, st_.eg
            for h in range(2):
                ts(k, "dve", RG1[:, hs_[h]], c["U"](), gstep[:, n, h:h + 1], ALU.mult, [cf, gstep], [RG1])
                ts(k, "pool", RG2[:, hs_[h]], c["Ls"](), gstep[:, n, h:h + 1], ALU.mult, [cf, gstep], [RG2])
            yield
            mm(k, X[:, 0:256], c["U"](), RG2[:, :], True, True, [cf, RG2], [X])
            mm(k, X[:, 256:512], c["Ls"](), RG1[:, :], True, True, [cf, RG1], [X])
            mm(k, Y[:, 0:256], c["ones"](), RG1[:, :], True, True, [cf, RG1], [Y])
            mm(k, Y[:, 256:258], c["U"](), gstep[:, n, 0:2], True, True, [cf, gstep], [Y])
            yield
            act(k, decs[:, :], X[:, :], AF.Exp, [X], [decs])
            act(k, EB[:, :], Y[:, 0:256], AF.Exp, [Y], [EB])
            act(k, eg[:, 0:2], Y[:, 256:258], AF.Exp, [Y], [eg])
            yield
            ts(k, "dve", eg[:, 2:4], eg[:, 0:2], -1.0, ALU.mult, [eg], [eg])
            for h in range(2):
                col = h * 128 + 127
                cp(k, "dve", eg[:, 4 + h:5 + h], EB[:, col:col + 1], [EB], [eg])
            mm(k, X[:, 0:256], kT, kq[:, n, :], True, True, [kq], [X])
            yield
            A0, B0 = st_.A[0], st_.B[0]
            t1, t2 = RG1, RG2
            for h in range(2):
                tt(k, "dve", t1[:, hs_[h]], X[:, 0:128], decs[:, h * 128:(h + 1) * 128], ALU.mult, [X, decs], [t1])
                stt(k, A0[:, hs_[h]], t1[:, hs_[h]], negb[:, n, h:h + 1], c["Ls"](), ALU.mult, ALU.mult,
                    [t1, negb, cf], [A0])
            yield
            for h in range(2):
                tr(k, Y[:, hs_[h]], A0[:, hs_[h]], c["I"](), [A0, cf], [Y])
            for h in range(2):
                tt(k, "dve", t2[:, hs_[h]], X[:, 128:256], decs[:, 256 + h * 128:256 + (h + 1) * 128],
                   ALU.mult, [X, decs], [t2])
                tt(k, "pool", st_.attnT[:, hs_[h]], t2[:, hs_[h]], c["U"](), ALU.mult, [t2, cf], [st_.attnT])
            yield
            cp(k, "act", B0[:, :], Y[:, 0:256], [Y], [B0])
            yield
            Pm = st_.P
            tt(k, "pool", Pm[:, :], c["I2"](), B0[:, :], ALU.add, [cf, B0], [Pm])
            ai = 0
            for m in range(1, 7):
                Ac, Bc = st_.A[ai], st_.B[ai]
                An, Bn = st_.A[1 - ai], st_.B[1 - ai]
                for h in range(2):
                    mm(k, X[:, hs_[h]], Bc[:, hs_[h]], Ac[:, hs_[h]], True, True, [Ac, Bc], [X])
                if m <= 5:
                    for h in range(2):
                        mm(k, Y[:, hs_[h]], Ac[:, hs_[h]], Bc[:, hs_[h]], True, True, [Ac, Bc], [Y])
                yield
                cp(k, "act", An[:, :], X[:, 0:256], [X], [An])
                if m <= 5:
                    cp(k, "dve", Bn[:, :], Y[:, 0:256], [Y], [Bn])
                yield
                for h in range(2):
                    mm(k, X[:, 256 + h * 128:384 + h * 128], An[:, hs_[h]], Pm[:, hs_[h]], True, True, [An, Pm], [X])
                yield
                tt(k, "dve", Pm[:, :], Pm[:, :], X[:, 256:512], ALU.add, [Pm, X], [Pm])
                ai = 1 - ai
            yield
            for h in range(2):
                ts(k, "dve", st_.WT[:, hs_[h]], Pm[:, hs_[h]], betas[:, n, h:h + 1], ALU.mult, [Pm, betas], [st_.WT])
                tt(k, "pool", st_.qdecT[:, hs_[h]], qT, EB[:, hs_[h]], ALU.mult, [kq, EB], [st_.qdecT])
            for h in range(2):
                tr(k, Yb[:, hs_[h]], vT[:, h, n * 128:(n + 1) * 128], c["Ib"](), [vT, cb], [Y])
            tr(k, Xb[:, 0:128], kT, c["Ib"](), [kq, cb], [X])
            yield
            cp(k, "act", st_.vtok[:, :], Yb[:, 0:256], [Y], [st_.vtok])
            for h in range(2):
                col = 256 + h * 128 + 127
                ts(k, "dve", st_.kdec[:, hs_[h]], Xb[:, 0:128], decs[:, col:col + 1], ALU.mult, [X, decs], [st_.kdec])
            yield

        def state(n, j=j):
            st_ = sets[n % NS]
            eg = st_.eg
            kT = kq[:, n, 0:128]
            Z1b = Z1[:, :].bitcast(BF16)
            mm(k, Z1[:, 0:256], kT, Sb[:, :], True, True, [kq, Sb], [Z1])
            yield
            for h in range(2):
                stt(k, R[:, hs_[h]], Z1[:, hs_[h]], eg[:, 2 + h:3 + h], st_.vtok[:, hs_[h]], ALU.mult, ALU.add,
                    [Z1, eg, st_.vtok], [R])
            yield
            for h in range(2):
                mm(k, Z2[:, hs_[h]], st_.WT[:, hs_[h]], R[:, hs_[h]], True, True, [st_.WT, R], [Z2])
            yield
            cp(k, "act", vn[:, :], Z2[:, 0:256], [Z2], [vn])
            yield
            for h in range(2):
                mm(k, Z2[:, 256 + h * 128:384 + h * 128], st_.kdec[:, hs_[h]], vn[:, hs_[h]], True, True,
                   [st_.kdec, vn], [Z2])
            for h in range(2):
                mm(k, Z1[:, 256 + h * 128:384 + h * 128], st_.qdecT[:, hs_[h]], Sb[:, hs_[h]], True, False,
                   [st_.qdecT, Sb], [Z1])
                mm(k, Z1[:, 256 + h * 128:384 + h * 128], st_.attnT[:, hs_[h]], vn[:, hs_[h]], False, True,
                   [st_.attnT, vn], [Z1])
            yield
            for h in range(2):
                stt(k, S[:, hs_[h]], S[:, hs_[h]], eg[:, 4 + h:5 + h], Z2[:, 256 + h * 128:384 + h * 128],
                    ALU.mult, ALU.add, [S, eg, Z2], [S])
            yield
            cp(k, "act", Sb[:, :], S[:, :], [S], [Sb])
            for h in range(2):
                act(k, junk[:, hs_[h]], Z1[:, 256 + h * 128:384 + h * 128], AF.Square, [Z1], [junk, ssq],
                    accum=ssq[:, h:h + 1])
            yield
            act(k, ssq[:, 2:4], ssq[:, 0:2], AF.Sqrt, [ssq], [ssq], bias=EPS, scale=1.0 / 128)
            yield
            k.op("dve", lambda en: en.reciprocal(out=ssq[:, 4:6], in_=ssq[:, 2:4]), [ssq], [ssq])
            for h in range(2):
                stt(k, og[:, hs_[h]], Z1[:, 256 + h * 128:384 + h * 128], ssq[:, 4 + h:5 + h], zsw[:, n, hs_[h]],
                    ALU.mult, ALU.mult, [Z1, ssq, zsw], [og])
            yield
            for h in range(2):
                tr(k, Z1b[:, hs_[h]], og[:, hs_[h]], c["Ib"](), [og, cb], [Z1])
            yield
            stg = ostg[(n // 3) % 2]
            tl = n % 3
            cp(k, "act", stg[:, :, tl * 128:(tl + 1) * 128], v3(Z1b[:, 0:256]), [Z1], [stg])
            if tl == 2:
                out_cb(j, n // 3, stg)
            yield

        run_chains(pre, state, nt if not DBG_SKIP else 0, NS)


def phase_outproj(k, P, xTs, kcs, wout, hres, out, nb, pfx, ssq_out=None, out_rows=None, load_cb=None,
                  wq="pool", single=False):
    KC = sum(kcs)
    nt = nb * 3
    mmA, mmB, DD = P.full
    groups = [[0, 1]] if single else [[0], [1]]
    wcols = 512 * len(groups[0])
    w_sb = k.sb(pfx + "w", [128, KC, wcols], BF16)
    xb = [k.sb(pfx + "xb%d" % i, [128, KC, TB], BF16) for i in range(2)]
    res = [k.sb(pfx + "res%d" % i, [128, 512], F32) for i in range(2)]
    ot = [k.sb(pfx + "ot%d" % i, [128, 512], F32) for i in range(2)]
    junk = k.sb(pfx + "junk", [128, 512], BF16)
    ssa = k.sb(pfx + "ssa", [128, 2, nt], F32)
    bi = 0
    ti = 0
    for grp in groups:
        c0 = grp[0] * 512
        for q8 in range(KC // 8):
            k.dma(wq, w_sb[:, q8 * 8:(q8 + 1) * 8, :],
                  wout[q8 * 1024:(q8 + 1) * 1024, c0:c0 + wcols].rearrange("(c p) n -> p c n", p=128),
                  [wout], [w_sb])
        for tb in range(nb):
            xbb = xb[bi % 2]
            bi += 1
            off = 0
            if load_cb is not None:
                load_cb(tb, xbb)
            else:
                for xT, kc in zip(xTs, kcs):
                    k.dma("sp", xbb[:, off:off + kc, :], xT[tb], [xT], [xbb])
                    off += kc
            for tl in range(3):
              for gi, cbk in enumerate(grp):
                cs = slice(cbk * 512, (cbk + 1) * 512)
                i = 3 * tb + tl
                ps = mmA if ti % 2 == 0 else mmB
                rs, o_ = res[ti % 2], ot[ti % 2]
                ti += 1
                k.dma("sp", rs[:, :], hres[i * 128:(i + 1) * 128, cs], [hres], [rs])
                for cc in range(KC):
                    mm(k, ps[:, :], xbb[:, cc, tl * 128:(tl + 1) * 128], w_sb[:, cc, gi * 512:(gi + 1) * 512],
                       cc == 0, cc == KC - 1, [xbb, w_sb], [ps])
                tt(k, "dve", o_[:, :], ps[:, :], rs[:, :], ALU.add, [ps, rs], [o_])
                if ssq_out is not None:
                    act(k, junk[:, :], o_[:, :], AF.Square, [o_], [junk, ssa], accum=ssa[:, cbk, i:i + 1])
                if out_rows is None:
                    k.dma("sp", out[i * 128:(i + 1) * 128, cs], o_[:, :], [o_], [out])
                else:
                    for dst_rows, src_rows in out_rows(i):
                        k.dma("sp", out[dst_rows, cs], o_[src_rows, :], [o_], [out])
    if ssq_out is not None:
        tt(k, "dve", ssa[:, 0, :], ssa[:, 0, :], ssa[:, 1, :], ALU.add, [ssa], [ssa])
        k.dma("sp", ssq_out[:, :], ssa[:, 0, :], [ssa], [ssq_out])


def phase_norm2(k, c, P, ssq_all, h1, nw, dst, nb, pfx, ssq_view=None, out_cb=None):
    nt = nb * 3
    TR = P.half[9]
    TRb = TR[:, :].bitcast(BF16)
    cb = c["b"]
    sq = k.sb(pfx + "sq", [128, 4, nt], F32)
    rstd = k.sb(pfx + "rstd", [128, nt], F32)
    wB = k.sb(pfx + "wB", [128, 1024], F32)
    xts = [k.sb(pfx + "xt%d" % i, [128, 1024], F32) for i in range(2)]
    hnb = [k.sb(pfx + "hn%d" % i, [128, 1024], BF16) for i in range(2)]
    st = [k.sb(pfx + "st%d" % i, [128, 8, TB], BF16) for i in range(2)]
    k.dma("sp", sq[:, :, :], ssq_view if ssq_view is not None else ssq_all[:, :, :].rearrange("g p n -> p g n"),
          [ssq_all], [sq])
    k.dma("sp", wB[:, :], nw[:, :], [nw], [wB])
    for g in range(1, 4):
        tt(k, "dve", sq[:, 0, :], sq[:, 0, :], sq[:, g, :], ALU.add, [sq], [sq])
    act(k, sq[:, 1, :], sq[:, 0, :], AF.Sqrt, [sq], [sq], bias=EPS, scale=1.0 / D)
    k.op("dve", lambda en: en.reciprocal(out=rstd[:, :], in_=sq[:, 1, :]), [sq], [rstd])
    for blk in range(nb):
        stg = st[blk % 2]
        for tl in range(3):
            i = blk * 3 + tl
            xt, hb = xts[i % 2], hnb[i % 2]
            k.dma("sp", xt[:, :], h1[i * 128:(i + 1) * 128, :], [h1], [xt])
            stt(k, hb[:, :], xt[:, :], rstd[:, i:i + 1], wB[:, :], ALU.mult, ALU.mult, [xt, rstd, wB], [hb])
            for cc in range(8):
                tr(k, TRb[:, (cc % 4) * 128:(cc % 4 + 1) * 128], hb[:, cc * 128:(cc + 1) * 128], c["Ib"](),
                   [hb, cb], [TR])
                if cc % 4 == 3:
                    g4 = cc // 4
                    cp(k, "act", stg[:, g4 * 4:(g4 + 1) * 4, tl * 128:(tl + 1) * 128], v3(TRb[:, 0:512]),
                       [TR], [stg])
        if out_cb is not None:
            out_cb(blk, stg)
        else:
            k.dma("sp", dst[blk], stg[:, :, :], [stg], [dst])


def phase_sb(k, c, P, hnTs, win, nwqk, o1T, nb, nheads, pfx, load_cb=None, out_cb=None):
    nt = nb * 3
    TT = nt * 128
    mmA, mmB, DD = P.full
    ZP = P.half[0].root
    TR = P.half[2]
    OO = P.half[4]
    TRb = TR[:, :].bitcast(BF16)
    cf, cb = c["f"], c["b"]
    w_sb = k.sb(pfx + "w", [128, NC_, 512], BF16)
    hbk = [k.sb(pfx + "hb%d" % i, [128, NC_, TB], BF16) for i in range(2)]
    qT = k.sb(pfx + "qT", [128, TT], BF16)
    kT = k.sb(pfx + "kT", [128, TT], BF16)
    vtok = k.sb(pfx + "vtok", [128, nt, 128], BF16)
    gs = k.sb(pfx + "gs", [128, nt, 128], BF16)
    oT = k.sb(pfx + "oT", [128, TT], BF16)
    nwt = k.sb(pfx + "nw", [128, 2], F32)
    sqb = k.sb(pfx + "sq", [128, TB], BF16)
    sb_ = k.sb(pfx + "s", [128, TB], F32)
    rn = k.sb(pfx + "rn", [128, TB], F32)
    ones512 = k.sb(pfx + "ones", [128, 512], F32)
    E1 = k.sb(pfx + "E1", [128, 512], F32)
    SP = k.sb(pfx + "SP", [128, 512], F32)
    PRE = k.sb(pfx + "PRE", [128, 512], F32)
    ARG = k.sb(pfx + "ARG", [128, 512], F32)
    Aw = k.sb(pfx + "Aw", [128, 512], BF16)
    AT = k.sb(pfx + "AT", [128, 512], BF16)
    car = k.sb(pfx + "car", [128, 4], F32)
    og = k.sb(pfx + "og", [128, 128], BF16)
    k.dma("sp", nwt[:, :], nwqk[:, :], [nwqk], [nwt])
    k.op("dve", lambda en: en.memset(ones512[:, :], 1.0), [], [ones512])
    bi = 0
    for j in range(nheads):
        for q4 in range(4):
            k.dma("pool", w_sb[:, q4 * 8:(q4 + 1) * 8, :],
                  win[j, q4 * 1024:(q4 + 1) * 1024, :].rearrange("(c p) n -> p c n", p=128), [win], [w_sb])
        for tb in range(nb):
            hb = hbk[bi % 2]
            bi += 1
            if load_cb is not None:
                load_cb(tb, hb)
            else:
                for g in range(4):
                    k.dma("sp", hb[:, g * 8:(g + 1) * 8, :], hnTs[g][tb], [hnTs[g]], [hb])
            for fc in range(2):
                ps = mmA if fc == 0 else mmB
                for cc in range(NC_):
                    mm(k, ps[:, 0:TB], w_sb[:, cc, fc * 128:(fc + 1) * 128], hb[:, cc, :], cc == 0, cc == NC_ - 1,
                       [w_sb, hb], [ps])
                act(k, sqb[:, :], ps[:, 0:TB], AF.Square, [ps], [sqb])
                mm(k, DD[:, 0:TB], c["onesb"](), sqb[:, :], True, True, [cb, sqb], [DD])
                act(k, sb_[:, :], DD[:, 0:TB], AF.Sqrt, [DD], [sb_], bias=EPS, scale=1.0 / 128)
                k.op("dve", lambda en: en.reciprocal(out=rn[:, :], in_=sb_[:, :]), [sb_], [rn])
                dstT = qT if fc == 0 else kT
                stt(k, rn[:, :], rn[:, :], nwt[:, fc:fc + 1], ps[:, 0:TB], ALU.mult, ALU.mult, [rn, nwt, ps], [rn])
                if fc == 0:
                    ts(k, "dve", dstT[:, tb * TB:(tb + 1) * TB], rn[:, :], 128.0 ** -0.5, ALU.mult, [rn], [dstT])
                else:
                    cp(k, "dve", dstT[:, tb * TB:(tb + 1) * TB], rn[:, :], [rn], [dstT])
            for tl in range(3):
                i = 3 * tb + tl
                ps = mmA if tl % 2 == 0 else mmB
                for cc in range(NC_):
                    mm(k, ps[:, 0:256], hb[:, cc, tl * 128:(tl + 1) * 128], w_sb[:, cc, 256:512], cc == 0,
                       cc == NC_ - 1, [w_sb, hb], [ps])
                cp(k, "dve", vtok[:, i, :], ps[:, 0:128], [ps], [vtok])
                act(k, gs[:, i, :], ps[:, 128:256], AF.Silu, [ps], [gs])
        for tq in range(nt):
            k.op("dve", lambda en: en.memset(car[:, 0:1], 0.0), [], [car])
            blocks = []
            t1 = tq + 1
            while t1 > 0:
                t0 = max(0, t1 - 4)
                blocks.append((t0, t1))
                t1 = t0
            nmm = tq + 1
            mi = 0
            for (t0, t1) in blocks:
                n = t1 - t0
                W = n * 128
                diag = (t1 - 1 == tq)
                mm(k, ZP[:, 0:W], qT[:, tq * 128:(tq + 1) * 128], kT[:, t0 * 128:t1 * 128], True, True,
                   [qT, kT], [ZP])
                act(k, E1[:, 0:W], ZP[:, 0:W], AF.Exp, [ZP], [E1])
                act(k, SP[:, 0:W], E1[:, 0:W], AF.Ln, [E1], [SP], bias=1.0)
                if diag:
                    tt(k, "pool", SP[:, W - 128:W], SP[:, W - 128:W], c["Ls"](), ALU.mult, [SP, cf], [SP])
                k.op("dve", lambda en: en.tensor_tensor_scan(out=PRE[:, 0:W], data0=ones512[:, 0:W],
                                                             data1=SP[:, 0:W], initial=0.0, op0=ALU.mult,
                                                             op1=ALU.add), [ones512, SP], [PRE])
                tt(k, "dve", ARG[:, 0:W], ZP[:, 0:W], SP[:, 0:W], ALU.subtract, [ZP, SP], [ARG])
                tt(k, "pool", ARG[:, 0:W], ARG[:, 0:W], PRE[:, 0:W], ALU.add, [ARG, PRE], [ARG])
                tt(k, "dve", car[:, 1:2], car[:, 0:1], PRE[:, W - 1:W], ALU.add, [car, PRE], [car])
                ts(k, "dve", car[:, 2:3], car[:, 1:2], -1.0, ALU.mult, [car], [car])
                act(k, Aw[:, 0:W], ARG[:, 0:W], AF.Exp, [ARG, car], [Aw], bias=car[:, 2:3])
                cp(k, "dve", car[:, 0:1], car[:, 1:2], [car], [car])
                if diag:
                    tt(k, "pool", Aw[:, W - 128:W], Aw[:, W - 128:W], c["b"][:, 384:512], ALU.mult, [Aw, cb], [Aw])
                for s_ in range(n):
                    tr(k, TRb[:, s_ * 128:(s_ + 1) * 128], Aw[:, s_ * 128:(s_ + 1) * 128], c["Ib"](), [Aw, cb], [TR])
                cp(k, "act", AT[:, 0:W], TRb[:, 0:W], [TR], [AT])
                for s_ in range(n):
                    mm(k, OO[:, 0:128], AT[:, s_ * 128:(s_ + 1) * 128], vtok[:, t0 + s_, :], mi == 0, mi == nmm - 1,
                       [AT, vtok], [OO])
                    mi += 1
            tt(k, "dve", og[:, :], OO[:, 0:128], gs[:, tq, :], ALU.mult, [OO, gs], [og])
            tr(k, TRb[:, 0:128], og[:, :], c["Ib"](), [og, cb], [TR])
            cp(k, "act", oT[:, tq * 128:(tq + 1) * 128], TRb[:, 0:128], [TR], [oT])
        if out_cb is not None:
            out_cb(j, oT)
        else:
            k.dma("sp", o1T[:, :, j, :].rearrange("t p x -> p t x"), v3(oT[:, :], TB), [oT], [o1T])


def _new(with_consts=True):
    nc = bass.Bass("TRN2", target_bir_lowering=False)
    k = K(nc)
    cd = k.dram("consts", [128, 640], F32, kind="ExternalInput") if with_consts else None
    return nc, k, cd


def build_l1(nb=NB, nh=8):
    nc, k, cd = _new()
    h0 = k.dram("h0", [nb * TB, D], F32, kind="ExternalInput")
    nw = k.dram("nw", [128, D], F32, kind="ExternalInput")
    win = k.dram("win", [nh, D, DNW], F32, kind="ExternalInput")
    cw = k.dram("cw", [128, nh, 4, 4], F32, kind="ExternalInput")
    hp = k.dram("hp", [128, nh, 4], F32, kind="ExternalInput")
    onw = k.dram("onw", [128, 256], F32, kind="ExternalInput")
    hnT = k.dram("hnT", [nb, 128, NC_, TB], BF16, kind="Internal")
    o0T = k.dram("o0T", [nb, 128, nh * 2, TB], BF16, kind="ExternalOutput")
    k.phase_begin()
    c = load_consts(k, cd, "a")
    phase_norm_T(k, c, h0, nw, hnT, nb, "n")
    k.phase_end()
    k.phase_begin()
    c = load_consts(k, cd, "b")
    def dn_out(j, tb, stg):
        k.dma("sp", o0T[tb, :, 2 * j:2 * j + 2, :], stg[:, :, :], [stg], [o0T])
    phase_dn2(k, c, hnT, win, cw, hp, onw, nb, nh, "d", dn_out)
    k.phase_end()
    k.finish()
    return nc


def build_outproj(kc_each, final, nb=NB):
    nc, k, cd = _new(False)
    xTs = [k.dram("xT%d" % g, [nb, 128, kc_each, TB], BF16, kind="ExternalInput") for g in range(4)]
    wout = k.dram("wout", [4 * kc_each * 128, 1024], F32, kind="ExternalInput")
    hres = k.dram("hres", [nb * TB, 1024], F32, kind="ExternalInput")
    P = PsumSet(k, "p")
    if not final:
        out = k.dram("h1", [nb * TB, 1024], F32, kind="ExternalOutput")
        ssq = k.dram("ssq", [128, nb * 3], F32, kind="ExternalOutput")
        phase_outproj(k, P, xTs, [kc_each] * 4, wout, hres, out, nb, "o", ssq_out=ssq)
    else:
        out = k.dram("out", [SEQ, 1024], F32, kind="ExternalOutput")

        def rows(i):
            lo = max(i * 128, NMETA)
            hi = min((i + 1) * 128, NMETA + SEQ)
            if hi <= lo:
                return []
            return [(slice(lo - NMETA, hi - NMETA), slice(lo - i * 128, hi - i * 128))]
        phase_outproj(k, P, xTs, [kc_each] * 4, wout, hres, out, nb, "o", out_rows=rows)
    k.finish()
    return nc


def build_l3(nb=NB):
    nc, k, cd = _new()
    ssq_all = k.dram("ssq_all", [4, 128, nb * 3], F32, kind="ExternalInput")
    h1 = k.dram("h1", [nb * TB, 1024], F32, kind="ExternalInput")
    nw = k.dram("nw", [128, 1024], F32, kind="ExternalInput")
    dst = k.dram("hn1T", [nb, 128, 8, TB], BF16, kind="ExternalOutput")
    c = load_consts(k, cd)
    P = PsumSet(k, "p")
    phase_norm2(k, c, P, ssq_all, h1, nw, dst, nb, "m")
    k.finish()
    return nc


def build_l4(nb=NB, nh=8):
    nc, k, cd = _new()
    hnTs = [k.dram("hnT%d" % g, [nb, 128, 8, TB], BF16, kind="ExternalInput") for g in range(4)]
    win = k.dram("win", [nh, D, 512], F32, kind="ExternalInput")
    nwqk = k.dram("nwqk", [128, 2], F32, kind="ExternalInput")
    o1T = k.dram("o1T", [nb, 128, nh, TB], BF16, kind="ExternalOutput")
    c = load_consts(k, cd)
    def load_h(tb, hb):
        for g in range(4):
            k.dma("sp", hb[:, g * 8:(g + 1) * 8, :], hnTs[g][tb], [hnTs[g]], [hb])

    def sb_out(j, oT):
        k.dma("sp", o1T[:, :, j, :].rearrange("t p x -> p t x"), v3(oT[:, :], TB), [oT], [o1T])
    phase_sb2(k, c, win, nwqk, nb, nh, "s", load_h, sb_out)
    k.finish()
    return nc


def run_rr(gens, width, bg=None):
    pending = list(gens)
    active = []
    free = list(range(width))
    while pending or active:
        while pending and len(active) < width:
            slot = free.pop(0)
            active.append((pending.pop(0)(slot), slot))
        for g in list(active):
            try:
                next(g[0])
            except StopIteration:
                active.remove(g)
                free.append(g[1])
        if bg is not None:
            try:
                next(bg)
            except StopIteration:
                bg = None
    if bg is not None:
        for _ in bg:
            pass


def phase_sb2(k, c, win, nwqk, nb, nheads, pfx, load_cb, out_cb, NS=3):
    nt = nb * 3
    TT = nt * 128
    banks = [k.ps(pfx + "bk%d" % i, [128, 512], F32) for i in range(8)]
    mmA, mmB = banks[0:2]
    cf, cb = c["f"], c["b"]
    hnTs = None
    w_sb = k.sb(pfx + "w", [128, NC_, 512], BF16)
    hbk = [k.sb(pfx + "hb%d" % i, [128, NC_, TB], BF16) for i in range(2)]
    arrs = [(k.sb(pfx + "qT%d" % i, [128, TT], BF16), k.sb(pfx + "kT%d" % i, [128, TT], BF16),
             k.sb(pfx + "vtok%d" % i, [128, nt, 128], BF16), k.sb(pfx + "gs%d" % i, [128, nt, 128], BF16))
            for i in range(2)]
    oT = k.sb(pfx + "oT", [128, TT], BF16)
    nwt = k.sb(pfx + "nw", [128, 2], F32)
    sqb = k.sb(pfx + "sq", [128, TB], BF16)
    sb_ = k.sb(pfx + "s", [128, TB], F32)
    rn = k.sb(pfx + "rn", [128, TB], F32)
    ones512 = k.sb(pfx + "ones", [128, 512], F32)

    class Set:
        pass
    sets = []
    for i in range(NS):
        S_ = Set()
        S_.ZP, S_.TA = banks[2 + 2 * i], banks[3 + 2 * i]
        S_.E1 = k.sb(pfx + "E1_%d" % i, [128, 512], F32)
        S_.SP = k.sb(pfx + "SP_%d" % i, [128, 512], F32)
        S_.PRE = k.sb(pfx + "PRE_%d" % i, [128, 512], F32)
        S_.Aw = k.sb(pfx + "Aw_%d" % i, [128, 512], BF16)
        S_.AT = k.sb(pfx + "AT_%d" % i, [128, 512], BF16)
        S_.car = k.sb(pfx + "car_%d" % i, [128, 4], F32)
        S_.Oa = k.sb(pfx + "Oa_%d" % i, [128, 128], F32)
        S_.og = k.sb(pfx + "og_%d" % i, [128, 128], BF16)
        sets.append(S_)
    def load_w(jj):
        for q4 in range(4):
            k.dma("pool", w_sb[:, q4 * 8:(q4 + 1) * 8, :],
                  win[jj, q4 * 1024:(q4 + 1) * 1024, :].rearrange("(c p) n -> p c n", p=128), [win], [w_sb])

    bi = [0]

    def inproj(j):
        qT, kT, vtok, gs = arrs[j % 2]
        for tb in range(nb):
            hb = hbk[bi[0] % 2]
            bi[0] += 1
            load_cb(tb, hb)
            for fc in range(2):
                ps, DD = (mmA, mmB) if fc == 0 else (mmB, mmA)
                for cc in range(NC_):
                    mm(k, ps[:, 0:TB], w_sb[:, cc, fc * 128:(fc + 1) * 128], hb[:, cc, :], cc == 0, cc == NC_ - 1,
                       [w_sb, hb], [ps])
                yield
                act(k, sqb[:, :], ps[:, 0:TB], AF.Square, [ps], [sqb])
                yield
                mm(k, DD[:, 0:TB], c["onesb"](), sqb[:, :], True, True, [cb, sqb], [DD])
                yield
                act(k, sb_[:, :], DD[:, 0:TB], AF.Sqrt, [DD], [sb_], bias=EPS, scale=1.0 / 128)
                yield
                k.op("dve", lambda en: en.reciprocal(out=rn[:, :], in_=sb_[:, :]), [sb_], [rn])
                dstT = qT if fc == 0 else kT
                stt(k, rn[:, :], rn[:, :], nwt[:, fc:fc + 1], ps[:, 0:TB], ALU.mult, ALU.mult, [rn, nwt, ps], [rn])
                if fc == 0:
                    ts(k, "dve", dstT[:, tb * TB:(tb + 1) * TB], rn[:, :], 128.0 ** -0.5, ALU.mult, [rn], [dstT])
                else:
                    cp(k, "dve", dstT[:, tb * TB:(tb + 1) * TB], rn[:, :], [rn], [dstT])
                yield
            for tl in range(3):
                i = 3 * tb + tl
                ps = mmA if tl % 2 == 0 else mmB
                for cc in range(NC_):
                    mm(k, ps[:, 0:256], hb[:, cc, tl * 128:(tl + 1) * 128], w_sb[:, cc, 256:512], cc == 0,
                       cc == NC_ - 1, [w_sb, hb], [ps])
                yield
                cp(k, "dve", vtok[:, i, :], ps[:, 0:128], [ps], [vtok])
                act(k, gs[:, i, :], ps[:, 128:256], AF.Silu, [ps], [gs])
                yield
        if j + 1 < nheads:
            load_w(j + 1)

    k.dma("sp", nwt[:, :], nwqk[:, :], [nwqk], [nwt])
    k.op("dve", lambda en: en.memset(ones512[:, :], 1.0), [], [ones512])
    load_w(0)
    for _ in inproj(0):
        pass
    for j in range(nheads):
        qT, kT, vtok, gs = arrs[j % 2]
        def chain(tq, slot, j=j):
            st_ = sets[slot]
            ZP, TA = st_.ZP, st_.TA
            TRb = TA[:, 0:256].bitcast(BF16)
            AV = TA[:, 256:384]
            E1, SP, PRE, ARG, Aw, AT, car, Oa = st_.E1, st_.SP, st_.PRE, st_.E1, st_.Aw, st_.AT, st_.car, st_.Oa
            k.op("dve", lambda en: en.memset(car[:, 0:1], 0.0), [], [car])
            t1 = tq + 1
            first = True
            while t1 > 0:
                t0 = max(0, t1 - 4)
                n = t1 - t0
                W = n * 128
                diag = (t1 - 1 == tq)
                mm(k, ZP[:, 0:W], qT[:, tq * 128:(tq + 1) * 128], kT[:, t0 * 128:t1 * 128], True, True,
                   [qT, kT], [ZP])
                yield
                act(k, E1[:, 0:W], ZP[:, 0:W], AF.Exp, [ZP], [E1])
                yield
                act(k, SP[:, 0:W], E1[:, 0:W], AF.Ln, [E1], [SP], bias=1.0)
                yield
                if diag:
                    tt(k, "pool", SP[:, W - 128:W], SP[:, W - 128:W], c["Ls"](), ALU.mult, [SP, cf], [SP])
                    yield
                k.op("dve", lambda en: en.tensor_tensor_scan(out=PRE[:, 0:W], data0=ones512[:, 0:W],
                                                             data1=SP[:, 0:W], initial=0.0, op0=ALU.mult,
                                                             op1=ALU.add), [ones512, SP], [PRE])
                tt(k, "dve", ARG[:, 0:W], ZP[:, 0:W], SP[:, 0:W], ALU.subtract, [ZP, SP], [ARG])
                yield
                tt(k, "pool", ARG[:, 0:W], ARG[:, 0:W], PRE[:, 0:W], ALU.add, [ARG, PRE], [ARG])
                tt(k, "dve", car[:, 1:2], car[:, 0:1], PRE[:, W - 1:W], ALU.add, [car, PRE], [car])
                ts(k, "dve", car[:, 2:3], car[:, 1:2], -1.0, ALU.mult, [car], [car])
                yield
                act(k, Aw[:, 0:W], ARG[:, 0:W], AF.Exp, [ARG, car], [Aw], bias=car[:, 2:3])
                yield
                cp(k, "dve", car[:, 0:1], car[:, 1:2], [car], [car])
                if diag:
                    tt(k, "pool", Aw[:, W - 128:W], Aw[:, W - 128:W], c["b"][:, 384:512], ALU.mult, [Aw, cb], [Aw])
                    yield
                for s_ in range(n):
                    tr(k, TRb[:, s_ * 128:(s_ + 1) * 128], Aw[:, s_ * 128:(s_ + 1) * 128], c["Ib"](), [Aw, cb], [TA])
                yield
                cp(k, "dve", AT[:, 0:W], TRb[:, 0:W], [TA], [AT])
                yield
                for s_ in range(n):
                    mm(k, AV, AT[:, s_ * 128:(s_ + 1) * 128], vtok[:, t0 + s_, :], s_ == 0, s_ == n - 1,
                       [AT, vtok], [TA])
                yield
                if first:
                    cp(k, "dve", Oa[:, :], AV, [TA], [Oa])
                else:
                    tt(k, "dve", Oa[:, :], Oa[:, :], AV, ALU.add, [Oa, TA], [Oa])
                first = False
                t1 = t0
                yield
            tt(k, "pool", st_.og[:, :], Oa[:, :], gs[:, tq, :], ALU.mult, [Oa, gs], [st_.og])
            yield
            tr(k, TRb[:, 0:128], st_.og[:, :], c["Ib"](), [st_.og, cb], [TA])
            yield
            cp(k, "act", oT[:, tq * 128:(tq + 1) * 128], TRb[:, 0:128], [TA], [oT])
            yield

        bg = inproj(j + 1) if j + 1 < nheads else None
        run_rr([(lambda slot, tq=tq: chain(tq, slot)) for tq in reversed(range(nt if not DBG_SKIP else 0))], NS, bg)
        out_cb(j, oT)


PCS1 = [(0, 4), (4, 8), (8, 11)]
PCS3 = [(0, 6), (6, 11)]


def _piece_of(pcs, tb):
    for i, (a, b) in enumerate(pcs):
        if a <= tb < b:
            return i, tb - a
    raise ValueError(tb)


def build_fused(nb=NB, nh=8):
    nc, k, cd = _new()
    nt = nb * 3
    h0 = k.dram("h0", [nb * TB, D], F32, kind="ExternalInput")
    h0c = k.dram("h0c", [nb * TB, 1024], F32, kind="ExternalInput")
    nw0 = k.dram("nw0", [128, D], F32, kind="ExternalInput")
    win0 = k.dram("win0", [nh, D, DNW], F32, kind="ExternalInput")
    cw = k.dram("cw", [128, nh, 4, 4], F32, kind="ExternalInput")
    hp = k.dram("hp", [128, nh, 4], F32, kind="ExternalInput")
    onw = k.dram("onw", [128, 256], F32, kind="ExternalInput")
    wout0 = k.dram("wout0", [nh * 4 * 256, 1024], F32, kind="ExternalInput")
    nw1 = k.dram("nw1", [128, 1024], F32, kind="ExternalInput")
    win1 = k.dram("win1", [nh, D, 512], F32, kind="ExternalInput")
    nwqk = k.dram("nwqk", [128, 2], F32, kind="ExternalInput")
    wout1 = k.dram("wout1", [nh * 4 * 128, 1024], F32, kind="ExternalInput")
    out = k.dram("out", [SEQ, 1024], F32, kind="ExternalOutput")
    hnT = k.dram("hnT", [nb, 128, NC_, TB], BF16)
    h1 = k.dram("h1", [nb * TB, 1024], F32)
    pcs1 = [(a, min(b, nb)) for (a, b) in PCS1 if a < nb]
    pcs3 = [(a, min(b, nb)) for (a, b) in PCS3 if a < nb]
    A1, G1, A2, G2, A3, G3 = [Buf(k, None, n) for n in ("A1", "G1", "A2", "G2", "A3", "G3")]
    a1 = [[nc.dram_tensor("a1_%d_%d" % (j, i), [(b - a) * 128, 768], BF16, kind="Internal").ap()
           for i, (a, b) in enumerate(pcs1)] for j in range(nh)]
    g1 = [[nc.dram_tensor("g1_%d_%d" % (j, i), [4 * (b - a) * 128, 768], BF16, kind="Internal").ap()
           for i, (a, b) in enumerate(pcs1)] for j in range(nh)]
    a2 = [nc.dram_tensor("a2_%d" % t, [128, 8 * TB], BF16, kind="Internal").ap() for t in range(nb)]
    g2 = [nc.dram_tensor("g2_%d" % t, [4 * 128, 8 * TB], BF16, kind="Internal").ap() for t in range(nb)]
    a3 = [[nc.dram_tensor("a3_%d_%d" % (j, i), [(b - a) * 128, TB], BF16, kind="Internal").ap()
           for i, (a, b) in enumerate(pcs3)] for j in range(nh)]
    g3 = [[nc.dram_tensor("g3_%d_%d" % (j, i), [4 * (b - a) * 128, TB], BF16, kind="Internal").ap()
           for i, (a, b) in enumerate(pcs3)] for j in range(nh)]
    ssq_in = k.dram("ssq_in", [128, nt], F32)
    ssq_all = k.dram("ssq_all", [4 * 128, nt], F32)

    wb0 = k.dram("wb0", [nh * 4 * 256, 1024], BF16)
    wb1 = k.dram("wb1", [nh * 4 * 128, 1024], BF16)
    def precast(j):
        for (src, dst) in ((wout0, wb0), (wout1, wb1)):
            per = src.t.shape[0] // nh
            r0 = j * per
            for q in range(r0, r0 + per, 1024):
                q1 = min(r0 + per, q + 1024)
                k.dma("pool", dst[q:q1, :], src[q:q1, :], [src], [dst])

    k.phase_begin()
    c = load_consts(k, cd, "a")
    phase_norm_T(k, c, h0, nw0, hnT, nb, "n")
    k.phase_end()

    def dn_out(j, tb, stg):
        i, tl = _piece_of(pcs1, tb)
        dst = a1[j][i].rearrange("(t p) (h x) -> p t h x", p=128, h=2)
        k.dma("sp", dst[:, tl, :, :], stg[:, :, :], [stg], [A1])
        if tb == pcs1[i][1] - 1:
            k.cc(a1[j][i][:, :], g1[j][i][:, :], [A1], [G1], G1)

    k.phase_begin()
    c = load_consts(k, cd, "b")
    phase_dn2(k, c, hnT, win0, cw, hp, onw, nb, nh, "d", dn_out, bg_cb=precast)
    k.phase_end()

    def load_x2(tb, xbb):
        i, tl = _piece_of(pcs1, tb)
        dstv = xbb[:, :, :].rearrange("p (r j h) x -> p r j h x", r=4, h=2)
        for j in range(nh):
            src = g1[j][i].rearrange("(r t p) (h x) -> p r t h x", r=4, p=128, h=2)
            k.dma("sp", dstv[:, :, j, :, :], src[:, :, tl, :, :], [G1], [xbb])

    k.phase_begin()
    P = PsumSet(k, "p2")
    phase_outproj(k, P, None, [nh * 2] * 4, wb0, h0c, h1, nb, "o", ssq_out=ssq_in, load_cb=load_x2, wq="sp")
    k.cc(ssq_in[:, :], ssq_all[:, :], [ssq_in], [ssq_all], ssq_all)
    k.phase_end()

    G2p = [Buf(k, None, "G2p%d" % t) for t in range(nb)]

    def n2_out(blk, stg):
        k.dma("sp", a2[blk].rearrange("p (c x) -> p c x", c=8), stg[:, :, :], [stg], [A2])
        k.cc(a2[blk][:, :], g2[blk][:, :], [A2], [G2], G2)

    k.phase_begin()
    c = load_consts(k, cd, "c")
    P = PsumSet(k, "p3")
    phase_norm2(k, c, P, ssq_all, h1, nw1, None, nb, "m",
                ssq_view=ssq_all[:, :].rearrange("(g p) n -> p g n", p=128), out_cb=n2_out)
    k.phase_end()

    def load_h3(tb, hb):
        k.dma("sp", hb[:, :, :].rearrange("p (r c) x -> p r c x", r=4),
              g2[tb].rearrange("(r p) (c x) -> p r c x", p=128, c=8), [G2], [hb])

    def sb_out(j, oT):
        for i, (a, b) in enumerate(pcs3):
            k.dma("sp", a3[j][i].rearrange("(t p) x -> p t x", p=128), v3(oT[:, a * TB:b * TB], TB), [oT], [A3])
        for i in range(len(pcs3)):
            k.cc(a3[j][i][:, :], g3[j][i][:, :], [A3], [G3], G3)

    k.phase_begin()
    c = load_consts(k, cd, "d")
    phase_sb2(k, c, win1, nwqk, nb, nh, "s", load_h3, sb_out)
    k.phase_end()

    def load_x4(tb, xbb):
        i, tl = _piece_of(pcs3, tb)
        dstv = xbb[:, :, :].rearrange("p (r j) x -> p r j x", r=4)
        for j in range(nh):
            src = g3[j][i].rearrange("(r t p) x -> p r t x", r=4, p=128)
            k.dma("sp", dstv[:, :, j, :], src[:, :, tl, :], [G3], [xbb])

    def rows(i):
        lo = max(i * 128, NMETA)
        hi = min((i + 1) * 128, NMETA + SEQ)
        if hi <= lo:
            return []
        return [(slice(lo - NMETA, hi - NMETA), slice(lo - i * 128, hi - i * 128))]

    k.phase_begin()
    P = PsumSet(k, "p5")
    phase_outproj(k, P, None, [nh] * 4, wb1, h1, out, nb, "q", out_rows=rows if nb == NB else None,
                  load_cb=load_x4, wq="sp", single=True)
    k.phase_end()
    k.finish()
    return nc


_DBG = None
FUSED = True


def _run(nc, in_maps):
    res = run_bass_kernel_spmd(nc, in_maps, core_ids=list(range(8)))
    return res.results


def _rep(v, n=128):
    return np.ascontiguousarray(np.tile(np.asarray(v, np.float32)[None], (n, 1)))


def kernel(x, meta_tokens, dn_norm_w, dn_w_in, dn_conv_w, dn_a_log, dn_dt_bias, dn_out_norm_w, dn_w_out,
           sb_norm_w, sb_w_in, sb_q_norm_w, sb_k_norm_w, sb_w_out):
    f32 = np.float32
    x = np.asarray(x, f32)
    cst = consts_np()
    h0p = []
    for b in range(2):
        h = np.zeros((T, D), f32)
        h[:NMETA] = np.asarray(meta_tokens, f32)
        h[NMETA:NMETA + SEQ] = x[b]
        h0p.append(h)
    w_in = np.asarray(dn_w_in, f32)[0]
    convw = np.asarray(dn_conv_w, f32)[0]
    a_log = np.asarray(dn_a_log, f32)[0]
    dtb = np.asarray(dn_dt_bias, f32)[0]
    onw = np.asarray(dn_out_norm_w, f32)[0]
    KQ, KV = 4096, 8192
    dn_win, dn_cw, dn_hp = [], [], []
    for g in range(4):
        ws, cws, hps = [], [], []
        for j in range(8):
            J = 8 * g + j
            cols = np.concatenate([np.arange(J * 128, (J + 1) * 128), KQ + np.arange(J * 128, (J + 1) * 128),
                                   2 * KQ + np.arange(2 * J * 128, (2 * J + 2) * 128),
                                   2 * KQ + KV + np.arange(2 * J * 128, (2 * J + 2) * 128),
                                   2 * KQ + 2 * KV + np.arange(2 * J, 2 * J + 2),
                                   2 * KQ + 2 * KV + 64 + np.arange(2 * J, 2 * J + 2)])
            ws.append(w_in[:, cols])
            cws.append(convw[:, cols[:512]].reshape(4, 4, 128).transpose(2, 1, 0))
            hps.append(np.concatenate([dtb[2 * J:2 * J + 2], a_log[2 * J:2 * J + 2]]))
        dn_win.append(np.ascontiguousarray(np.stack(ws)))
        dn_cw.append(np.ascontiguousarray(np.stack(cws, axis=1)))
        dn_hp.append(np.ascontiguousarray(np.tile(np.stack(hps)[None], (128, 1, 1))))
    dn_nw_r = _rep(np.asarray(dn_norm_w, f32)[0])
    onw_r = _rep(np.concatenate([onw, onw]))
    w_out0 = np.asarray(dn_w_out, f32)[0]
    sbw = np.asarray(sb_w_in, f32)[0]
    sb_win = []
    for g in range(4):
        ws = []
        for j in range(8):
            H = 8 * g + j
            cols = np.concatenate([q * 4096 + np.arange(H * 128, (H + 1) * 128) for q in range(4)])
            ws.append(sbw[:, cols])
        sb_win.append(np.ascontiguousarray(np.stack(ws)))
    sb_nw = np.asarray(sb_norm_w, f32)[0]
    nwqk = np.ascontiguousarray(np.stack([np.asarray(sb_q_norm_w, f32)[0], np.asarray(sb_k_norm_w, f32)[0]], 1))
    w_out1 = np.asarray(sb_w_out, f32)[0]

    cores = [(c // 4, c % 4) for c in range(8)]
    if FUSED:
        rf = _run(build_fused(), [{
            "consts": cst, "h0": h0p[b], "h0c": np.ascontiguousarray(h0p[b][:, g * 1024:(g + 1) * 1024]),
            "nw0": dn_nw_r, "win0": dn_win[g], "cw": dn_cw[g], "hp": dn_hp[g], "onw": onw_r,
            "wout0": np.ascontiguousarray(w_out0[:, g * 1024:(g + 1) * 1024]),
            "nw1": _rep(sb_nw[g * 1024:(g + 1) * 1024]), "win1": sb_win[g], "nwqk": nwqk,
            "wout1": np.ascontiguousarray(w_out1[:, g * 1024:(g + 1) * 1024])} for (b, g) in cores])
        out = np.empty((2, SEQ, D), f32)
        for ci, (b, g) in enumerate(cores):
            out[b][:, g * 1024:(g + 1) * 1024] = np.asarray(rf[ci]["out"])
        return out
    r1 = _run(build_l1(), [{"consts": cst, "h0": h0p[b], "nw": dn_nw_r, "win": dn_win[g], "cw": dn_cw[g],
                            "hp": dn_hp[g], "onw": onw_r} for (b, g) in cores])
    o0T = [np.asarray(r["o0T"]) for r in r1]
    r2 = _run(build_outproj(16, False),
              [dict({"wout": np.ascontiguousarray(w_out0[:, g * 1024:(g + 1) * 1024]),
                     "hres": np.ascontiguousarray(h0p[b][:, g * 1024:(g + 1) * 1024])},
                    **{"xT%d" % gg: o0T[4 * b + gg] for gg in range(4)}) for (b, g) in cores])
    h1 = [np.asarray(r["h1"]) for r in r2]
    ssq = [np.asarray(r["ssq"]) for r in r2]
    r3 = _run(build_l3(), [{"consts": cst, "ssq_all": np.ascontiguousarray(np.stack(ssq[4 * b:4 * b + 4])),
                            "h1": h1[4 * b + g], "nw": _rep(sb_nw[g * 1024:(g + 1) * 1024])} for (b, g) in cores])
    hn1T = [np.asarray(r["hn1T"]) for r in r3]
    r4 = _run(build_l4(), [dict({"consts": cst, "win": sb_win[g], "nwqk": nwqk},
                                **{"hnT%d" % gg: hn1T[4 * b + gg] for gg in range(4)}) for (b, g) in cores])
    o1T = [np.asarray(r["o1T"]) for r in r4]
    r5 = _run(build_outproj(8, True),
              [dict({"wout": np.ascontiguousarray(w_out1[:, g * 1024:(g + 1) * 1024]),
                     "hres": h1[4 * b + g]},
                    **{"xT%d" % gg: o1T[4 * b + gg] for gg in range(4)}) for (b, g) in cores])
    if _DBG is not None:
        _DBG.update(o0T=o0T, h1=h1, ssq=ssq, hn1T=hn1T, o1T=o1T)
    out = np.empty((2, SEQ, D), f32)
    for ci, (b, g) in enumerate(cores):
        out[b][:, g * 1024:(g + 1) * 1024] = np.asarray(r5[ci]["out"])
    return out
```
